# Optimizing a Trainium2 kernel written in Bass

```python
import math
import jax, jax.numpy as jnp
from jax import lax
import numpy as np

D_MODEL = 1024
BATCH = 1
SEQ = 16384
DEPTH = 1
DEC_BATCH = 8
DEC_SEQ = 8192
PAST_LEN = 128

N_HEADS = 8
N_KV_HEADS = 2
HEAD_DIM = 128
GROUP = N_HEADS // N_KV_HEADS
D_ATTN = N_HEADS * HEAD_DIM
D_KV = N_KV_HEADS * HEAD_DIM
Q_BLOCK = 128
ROPE_THETA = 10000.0
GRID_W = 64
D_LRU = D_MODEL
N_LRU_BLOCKS = 8
LRU_BLOCK = D_LRU // N_LRU_BLOCKS
CONV_W = 4
RG_C = 8.0
N_BRANCH = 2
EPS = 1e-6
D_IN_PROJ = D_ATTN + 2 * D_KV + D_ATTN + 2 * D_LRU + N_BRANCH * D_MODEL

kernel_name = "hybrid_gqa_rglru_gated_merge_encoder"


def rmsnorm(x, g):
    xf = x.astype(jnp.float32)
    y = xf * lax.rsqrt(jnp.mean(xf * xf, axis=-1, keepdims=True) + EPS)
    return (y * g.astype(jnp.float32)).astype(x.dtype)


def axial_rope(x):
    b, s, h, d = x.shape
    rows_n = s // GRID_W
    rows = jnp.repeat(jnp.arange(rows_n, dtype=jnp.float32), GRID_W)
    cols = jnp.tile(jnp.arange(GRID_W, dtype=jnp.float32), rows_n)
    n_pair_axis = d // 4
    inv_freq = ROPE_THETA ** (-jnp.arange(n_pair_axis, dtype=jnp.float32) / n_pair_axis)
    ang = jnp.concatenate([rows[:, None] * inv_freq, cols[:, None] * inv_freq], axis=-1)
    cos = jnp.cos(ang)[None, :, None, :]
    sin = jnp.sin(ang)[None, :, None, :]
    xp = x.astype(jnp.float32).reshape(b, s, h, d // 2, 2)
    x0, x1 = xp[..., 0], xp[..., 1]
    out = jnp.stack([x0 * cos - x1 * sin, x0 * sin + x1 * cos], axis=-1)
    return out.reshape(b, s, h, d).astype(x.dtype)


def block_attention(q, k, v):
    b, s, nkv, g, d = q.shape
    nb = s // Q_BLOCK
    scale = 1.0 / math.sqrt(d)
    qb = q.reshape(b, nb, Q_BLOCK, nkv, g, d).transpose(1, 0, 2, 3, 4, 5)

    def one_block(qblk):
        sc = jnp.einsum('bqkgd,bskd->bkgqs', qblk, k).astype(jnp.float32) * scale
        p = jax.nn.softmax(sc, axis=-1).astype(v.dtype)
        return jnp.einsum('bkgqs,bskd->bqkgd', p, v)

    o = lax.map(one_block, qb)
    return o.transpose(1, 0, 2, 3, 4, 5).reshape(b, s, nkv * g * d)


def centred_dwconv(x, w, bias):
    s = x.shape[1]
    left = CONV_W // 2
    xp = jnp.pad(x, ((0, 0), (left, CONV_W - 1 - left), (0, 0)))
    y = xp[:, 0:s] * w[0]
    for j in range(1, CONV_W):
        y = y + xp[:, j:j + s] * w[j]
    return y + bias


def _lin_combine(c1, c2):
    a1, b1 = c1
    a2, b2 = c2
    return a1 * a2, a2 * b1 + b2


def rg_lru(xc, w_r, b_r, w_i, b_i, lam, reverse):
    b, s, c = xc.shape
    xb = xc.reshape(b, s, N_LRU_BLOCKS, LRU_BLOCK)
    r = jax.nn.sigmoid(jnp.einsum('bsnc,ncd->bsnd', xb, w_r).reshape(b, s, c) + b_r)
    i = jax.nn.sigmoid(jnp.einsum('bsnc,ncd->bsnd', xb, w_i).reshape(b, s, c) + b_i)
    log_a = -RG_C * r.astype(jnp.float32) * jax.nn.softplus(-lam.astype(jnp.float32))
    a = jnp.exp(log_a)
    u = jnp.sqrt(-jnp.expm1(2.0 * log_a)) * (i * xc).astype(jnp.float32)
    _, h = lax.associative_scan(_lin_combine, (a, u), axis=1, reverse=reverse)
    return h


def layer(x, norm_in, w_in, b_merge, q_norm, k_norm, conv_w, conv_b,
          w_rgate, b_rgate, w_igate, b_igate, lam, w_branch, w_out):
    b, s, _ = x.shape
    h = rmsnorm(x, norm_in)
    z = jnp.einsum('bsd,de->bse', h, w_in)
    splits = np.cumsum([D_ATTN, D_KV, D_KV, D_ATTN, D_LRU, D_LRU])
    q, k, v, g_attn, x_lru, g_lru, m_logit = jnp.split(z, splits, axis=-1)

    q = axial_rope(rmsnorm(q.reshape(b, s, N_HEADS, HEAD_DIM), q_norm))
    k = axial_rope(rmsnorm(k.reshape(b, s, N_KV_HEADS, HEAD_DIM), k_norm))
    v = v.reshape(b, s, N_KV_HEADS, HEAD_DIM)
    q = q.reshape(b, s, N_KV_HEADS, GROUP, HEAD_DIM)
    a_out = block_attention(q, k, v) * jax.nn.silu(g_attn)

    xc = centred_dwconv(x_lru, conv_w, conv_b)
    h_f = rg_lru(xc, w_rgate[0], b_rgate[0], w_igate[0], b_igate[0], lam[0], False)
    h_b = rg_lru(xc, w_rgate[1], b_rgate[1], w_igate[1], b_igate[1], lam[1], True)
    l_out = (h_f + h_b).astype(x.dtype) * jax.nn.silu(g_lru)

    a_proj = jnp.einsum('bsc,cd->bsd', a_out, w_branch[0])
    l_proj = jnp.einsum('bsc,cd->bsd', l_out, w_branch[1])
    gates = jax.nn.sigmoid(m_logit + b_merge)
    g_a, g_l = jnp.split(gates, 2, axis=-1)
    merged = g_a * a_proj + g_l * l_proj
    return x + jnp.einsum('bsd,de->bse', merged, w_out)


def trunk(x, norm_in, w_in, b_merge, q_norm, k_norm, conv_w, conv_b,
          w_rgate, b_rgate, w_igate, b_igate, lam, w_branch, w_out, norm_final):
    for l in range(DEPTH):
        x = layer(x, norm_in[l], w_in[l], b_merge[l], q_norm[l], k_norm[l], conv_w[l], conv_b[l],
                  w_rgate[l], b_rgate[l], w_igate[l], b_igate[l], lam[l], w_branch[l], w_out[l])
    return rmsnorm(x, norm_final)


def setup_inputs(seed: int = 0) -> dict:
    key = jax.random.key(seed)
    ks = jax.random.split(key, 20)
    f32 = jnp.float32
    nrm = lambda k, shape, sc: jax.random.normal(k, shape, f32) * sc
    a_c = jax.random.uniform(ks[12], (DEPTH, 2, D_LRU), f32, 0.9, 0.999)
    a_base = a_c ** (1.0 / RG_C)
    lam = jnp.log(a_base) - jnp.log1p(-a_base)
    return {
        "x_prompt": nrm(ks[0], (BATCH, SEQ, D_MODEL), 1.0),
        "x_sample": nrm(ks[1], (DEC_BATCH, DEC_SEQ, D_MODEL), 1.0),
        "norm_in": 1.0 + nrm(ks[2], (DEPTH, D_MODEL), 0.02),
        "w_in": nrm(ks[3], (DEPTH, D_MODEL, D_IN_PROJ), D_MODEL ** -0.5),
        "b_merge": nrm(ks[4], (DEPTH, N_BRANCH * D_MODEL), 0.1),
        "q_norm": 1.0 + nrm(ks[5], (DEPTH, HEAD_DIM), 0.02),
        "k_norm": 1.0 + nrm(ks[6], (DEPTH, HEAD_DIM), 0.02),
        "conv_w": nrm(ks[7], (DEPTH, CONV_W, D_LRU), CONV_W ** -0.5),
        "conv_b": nrm(ks[8], (DEPTH, D_LRU), 0.02),
        "w_rgate": nrm(ks[9], (DEPTH, 2, N_LRU_BLOCKS, LRU_BLOCK, LRU_BLOCK), LRU_BLOCK ** -0.5),
        "b_rgate": nrm(ks[10], (DEPTH, 2, D_LRU), 0.1),
        "w_igate": nrm(ks[11], (DEPTH, 2, N_LRU_BLOCKS, LRU_BLOCK, LRU_BLOCK), LRU_BLOCK ** -0.5),
        "b_igate": nrm(ks[13], (DEPTH, 2, D_LRU), 0.1),
        "lam": lam,
        "w_branch": nrm(ks[14], (DEPTH, N_BRANCH, D_ATTN, D_MODEL), D_ATTN ** -0.5),
        "w_out": nrm(ks[15], (DEPTH, D_MODEL, D_MODEL), D_MODEL ** -0.5),
        "norm_final": 1.0 + nrm(ks[16], (D_MODEL,), 0.02),
    }


def reference(x_prompt, x_sample, norm_in, w_in, b_merge, q_norm, k_norm, conv_w, conv_b,
              w_rgate, b_rgate, w_igate, b_igate, lam, w_branch, w_out, norm_final):
    y_prompt = trunk(x_prompt, norm_in, w_in, b_merge, q_norm, k_norm, conv_w, conv_b,
                     w_rgate, b_rgate, w_igate, b_igate, lam, w_branch, w_out, norm_final)
    y_sample = trunk(x_sample, norm_in, w_in, b_merge, q_norm, k_norm, conv_w, conv_b,
                     w_rgate, b_rgate, w_igate, b_igate, lam, w_branch, w_out, norm_final)
    return (y_prompt, y_sample)
```

```python
import contextlib
import math
import numpy as np
import concourse.bass as bass
import concourse.mybir as mybir
from concourse.bass_utils import run_bass_kernel_spmd

F32 = mybir.dt.float32
BF16 = mybir.dt.bfloat16
AF = mybir.ActivationFunctionType
ALU = mybir.AluOpType
AX = mybir.AxisListType

NCORES = 8
D = 1024
DC = 8
H = 8
KVH = 2
HD = 128
CH = 512
NPROJ = 6656
EPS = 1e-6
SM_SCALE = 1.0 / math.sqrt(HD)
OQ, OK_, OV, OGA, OXL, OGL, OML = 0, 1024, 1280, 1536, 2560, 3584, 4608


class Prog:
    ENG = ("sp", "act", "dve", "pool", "pe")

    def __init__(self):
        self.ops = []
        self.last_writer = {}
        self.readers = {}
        self.last_on = {e: None for e in self.ENG}
        self.dma_since = set()
        self.pending = {e: set() for e in self.ENG}

    def op(self, eng, fn, reads=(), writes=(), dma=False):
        idx = len(self.ops)
        deps = {}
        for r in reads:
            w = self.last_writer.get(r)
            if w is not None:
                deps[w] = "raw"
        for r in writes:
            w = self.last_writer.get(r)
            if w is not None and w not in deps:
                deps[w] = "waw"
            for rd in self.readers.get(r, ()):
                if rd not in deps:
                    deps[rd] = "war"
        if self.pending[eng]:
            for p in self.pending[eng]:
                deps[p] = "raw"
            self.pending[eng] = set()
        for r in reads:
            self.readers.setdefault(r, []).append(idx)
        for r in writes:
            self.last_writer[r] = idx
            self.readers[r] = []
        self.ops.append(dict(eng=eng, fn=fn, dma=dma, deps=deps, idx=idx))
        self.last_on[eng] = idx
        if dma:
            self.dma_since.add(idx)
        return idx

    def barrier(self):
        s = set(self.dma_since)
        for e in self.ENG:
            if self.last_on[e] is not None:
                s.add(self.last_on[e])
        self.dma_since = set()
        self.pending = {e: set(s) for e in self.ENG}
        self.last_writer.clear()
        self.readers.clear()

    def emit(self, nc, ndma_sems=20):
        ops = self.ops
        for o in ops:
            keep = []
            for p, kind in o["deps"].items():
                po = ops[p]
                if (not po["dma"]) and (not o["dma"]) and po["eng"] == o["eng"]:
                    if not (kind == "raw" and o["eng"] in ("act", "dve", "pool")):
                        continue
                keep.append(p)
            o["deps"] = sorted(keep)
            o["signal"] = False
        for o in ops:
            for p in o["deps"]:
                ops[p]["signal"] = True
        cnt = {e: 0 for e in self.ENG}
        dma_i = {e: 0 for e in self.ENG}
        dma_cum = {}
        for o in ops:
            if o["dma"]:
                slot = (o["eng"], dma_i[o["eng"]] % ndma_sems)
                dma_i[o["eng"]] += 1
                prev = dma_cum.get(slot, 0)
                o["sem"], o["prev"], o["val"] = slot, prev, prev + 16
                dma_cum[slot] = prev + 16
            elif o["signal"]:
                cnt[o["eng"]] += 1
                o["sem"], o["val"] = o["eng"], cnt[o["eng"]]
        dma_queues = sorted({o["eng"] for o in ops if o["dma"]})
        with contextlib.ExitStack() as st:
            sems = {}
            for e in self.ENG:
                sems[e] = st.enter_context(nc.semaphore("s_" + e))
            for q in dma_queues:
                for i in range(ndma_sems):
                    sems[(q, i)] = st.enter_context(nc.semaphore("d_%s_%d" % (q, i)))
            block = st.enter_context(nc.Block())
            per_eng = {e: [o for o in ops if o["eng"] == e] for e in self.ENG}

            def run_engine(ename, eng):
                waited = {}
                for o in per_eng[ename]:
                    need = {}
                    for p in o["deps"]:
                        po = ops[p]
                        s = po["sem"]
                        if po["val"] > need.get(s, 0):
                            need[s] = po["val"]
                    if o["dma"] and o["prev"] > 0:
                        s = o["sem"]
                        if o["prev"] > need.get(s, 0):
                            need[s] = o["prev"]
                    for s, v in need.items():
                        if waited.get(s, 0) >= v:
                            continue
                        eng.wait_ge(sems[s], v)
                        waited[s] = v
                    ins = o["fn"](eng)
                    if o["dma"]:
                        ins.then_inc(sems[o["sem"]], 16)
                    elif o["signal"]:
                        ins.then_inc(sems[o["sem"]], 1)
                if ename == "sp":
                    for slot, v in dma_cum.items():
                        if waited.get(slot, 0) < v:
                            eng.wait_ge(sems[slot], v)

            @block.sync
            def _(e):
                run_engine("sp", e)

            @block.scalar
            def _(e):
                run_engine("act", e)

            @block.vector
            def _(e):
                run_engine("dve", e)

            @block.gpsimd
            def _(e):
                run_engine("pool", e)

            @block.tensor
            def _(e):
                run_engine("pe", e)


class Arena:
    def __init__(self, ap, nwords):
        self.ap = ap
        self.n = nwords
        self.off = 0
        self.mark = 0

    def alloc(self, shape, dt):
        n = 1
        for s in shape:
            n *= s
        words = n if dt == F32 else (n + 1) // 2
        words = (words + 7) // 8 * 8
        assert self.off + words <= self.n, ("arena overflow", self.off, words, self.n)
        v = self.ap[:, self.off:self.off + words]
        self.off += words
        if dt != F32:
            v = v.bitcast(dt)
        v = v[:, 0:n]
        if len(shape) == 2:
            v = v.rearrange("p (a b) -> p a b", a=shape[0])
        elif len(shape) == 3:
            v = v.rearrange("p (a b c) -> p a b c", a=shape[0], b=shape[1])
        return v

    def persist(self):
        self.mark = self.off

    def reset(self):
        self.off = self.mark


def build_nc(S_S, S_P):
    OWN = S_P // NCORES
    assert OWN % CH == 0 and S_S % CH == 0
    SMAX = max(S_S, S_P)
    NSLOT = OWN // CH
    nc = bass.Bass("TRN2", target_bir_lowering=False)

    def din(name, shape, dt=F32):
        return nc.dram_tensor(name, list(shape), dt, kind="ExternalInput").ap()

    xs = din("xs", [S_S, D])
    xp = din("xp", [S_P, D])
    xq = din("xq", [OWN, D])
    cs_s = din("cs_s", [S_S, HD])
    cs_p = din("cs_p", [S_P, HD])
    cs_q = din("cs_q", [OWN, HD])
    w_in = din("w_in", [D, NPROJ])
    w_br = din("w_br", [2, D, D])
    w_out = din("w_out", [D, D])
    w_gate = din("w_gate", [4, 8, 128, 128])
    chp = din("chp", [128, 8, 11])
    g_in = din("g_in", [1, D])
    g_fin = din("g_fin", [1, D])
    qn = din("qn", [1, HD])
    kn = din("kn", [1, HD])
    bm = din("bm", [128, 16])
    msk = din("msk", [128, NCORES])
    ident = din("ident", [128, 128])
    ys = nc.dram_tensor("ys", [S_S, D], F32, kind="ExternalOutput").ap()
    yq = nc.dram_tensor("yq", [OWN, D], F32, kind="ExternalOutput").ap()
    kT_scr = nc.dram_tensor("kT_scr", [KVH, 128, SMAX], BF16).ap()
    v_scr = nc.dram_tensor("v_scr", [KVH, 128, SMAX // 128, HD], BF16).ap()
    xl_scr = nc.dram_tensor("xl_scr", [128, 8, SMAX], F32).ap()
    hf_scr = nc.dram_tensor("hf_scr", [128, 8, SMAX], F32).ap()
    ls_scr = nc.dram_tensor("ls_scr", [128, 8, max(S_S, OWN)], F32).ap()
    ao_scr = nc.dram_tensor("ao_scr", [128, 8, max(S_S, OWN)], BF16).ap()

    PG = Prog()
    w_in_v = w_in.rearrange("(c p) n -> p c n", p=128)

    def OP(eng, reads, writes, f):
        PG.op(eng, f, reads, writes)

    def DMA(eng, out, in_, reads, writes):
        PG.op(eng, lambda e: e.dma_start(out=out, in_=in_), reads, writes, dma=True)

    with contextlib.ExitStack() as st:
        NW = 53000
        arena_t = st.enter_context(nc.sbuf_tensor("arena", [128, NW], F32))
        ps_t = st.enter_context(nc.psum_tensor("ps", [128, 4096], F32))
        AR = Arena(arena_t, NW)
        ps = ps_t
        psb = ps_t[:].bitcast(BF16)

        def bank(b, n=512):
            return ps[:, b * 512:b * 512 + n]

        def bankb(b, n=1024):
            return psb[:, b * 1024:b * 1024 + n]

        bank_ctr = [0]

        def next_bank():
            b = bank_ctr[0] % 8
            bank_ctr[0] += 1
            return b

        ident_b = AR.alloc([128], BF16)
        ones_b = AR.alloc([128], BF16)
        gin_bc = AR.alloc([D], F32)
        gfin_bc = AR.alloc([D], F32)
        qn_bc = AR.alloc([HD], F32)
        kn_bc = AR.alloc([HD], F32)
        chp_t = AR.alloc([8, 11], F32)
        bm_t = AR.alloc([16], F32)
        bmh_t = AR.alloc([16], F32)
        msk_t = AR.alloc([NCORES], F32)
        lam_c = AR.alloc([16], F32)
        cl_h = AR.alloc([16], F32)
        cl_1 = AR.alloc([16], F32)
        brh = AR.alloc([16], F32)
        bih = AR.alloc([16], F32)
        tmpa = AR.alloc([16], F32)
        tmpb = AR.alloc([16], F32)
        tmpc = AR.alloc([16], F32)
        negB = AR.alloc([1], F32)
        mq = AR.alloc([1], F32)
        mk = AR.alloc([1], F32)
        absq = AR.alloc([HD], F32)
        wg = AR.alloc([4, 8, 128], BF16)
        AR.persist()

        DMA("pool", ident_b, ident[:, :], [], ["ident_b"])
        OP("dve", [], ["ones_b"], lambda e: e.memset(ones_b, 1.0))
        DMA("sp", gin_bc, g_in[0:1, :].partition_broadcast(128), [], ["gin_bc"])
        DMA("sp", gfin_bc, g_fin[0:1, :].partition_broadcast(128), [], ["gfin_bc"])
        DMA("sp", qn_bc, qn[0:1, :].partition_broadcast(128), [], ["qn_bc"])
        DMA("sp", kn_bc, kn[0:1, :].partition_broadcast(128), [], ["kn_bc"])
        DMA("sp", chp_t, chp[:, :, :], [], ["chp"])
        DMA("sp", bm_t, bm[:, :], [], ["bm"])
        DMA("sp", msk_t, msk[:, :], [], ["msk"])
        for gi in range(4):
            DMA("pool", wg[:, gi, :, :], w_gate[gi].rearrange("b i o -> i b o"), [], ["wg"])
        for d_ in range(2):
            OP("dve", ["chp"], ["lam_c"], lambda e, d_=d_: e.tensor_copy(out=lam_c[:, d_ * 8:(d_ + 1) * 8], in_=chp_t[:, :, 9 + d_]))
            OP("dve", ["chp"], ["brh"], lambda e, d_=d_: e.tensor_scalar(out=brh[:, d_ * 8:(d_ + 1) * 8], in0=chp_t[:, :, 5 + d_], scalar1=0.5, scalar2=None, op0=ALU.mult))
            OP("dve", ["chp"], ["bih"], lambda e, d_=d_: e.tensor_scalar(out=bih[:, d_ * 8:(d_ + 1) * 8], in0=chp_t[:, :, 7 + d_], scalar1=0.5, scalar2=None, op0=ALU.mult))
        OP("dve", ["bm"], ["bmh"], lambda e: e.tensor_scalar(out=bmh_t, in0=bm_t, scalar1=0.5, scalar2=None, op0=ALU.mult))
        OP("act", ["lam_c"], ["tmpa"], lambda e: e.activation(out=tmpa, in_=lam_c, func=AF.Exp, scale=-1.0))
        OP("dve", ["tmpa"], ["tmpb"], lambda e: e.tensor_scalar(out=tmpb, in0=tmpa, scalar1=1.0, scalar2=None, op0=ALU.add))
        OP("act", ["tmpb"], ["tmpc"], lambda e: e.activation(out=tmpc, in_=tmpb, func=AF.Ln))
        OP("dve", ["tmpb"], ["tmpb"], lambda e: e.tensor_scalar(out=tmpb, in0=tmpb, scalar1=-1.0, scalar2=1e-30, op0=ALU.add, op1=ALU.max))
        OP("dve", ["tmpb"], ["tmpb"], lambda e: e.reciprocal(out=tmpb, in_=tmpb))
        OP("dve", ["tmpa", "tmpc"], ["tmpc"], lambda e: e.tensor_tensor(out=tmpc, in0=tmpc, in1=tmpa, op=ALU.mult))
        OP("dve", ["tmpb", "tmpc"], ["tmpc"], lambda e: e.tensor_tensor(out=tmpc, in0=tmpc, in1=tmpb, op=ALU.mult))
        OP("dve", ["tmpc"], ["cl_h"], lambda e: e.tensor_scalar(out=cl_h, in0=tmpc, scalar1=-4.0, scalar2=None, op0=ALU.mult))
        OP("dve", ["tmpc"], ["cl_1"], lambda e: e.tensor_scalar(out=cl_1, in0=tmpc, scalar1=-8.0, scalar2=None, op0=ALU.mult))
        OP("act", ["qn_bc"], ["absq"], lambda e: e.activation(out=absq, in_=qn_bc, func=AF.Abs))
        OP("dve", ["absq"], ["mq"], lambda e: e.tensor_reduce(out=mq, in_=absq, axis=AX.X, op=ALU.max))
        OP("act", ["kn_bc", "mq"], ["absq"], lambda e: e.activation(out=absq, in_=kn_bc, func=AF.Abs))
        OP("dve", ["absq"], ["mk"], lambda e: e.tensor_reduce(out=mk, in_=absq, axis=AX.X, op=ALU.max))
        OP("dve", ["mq", "mk"], ["negB"], lambda e: e.tensor_tensor(out=negB, in0=mq, in1=mk, op=ALU.mult))
        OP("dve", ["negB"], ["negB"], lambda e: e.tensor_scalar(out=negB, in0=negB, scalar1=-math.sqrt(HD), scalar2=None, op0=ALU.mult))
        PG.barrier()

        def wload(dst, src_cols):
            DMA("pool", dst, src_cols, [], ["W"])

        def make_hT(bufs, x_ap, t0, nt, hT, hT_res, banks=None):
            for i in range(nt):
                sl = bufs["ctr"] % 2
                bufs["ctr"] += 1
                xt = bufs["xt"][sl]
                xtr = "xt%d" % sl
                ssq = bufs["ssq"][sl]
                xn = bufs["xn"][sl]
                junk = bufs["junk"]
                DMA("sp", xt, x_ap[t0 + i * 128:t0 + (i + 1) * 128, :], [], [xtr])
                OP("act", [xtr], ["junk", "ssq%d" % sl], lambda e, xt=xt, ssq=ssq: e.activation(out=junk, in_=xt, func=AF.Square, accum_out=ssq))
                OP("act", ["ssq%d" % sl], ["ssq%d" % sl], lambda e, ssq=ssq: e.activation(out=ssq, in_=ssq, func=AF.Sqrt, scale=1.0 / D, bias=EPS))
                OP("dve", ["ssq%d" % sl], ["ssq%d" % sl], lambda e, ssq=ssq: e.reciprocal(out=ssq, in_=ssq))
                OP("dve", [xtr, "ssq%d" % sl], ["xn%d" % sl], lambda e, xt=xt, ssq=ssq, xn=xn: e.scalar_tensor_tensor(out=xn, in0=xt, scalar=ssq[:, 0:1], in1=gin_bc, op0=ALU.mult, op1=ALU.mult))
                if banks is None:
                    b = next_bank()
                else:
                    b = banks[bufs["ctr"] % len(banks)]
                for c in range(DC):
                    OP("pe", ["xn%d" % sl], ["pb%d" % b], lambda e, c=c, b=b, xn=xn: e.transpose(out=bankb(b)[:, c * 128:(c + 1) * 128], in_=xn[:, c * 128:(c + 1) * 128], identity=ident_b))
                OP("act", ["pb%d" % b], [hT_res], lambda e, b=b, i=i: e.activation(out=hT[:, :, i * 128:(i + 1) * 128], in_=bankb(b).rearrange("p (c t) -> p c t", c=DC), func=AF.Copy))

        def norm_rope(tb, src_ps, nt, nh, w_bc, cs_t, cs_res, src_res, out_b, out_res):
            nu = nt * nh
            n = nu * HD
            s = tb["sl"] % tb["nbuf"]
            tb["sl"] += 1
            qf, sq, ssq, qn_, t1, t2, t3, t4 = (tb[k][s] for k in ("qf", "sq", "ssq", "qn", "t1", "t2", "t3", "t4"))
            R = lambda k: "%s%d" % (k, s)
            OP("act", src_res, [R("qf")], lambda e: e.activation(out=qf[:, 0:n].rearrange("p (t x) -> p t x", t=nt), in_=src_ps, func=AF.Copy))
            OP("act", [R("qf")], [R("sq")], lambda e: e.activation(out=sq[:, 0:n], in_=qf[:, 0:n], func=AF.Square))
            OP("dve", [R("sq")], [R("qssq")], lambda e: e.tensor_reduce(out=ssq[:, 0:nu], in_=sq[:, 0:n].rearrange("p (h d) -> p h d", h=nu), axis=AX.X, op=ALU.add))
            OP("act", [R("qssq")], [R("qssq")], lambda e: e.activation(out=ssq[:, 0:nu], in_=ssq[:, 0:nu], func=AF.Sqrt, scale=1.0 / HD, bias=EPS))
            OP("dve", [R("qssq")], [R("qssq")], lambda e: e.reciprocal(out=ssq[:, 0:nu], in_=ssq[:, 0:nu]))
            for h in range(nu):
                OP("dve", [R("qf"), R("qssq")], [R("qn")], lambda e, h=h: e.scalar_tensor_tensor(out=qn_[:, h * HD:(h + 1) * HD], in0=qf[:, h * HD:(h + 1) * HD], scalar=ssq[:, h:h + 1], in1=w_bc, op0=ALU.mult, op1=ALU.mult))
            q4 = qn_[:, 0:n].rearrange("p (t h d) -> p t h d", t=nt, h=nh)
            x0, x1 = q4[:, :, :, 0:64], q4[:, :, :, 64:128]
            cosb = cs_t[:, :, 0:64].unsqueeze(2).to_broadcast([128, nt, nh, 64])
            sinb = cs_t[:, :, 64:128].unsqueeze(2).to_broadcast([128, nt, nh, 64])
            o4 = out_b[:, 0:n].rearrange("p (t h d) -> p t h d", t=nt, h=nh)

            def v4(t):
                return t[:, 0:nu * 64].rearrange("p (t h d) -> p t h d", t=nt, h=nh)
            OP("dve", [R("qn"), cs_res], [R("t1")], lambda e: e.tensor_tensor(out=v4(t1), in0=x0, in1=cosb, op=ALU.mult))
            OP("dve", [R("qn"), cs_res], [R("t2")], lambda e: e.tensor_tensor(out=v4(t2), in0=x1, in1=sinb, op=ALU.mult))
            OP("dve", [R("t1"), R("t2")], [out_res], lambda e: e.tensor_tensor(out=o4[:, :, :, 0:64], in0=v4(t1), in1=v4(t2), op=ALU.subtract))
            OP("dve", [R("qn"), cs_res], [R("t3")], lambda e: e.tensor_tensor(out=v4(t3), in0=x0, in1=sinb, op=ALU.mult))
            OP("dve", [R("qn"), cs_res], [R("t4")], lambda e: e.tensor_tensor(out=v4(t4), in0=x1, in1=cosb, op=ALU.mult))
            OP("dve", [R("t3"), R("t4")], [out_res], lambda e: e.tensor_tensor(out=o4[:, :, :, 64:128], in0=v4(t3), in1=v4(t4), op=ALU.add))

        def alloc_x_bufs():
            return dict(ctr=0, xt=[AR.alloc([D], F32) for _ in range(2)], ssq=[AR.alloc([1], F32) for _ in range(2)],
                        xn=[AR.alloc([D], BF16) for _ in range(2)], junk=AR.alloc([D], BF16))

        def alloc_rope_bufs(nbuf=2):
            n = 8 * HD
            d = dict(sl=0)
            for k, sz in (("qf", n), ("sq", n), ("ssq", 8), ("qn", n), ("t1", n // 2), ("t2", n // 2), ("t3", n // 2), ("t4", n // 2)):
                d[k] = [AR.alloc([sz], F32) for _ in range(nbuf)]
                if nbuf == 1:
                    d[k] = d[k] * 2
            if nbuf == 1:
                d["sl"] = 0
            d["nbuf"] = nbuf
            return d

        def phase_A1(x_all, S, cs_all):
            AR.reset()
            Wkv = AR.alloc([DC, 512], BF16)
            Wxl = AR.alloc([DC, 1024], BF16)
            wload(Wkv, w_in_v[:, :, OK_:OK_ + 512])
            wload(Wxl, w_in_v[:, :, OXL:OXL + 1024])
            xb = alloc_x_bufs()
            rb = alloc_rope_bufs(2)
            hTs = [AR.alloc([DC, CH], BF16) for _ in range(2)]
            cst = [AR.alloc([4, HD], F32) for _ in range(2)]
            kb = [AR.alloc([4 * KVH * HD], BF16) for _ in range(2)]
            kTst = [AR.alloc([KVH, CH], BF16) for _ in range(2)]
            vst = [AR.alloc([KVH, 4, HD], BF16) for _ in range(2)]
            xlst = [AR.alloc([8, CH], F32) for _ in range(2)]
            nchunk = S // CH
            make_hT(xb, x_all, 0, 4, hTs[0], "hT0", banks=(0, 1))
            for j in range(nchunk):
                s2 = j % 2
                t0 = j * CH
                hT = hTs[s2]
                hres = "hT%d" % s2
                DMA("sp", cst[s2], cs_all[t0:t0 + CH, :].rearrange("(i p) d -> p i d", p=128), [], ["cs%d" % s2])
                if j + 1 < nchunk:
                    make_hT(xb, x_all, t0 + CH, 4, hTs[1 - s2], "hT%d" % (1 - s2), banks=(0, 1))
                for i in range(4):
                    b = 2 + i
                    for c in range(DC):
                        OP("pe", [hres, "W"], ["pb%d" % b], lambda e, c=c, b=b, i=i, hT=hT: e.matmul(bank(b), lhsT=hT[:, c, i * 128:(i + 1) * 128], rhs=Wkv[:, c, :], start=(c == 0), stop=(c == DC - 1)))
                kvr = ["pb%d" % (2 + i) for i in range(4)]
                kv4 = ps[:, 2 * 512:6 * 512].rearrange("p (i x) -> p i x", i=4)
                OP("act", kvr, ["vst%d" % s2], lambda e, s2=s2, kv4=kv4: e.activation(out=vst[s2].rearrange("p g i d -> p i g d"), in_=kv4[:, :, 256:512].rearrange("p i (g d) -> p i g d", g=KVH), func=AF.Copy))
                norm_rope(rb, kv4[:, :, 0:256], 4, KVH, kn_bc, cst[s2], "cs%d" % s2, kvr, kb[s2], "kb%d" % s2)
                for u in range(4 * KVH):
                    OP("pe", ["kb%d" % s2], ["pb6"], lambda e, u=u, s2=s2: e.transpose(out=bankb(6)[:, u * 128:(u + 1) * 128], in_=kb[s2][:, u * HD:(u + 1) * HD], identity=ident_b))
                OP("dve", ["pb6"], ["kTst%d" % s2], lambda e, s2=s2: e.tensor_copy(out=kTst[s2].rearrange("p g (i t) -> p i g t", i=4), in_=bankb(6).rearrange("p (i g t) -> p i g t", i=4, g=KVH)))
                for g in range(KVH):
                    DMA("sp", kT_scr[g, :, t0:t0 + CH], kTst[s2][:, g, :], ["kTst%d" % s2], [])
                    DMA("sp", v_scr[g, :, j * 4:(j + 1) * 4, :], vst[s2][:, g, :, :], ["vst%d" % s2], [])
                for eb in range(8):
                    b = 7 if eb % 2 == 0 else 6
                    for c in range(DC):
                        OP("pe", [hres, "W"], ["pb%d" % b], lambda e, c=c, b=b, eb=eb, hT=hT: e.matmul(bank(b), lhsT=Wxl[:, c, eb * 128:(eb + 1) * 128], rhs=hT[:, c, :], start=(c == 0), stop=(c == DC - 1)))
                    if eb % 2 == 0:
                        OP("act", ["pb%d" % b], ["xlst%d" % s2], lambda e, b=b, eb=eb, s2=s2: e.activation(out=xlst[s2][:, eb, :], in_=bank(b), func=AF.Copy))
                    else:
                        OP("dve", ["pb%d" % b], ["xlst%d" % s2], lambda e, b=b, eb=eb, s2=s2: e.tensor_copy(out=xlst[s2][:, eb, :], in_=bank(b)))
                DMA("sp", xl_scr[:, :, t0:t0 + CH], xlst[s2], ["xlst%d" % s2], [])
            PG.barrier()

        def phase_LRU(S, direction, own_acc):
            AR.reset()
            nb = 1 if own_acc else 2
            BG = 4
            xlh = [AR.alloc([8, CH + 3], F32) for _ in range(nb)]
            hst = [AR.alloc([8, CH], F32) for _ in range(2)]
            hfl = [AR.alloc([8, CH], F32) for _ in range(nb)] if direction == 1 else None
            acc = AR.alloc([8, OWN], F32) if own_acc else None
            xc = [AR.alloc([CH], F32) for _ in range(BG)]
            xcb = [AR.alloc([CH], BF16) for _ in range(BG)]
            thr = [AR.alloc([CH], F32) for _ in range(BG)]
            thi = [AR.alloc([CH], F32) for _ in range(BG)]
            ngb = 1 if own_acc else 2
            a_t = [AR.alloc([BG, CH], F32) for _ in range(ngb)]
            s_t = [AR.alloc([BG, CH], F32) for _ in range(ngb)]
            w_t = [AR.alloc([BG, CH], F32) for _ in range(ngb)]
            if own_acc:
                OP("dve", [], ["acc"], lambda e: e.memset(acc, 0.0))
            nchunk = S // CH
            order = list(range(nchunk)) if direction == 0 else list(range(nchunk - 1, -1, -1))

            def load(k):
                j = order[k]
                sl = k % nb
                t0 = j * CH
                lo = max(t0 - 2, 0)
                hi = min(t0 + CH + 1, S)
                if t0 == 0:
                    OP("dve", [], ["xlh%d" % sl], lambda e, sl=sl: e.memset(xlh[sl][:, :, 0:2], 0.0))
                if t0 + CH == S:
                    OP("dve", [], ["xlh%d" % sl], lambda e, sl=sl: e.memset(xlh[sl][:, :, CH + 2:CH + 3], 0.0))
                DMA("sp", xlh[sl][:, :, lo - (t0 - 2):hi - (t0 - 2)], xl_scr[:, :, lo:hi], [], ["xlh%d" % sl])
                if direction == 1:
                    DMA("sp", hfl[sl], hf_scr[:, :, t0:t0 + CH], [], ["hfl%d" % sl])

            load(0)
            gctr = 0
            for k in range(nchunk):
                j = order[k]
                sl = k % nb
                hs = k % 2
                t0 = j * CH
                if k + 1 < nchunk and nb == 2:
                    load(k + 1)
                for g0 in range(0, 8, BG):
                    gs = gctr % ngb
                    gctr += 1
                    for bi in range(BG):
                        eb = g0 + bi
                        cw = chp_t[:, eb, :]
                        src = xlh[sl][:, eb, :]
                        OP("act", ["xlh%d" % sl, "chp"], ["xc%d" % bi], lambda e, bi=bi, src=src, cw=cw: e.activation(out=xc[bi], in_=src[:, 0:CH], func=AF.Identity, scale=cw[:, 0:1], bias=cw[:, 4:5]))
                        for tap in range(1, 4):
                            OP("dve", ["xlh%d" % sl, "xc%d" % bi], ["xc%d" % bi], lambda e, bi=bi, src=src, cw=cw, tap=tap: e.scalar_tensor_tensor(out=xc[bi], in0=src[:, tap:tap + CH], scalar=cw[:, tap:tap + 1], in1=xc[bi], op0=ALU.mult, op1=ALU.add))
                    for bi in range(BG):
                        eb = g0 + bi
                        OP("act", ["xc%d" % bi], ["xcb%d" % bi], lambda e, bi=bi: e.activation(out=xcb[bi], in_=xc[bi], func=AF.Copy))
                        OP("pe", ["xcb%d" % bi, "wg"], ["pb%d" % (2 * bi)], lambda e, bi=bi, eb=eb: e.matmul(bank(2 * bi), lhsT=wg[:, 0 + direction, eb, :], rhs=xcb[bi], start=True, stop=True))
                        OP("pe", ["xcb%d" % bi, "wg"], ["pb%d" % (2 * bi + 1)], lambda e, bi=bi, eb=eb: e.matmul(bank(2 * bi + 1), lhsT=wg[:, 2 + direction, eb, :], rhs=xcb[bi], start=True, stop=True))
                    for bi in range(BG):
                        eb = g0 + bi
                        ci = direction * 8 + eb
                        OP("act", ["pb%d" % (2 * bi)], ["thr%d" % bi], lambda e, bi=bi, ci=ci: e.activation(out=thr[bi], in_=bank(2 * bi), func=AF.Tanh, scale=0.5, bias=brh[:, ci:ci + 1]))
                        OP("act", ["pb%d" % (2 * bi + 1)], ["thi%d" % bi], lambda e, bi=bi, ci=ci: e.activation(out=thi[bi], in_=bank(2 * bi + 1), func=AF.Tanh, scale=0.5, bias=bih[:, ci:ci + 1]))
                        OP("act", ["thr%d" % bi], ["a%d" % gs], lambda e, bi=bi, gs=gs, ci=ci: e.activation(out=a_t[gs][:, bi, :], in_=thr[bi], func=AF.Exp, scale=cl_h[:, ci:ci + 1], bias=cl_h[:, ci:ci + 1]))
                        OP("act", ["thr%d" % bi], ["s%d" % gs], lambda e, bi=bi, gs=gs, ci=ci: e.activation(out=s_t[gs][:, bi, :], in_=thr[bi], func=AF.Exp, scale=cl_1[:, ci:ci + 1], bias=cl_1[:, ci:ci + 1]))
                        OP("dve", ["thi%d" % bi, "xc%d" % bi], ["w%d" % gs], lambda e, bi=bi, gs=gs: e.scalar_tensor_tensor(out=w_t[gs][:, bi, :], in0=thi[bi], scalar=1.0, in1=xc[bi], op0=ALU.add, op1=ALU.mult))
                    OP("act", ["s%d" % gs], ["s%d" % gs], lambda e, gs=gs: e.activation(out=s_t[gs], in_=s_t[gs], func=AF.Sqrt, scale=-1.0, bias=1.0))
                    OP("dve", ["s%d" % gs, "w%d" % gs], ["w%d" % gs], lambda e, gs=gs: e.scalar_tensor_tensor(out=w_t[gs], in0=s_t[gs], scalar=0.5, in1=w_t[gs], op0=ALU.mult, op1=ALU.mult))
                    for bi in range(BG):
                        eb = g0 + bi
                        if k == 0:
                            init = 0.0
                            rd = []
                        else:
                            pcol = CH - 1 if direction == 0 else 0
                            init = hst[1 - hs][:, eb, pcol:pcol + 1]
                            rd = ["hst%d" % (1 - hs)]
                        if direction == 0:
                            OP("dve", ["a%d" % gs, "w%d" % gs] + rd, ["hst%d" % hs], lambda e, gs=gs, bi=bi, eb=eb, hs=hs, init=init: e.tensor_tensor_scan(out=hst[hs][:, eb, :], data0=a_t[gs][:, bi, :], data1=w_t[gs][:, bi, :], initial=init, op0=ALU.mult, op1=ALU.add))
                        else:
                            OP("dve", ["a%d" % gs, "w%d" % gs] + rd, ["hst%d" % hs], lambda e, gs=gs, bi=bi, eb=eb, hs=hs, init=init: e.tensor_tensor_scan(out=hst[hs][:, eb, ::-1], data0=a_t[gs][:, bi, ::-1], data1=w_t[gs][:, bi, ::-1], initial=init, op0=ALU.mult, op1=ALU.add))
                if direction == 0:
                    DMA("sp", hf_scr[:, :, t0:t0 + CH], hst[hs], ["hst%d" % hs], [])
                else:
                    if own_acc:
                        slot = j % NSLOT
                        grp = j // NSLOT
                        OP("dve", ["hst%d" % hs, "hfl%d" % sl], ["hfl%d" % sl], lambda e, hs=hs, sl=sl: e.tensor_tensor(out=hfl[sl], in0=hfl[sl], in1=hst[hs], op=ALU.add))
                        OP("dve", ["hfl%d" % sl, "acc", "msk"], ["acc"], lambda e, sl=sl, slot=slot, grp=grp: e.scalar_tensor_tensor(out=acc[:, :, slot * CH:(slot + 1) * CH], in0=hfl[sl], scalar=msk_t[:, grp:grp + 1], in1=acc[:, :, slot * CH:(slot + 1) * CH], op0=ALU.mult, op1=ALU.add))
                    else:
                        OP("dve", ["hst%d" % hs, "hfl%d" % sl], ["hfl%d" % sl], lambda e, hs=hs, sl=sl: e.tensor_tensor(out=hfl[sl], in0=hfl[sl], in1=hst[hs], op=ALU.add))
                        DMA("sp", ls_scr[:, :, t0:t0 + CH], hfl[sl], ["hfl%d" % sl], [])
                if k + 1 < nchunk and nb == 1:
                    load(k + 1)
            if own_acc:
                DMA("sp", ls_scr[:, :, 0:OWN], acc, ["acc"], [])
            PG.barrier()

        def phase_B1(x_own, N_OWN, cs_own, S):
            AR.reset()
            QB = 1024 if N_OWN % 1024 == 0 else 512
            QH = QB // 512
            NKT = S // 128
            big = S > 8192
            Wq = AR.alloc([DC, 1024], BF16)
            Wga = AR.alloc([DC, 1024], BF16)
            wload(Wq, w_in_v[:, :, OQ:OQ + 1024])
            wload(Wga, w_in_v[:, :, OGA:OGA + 1024])
            KT = AR.alloc([S], BF16)
            VV = AR.alloc([NKT, HD], BF16)
            xb = alloc_x_bufs()
            rb = alloc_rope_bufs(1 if big else 2)
            hT = AR.alloc([DC, QB], BF16)
            qT = AR.alloc([H, QB], BF16)
            cst = [AR.alloc([1, HD], F32) for _ in range(2)]
            qb_ = [AR.alloc([H * HD], BF16) for _ in range(2)]
            pT = [AR.alloc([QB], BF16) for _ in range(3)]
            rs = AR.alloc([QB], F32)
            nsg = 1 if big else 2
            sg = [AR.alloc([QB], F32) for _ in range(nsg)]
            aob = [AR.alloc([QB], BF16)]
            tctr = 0
            cnt = [0]
            hctr = 0
            for qb in range(N_OWN // QB):
                q0 = qb * QB
                make_hT(xb, x_own, q0, QB // 128, hT, "hT")
                for i in range(QB // 128):
                    ts = tctr % 2
                    tctr += 1
                    b0 = 2 * (next_bank() % 4)
                    for hf in range(2):
                        for c in range(DC):
                            OP("pe", ["hT", "W"], ["pb%d" % (b0 + hf)], lambda e, c=c, b0=b0, hf=hf, i=i: e.matmul(bank(b0 + hf), lhsT=hT[:, c, i * 128:(i + 1) * 128], rhs=Wq[:, c, hf * 512:(hf + 1) * 512], start=(c == 0), stop=(c == DC - 1)))
                    DMA("sp", cst[ts], cs_own[q0 + i * 128:q0 + (i + 1) * 128, :].rearrange("(i p) d -> p i d", p=128), [], ["cs%d" % ts])
                    norm_rope(rb, ps[:, b0 * 512:b0 * 512 + 1024].rearrange("p (t x) -> p t x", t=1), 1, H, qn_bc, cst[ts], "cs%d" % ts, ["pb%d" % b0, "pb%d" % (b0 + 1)], qb_[ts], "qb%d" % ts)
                    b2 = next_bank()
                    for h in range(H):
                        OP("pe", ["qb%d" % ts], ["pb%d" % b2], lambda e, h=h, b2=b2, ts=ts: e.transpose(out=bankb(b2)[:, h * 128:(h + 1) * 128], in_=qb_[ts][:, h * HD:(h + 1) * HD], identity=ident_b))
                    OP("dve", ["pb%d" % b2], ["qT"], lambda e, b2=b2, i=i: e.tensor_copy(out=qT[:, :, i * 128:(i + 1) * 128], in_=bankb(b2).rearrange("p (h t) -> p h t", h=H)))
                for g in range(KVH):
                    DMA("sp", KT, kT_scr[g, :, 0:S], [], ["KT"])
                    DMA("sp", VV, v_scr[g, :, 0:NKT, :], [], ["VV"])
                    for hh in range(H // KVH):
                        h = g * (H // KVH) + hh
                        hsg = hctr % nsg
                        hctr += 1
                        for hf in range(QH):
                            for c in range(DC):
                                OP("pe", ["hT", "W"], ["pb%d" % hf], lambda e, c=c, hf=hf, h=h: e.matmul(bank(hf), lhsT=Wga[:, c, h * 128:(h + 1) * 128], rhs=hT[:, c, hf * 512:(hf + 1) * 512], start=(c == 0), stop=(c == DC - 1)))
                        OP("act", ["pb%d" % hf for hf in range(QH)], ["sg%d" % hsg], lambda e, hsg=hsg: e.activation(out=sg[hsg], in_=ps[:, 0:QB], func=AF.Silu))
                        slots = []

                        def emit_qk(kt, h=h):
                            sp_ = (cnt[0] % 2) * 2
                            pt = cnt[0] % 3
                            cnt[0] += 1
                            slots.append((sp_, pt))
                            for hf in range(QH):
                                OP("pe", ["KT", "qT"], ["pb%d" % (sp_ + hf)], lambda e, kt=kt, sp_=sp_, hf=hf, h=h: e.matmul(bank(sp_ + hf), lhsT=KT[:, kt * 128:(kt + 1) * 128], rhs=qT[:, h, hf * 512:(hf + 1) * 512], start=True, stop=True))

                        emit_qk(0)
                        for kt in range(NKT):
                            if kt + 1 < NKT:
                                emit_qk(kt + 1)
                            sp_, pt = slots[kt]
                            OP("act", ["pb%d" % (sp_ + hf) for hf in range(QH)] + ["negB"], ["pT%d" % pt], lambda e, sp_=sp_, pt=pt: e.activation(out=pT[pt], in_=ps[:, sp_ * 512:sp_ * 512 + QB], func=AF.Exp, scale=SM_SCALE, bias=negB[:, 0:1]))
                            for hf in range(QH):
                                OP("pe", ["pT%d" % pt, "VV"], ["pb%d" % (4 + hf)], lambda e, kt=kt, pt=pt, hf=hf: e.matmul(bank(4 + hf), lhsT=VV[:, kt, :], rhs=pT[pt][:, hf * 512:(hf + 1) * 512], start=(kt == 0), stop=(kt == NKT - 1)))
                                OP("pe", ["pT%d" % pt, "ones_b"], ["pb%d" % (6 + hf)], lambda e, kt=kt, pt=pt, hf=hf: e.matmul(bank(6 + hf), lhsT=ones_b, rhs=pT[pt][:, hf * 512:(hf + 1) * 512], start=(kt == 0), stop=(kt == NKT - 1)))
                        accr = ["pb%d" % (4 + hf) for hf in range(QH)]
                        sumr = ["pb%d" % (6 + hf) for hf in range(QH)]
                        OP("dve", sumr, ["rs"], lambda e: e.reciprocal(out=rs, in_=ps[:, 6 * 512:6 * 512 + QB]))
                        OP("dve", accr + ["rs"], ["rs"], lambda e: e.tensor_tensor(out=rs, in0=ps[:, 4 * 512:4 * 512 + QB], in1=rs, op=ALU.mult))
                        OP("dve", ["rs", "sg%d" % hsg], ["aob0"], lambda e, hsg=hsg: e.tensor_tensor(out=aob[0], in0=rs, in1=sg[hsg], op=ALU.mult))
                        DMA("sp", ao_scr[:, h, q0:q0 + QB], aob[0], ["aob0"], [])
            PG.barrier()

        def phase_B2(x_own, N_OWN, y_out):
            AR.reset()
            Wgl = AR.alloc([DC, 1024], BF16)
            Wml = AR.alloc([DC, 2048], BF16)
            WbA = AR.alloc([DC, 1024], BF16)
            WbL = AR.alloc([DC, 1024], BF16)
            Wo = AR.alloc([DC, 1024], BF16)
            wload(Wgl, w_in_v[:, :, OGL:OGL + 1024])
            wload(Wml[:, :, 0:1024], w_in_v[:, :, OML:OML + 1024])
            wload(Wml[:, :, 1024:2048], w_in_v[:, :, OML + 1024:OML + 2048])
            wload(WbA, w_br[0].rearrange("(c p) n -> p c n", p=128))
            wload(WbL, w_br[1].rearrange("(c p) n -> p c n", p=128))
            wload(Wo, w_out.rearrange("(c p) n -> p c n", p=128))
            xb = alloc_x_bufs()
            hT = AR.alloc([DC, CH], BF16)
            hres = "hT"
            lsl = [AR.alloc([CH], F32) for _ in range(2)]
            aol = AR.alloc([8, CH], BF16)
            loT = AR.alloc([8, CH], BF16)
            mT = AR.alloc([8, CH], BF16)
            sgl = [AR.alloc([CH], F32) for _ in range(2)]
            ga = [AR.alloc([CH], F32) for _ in range(2)]
            gl = [AR.alloc([CH], F32) for _ in range(2)]
            t1 = [AR.alloc([CH], F32) for _ in range(2)]
            t2 = [AR.alloc([CH], F32) for _ in range(2)]
            yt = [AR.alloc([D], F32)]
            yo = [AR.alloc([D], F32)]
            yss = [AR.alloc([1], F32)]
            xres = [AR.alloc([D], F32)]
            junk2 = xb["junk"]
            nchunk = N_OWN // CH
            ectr = 0
            yctr = 0
            for j in range(nchunk):
                t0 = j * CH
                DMA("sp", aol, ao_scr[:, :, t0:t0 + CH], [], ["aol"])
                make_hT(xb, x_own, t0, 4, hT, hres)
                for eb in range(8):
                    es = ectr % 2
                    ectr += 1
                    DMA("sp", lsl[es], ls_scr[:, eb, t0:t0 + CH], [], ["lsl%d" % es])
                    b = next_bank()
                    for c in range(DC):
                        OP("pe", [hres, "W"], ["pb%d" % b], lambda e, c=c, b=b, eb=eb: e.matmul(bank(b), lhsT=Wgl[:, c, eb * 128:(eb + 1) * 128], rhs=hT[:, c, :], start=(c == 0), stop=(c == DC - 1)))
                    OP("act", ["pb%d" % b], ["sgl%d" % es], lambda e, b=b, es=es: e.activation(out=sgl[es], in_=bank(b), func=AF.Silu))
                    OP("dve", ["sgl%d" % es, "lsl%d" % es], ["loT"], lambda e, es=es, eb=eb: e.tensor_tensor(out=loT[:, eb, :], in0=lsl[es], in1=sgl[es], op=ALU.mult))
                for eb in range(8):
                    es = ectr % 2
                    ectr += 1
                    bA, bL, bga, bgl = next_bank(), next_bank(), next_bank(), next_bank()
                    for c in range(DC):
                        OP("pe", ["aol", "W"], ["pb%d" % bA], lambda e, c=c, bA=bA, eb=eb: e.matmul(bank(bA), lhsT=WbA[:, c, eb * 128:(eb + 1) * 128], rhs=aol[:, c, :], start=(c == 0), stop=(c == DC - 1)))
                    for c in range(DC):
                        OP("pe", ["loT", "W"], ["pb%d" % bL], lambda e, c=c, bL=bL, eb=eb: e.matmul(bank(bL), lhsT=WbL[:, c, eb * 128:(eb + 1) * 128], rhs=loT[:, c, :], start=(c == 0), stop=(c == DC - 1)))
                    for c in range(DC):
                        OP("pe", [hres, "W"], ["pb%d" % bga], lambda e, c=c, bga=bga, eb=eb: e.matmul(bank(bga), lhsT=Wml[:, c, eb * 128:(eb + 1) * 128], rhs=hT[:, c, :], start=(c == 0), stop=(c == DC - 1)))
                    for c in range(DC):
                        OP("pe", [hres, "W"], ["pb%d" % bgl], lambda e, c=c, bgl=bgl, eb=eb: e.matmul(bank(bgl), lhsT=Wml[:, c, 1024 + eb * 128:1024 + (eb + 1) * 128], rhs=hT[:, c, :], start=(c == 0), stop=(c == DC - 1)))
                    OP("act", ["pb%d" % bga, "bm"], ["ga%d" % es], lambda e, bga=bga, es=es, eb=eb: e.activation(out=ga[es], in_=bank(bga), func=AF.Sigmoid, bias=bm_t[:, eb:eb + 1]))
                    OP("act", ["pb%d" % bgl, "bm"], ["gl%d" % es], lambda e, bgl=bgl, es=es, eb=eb: e.activation(out=gl[es], in_=bank(bgl), func=AF.Sigmoid, bias=bm_t[:, 8 + eb:9 + eb]))
                    OP("dve", ["pb%d" % bA, "ga%d" % es], ["t1%d" % es], lambda e, bA=bA, es=es: e.tensor_tensor(out=t1[es], in0=bank(bA), in1=ga[es], op=ALU.mult))
                    OP("dve", ["pb%d" % bL, "gl%d" % es], ["t2%d" % es], lambda e, bL=bL, es=es: e.tensor_tensor(out=t2[es], in0=bank(bL), in1=gl[es], op=ALU.mult))
                    OP("pool", ["t1%d" % es, "t2%d" % es], ["mT"], lambda e, es=es, eb=eb: e.tensor_tensor(out=mT[:, eb, :], in0=t1[es], in1=t2[es], op=ALU.add))
                for i in range(4):
                    ys_ = 0
                    b0 = 2 * (next_bank() % 4)
                    DMA("sp", xres[ys_], x_own[t0 + i * 128:t0 + (i + 1) * 128, :], [], ["xres%d" % ys_])
                    for hf in range(2):
                        for c in range(DC):
                            OP("pe", ["mT", "W"], ["pb%d" % (b0 + hf)], lambda e, c=c, b0=b0, hf=hf, i=i: e.matmul(bank(b0 + hf), lhsT=mT[:, c, i * 128:(i + 1) * 128], rhs=Wo[:, c, hf * 512:(hf + 1) * 512], start=(c == 0), stop=(c == DC - 1)))
                    OP("dve", ["pb%d" % b0, "pb%d" % (b0 + 1), "xres%d" % ys_], ["yt%d" % ys_], lambda e, b0=b0, ys_=ys_: e.tensor_tensor(out=yt[ys_], in0=ps[:, b0 * 512:b0 * 512 + 1024], in1=xres[ys_], op=ALU.add))
                    OP("act", ["yt%d" % ys_], ["junk2", "yss%d" % ys_], lambda e, ys_=ys_: e.activation(out=junk2, in_=yt[ys_], func=AF.Square, accum_out=yss[ys_]))
                    OP("act", ["yss%d" % ys_], ["yss%d" % ys_], lambda e, ys_=ys_: e.activation(out=yss[ys_], in_=yss[ys_], func=AF.Sqrt, scale=1.0 / D, bias=EPS))
                    OP("dve", ["yss%d" % ys_], ["yss%d" % ys_], lambda e, ys_=ys_: e.reciprocal(out=yss[ys_], in_=yss[ys_]))
                    OP("dve", ["yt%d" % ys_, "yss%d" % ys_], ["yo%d" % ys_], lambda e, ys_=ys_: e.scalar_tensor_tensor(out=yo[ys_], in0=yt[ys_], scalar=yss[ys_][:, 0:1], in1=gfin_bc, op0=ALU.mult, op1=ALU.mult))
                    DMA("sp", y_out[t0 + i * 128:t0 + (i + 1) * 128, :], yo[ys_], ["yo%d" % ys_], [])
            PG.barrier()

        for (x_all, S, cs_all, x_own, n_own, cs_own, y_out, is_p) in (
            (xs, S_S, cs_s, xs, S_S, cs_s, ys, False),
            (xp, S_P, cs_p, xq, OWN, cs_q, yq, True),
        ):
            phase_A1(x_all, S, cs_all)
            phase_LRU(S, 0, False)
            phase_LRU(S, 1, is_p)
            phase_B1(x_own, n_own, cs_own, S)
            phase_B2(x_own, n_own, y_out)
        PG.emit(nc)
    return nc


def _rope_table(S):
    grid_w = 64
    pos = np.arange(S)
    rows = (pos // grid_w).astype(np.float32)
    cols = (pos % grid_w).astype(np.float32)
    n = HD // 4
    inv = (np.float32(10000.0) ** (-(np.arange(n, dtype=np.float32) / np.float32(n)))).astype(np.float32)
    ang = np.concatenate([rows[:, None] * inv[None, :], cols[:, None] * inv[None, :]], axis=-1).astype(np.float32)
    return np.concatenate([np.cos(ang), np.sin(ang)], axis=-1).astype(np.float32)


_NC_CACHE = {}


def kernel(x_prompt, x_sample, norm_in, w_in, b_merge, q_norm, k_norm, conv_w, conv_b,
           w_rgate, b_rgate, w_igate, b_igate, lam, w_branch, w_out, norm_final):
    f = lambda a: np.ascontiguousarray(np.asarray(a, dtype=np.float32))
    x_prompt, x_sample = f(x_prompt), f(x_sample)
    S_P = x_prompt.shape[1]
    S_S = x_sample.shape[1]
    OWN = S_P // NCORES
    assert x_prompt.shape[0] == 1 and x_sample.shape[0] == NCORES
    perm = np.concatenate([np.arange(0, HD, 2), np.arange(1, HD, 2)])
    cols = np.arange(NPROJ)
    for hh in range(H):
        cols[OQ + hh * HD:OQ + (hh + 1) * HD] = OQ + hh * HD + perm
    for hh in range(KVH):
        cols[OK_ + hh * HD:OK_ + (hh + 1) * HD] = OK_ + hh * HD + perm
    w_in_p = f(f(w_in)[0][:, cols])
    qn = f(f(q_norm)[0][perm][None, :])
    kn = f(f(k_norm)[0][perm][None, :])
    plist = [f(conv_w)[0][0], f(conv_w)[0][1], f(conv_w)[0][2], f(conv_w)[0][3], f(conv_b)[0],
             f(b_rgate)[0][0], f(b_rgate)[0][1], f(b_igate)[0][0], f(b_igate)[0][1], f(lam)[0][0], f(lam)[0][1]]
    chp = f(np.stack(plist, axis=-1).reshape(8, 128, 11).transpose(1, 0, 2))
    w_gate = f(np.stack([f(w_rgate)[0][0], f(w_rgate)[0][1], f(w_igate)[0][0], f(w_igate)[0][1]], axis=0))
    bm = f(f(b_merge)[0].reshape(16, 128).T)
    cs_p = _rope_table(S_P)
    cs_s = _rope_table(S_S)
    ident = np.eye(128, dtype=np.float32)
    key = (S_S, S_P)
    if key not in _NC_CACHE:
        _NC_CACHE[key] = build_nc(S_S, S_P)
    nc = _NC_CACHE[key]
    xp2 = x_prompt[0]
    in_maps = []
    for c in range(NCORES):
        m = np.zeros((128, NCORES), np.float32)
        m[:, c] = 1.0
        in_maps.append(dict(
            xs=x_sample[c], xp=xp2, xq=f(xp2[c * OWN:(c + 1) * OWN]),
            cs_s=cs_s, cs_p=cs_p, cs_q=f(cs_p[c * OWN:(c + 1) * OWN]),
            w_in=w_in_p, w_br=f(w_branch)[0], w_out=f(w_out)[0], w_gate=w_gate, chp=chp,
            g_in=f(norm_in), g_fin=f(norm_final)[None, :], qn=qn, kn=kn, bm=bm, msk=m, ident=ident))
    res = run_bass_kernel_spmd(nc, in_maps, core_ids=list(range(NCORES)))
    y_s = np.stack([np.asarray(res.results[c]["ys"], dtype=np.float32) for c in range(NCORES)], axis=0)
    y_p = np.concatenate([np.asarray(res.results[c]["yq"], dtype=np.float32) for c in range(NCORES)], axis=0)[None]
    return (y_p, y_s)
```

```python
import contextlib
import math
import numpy as np
import concourse.bass as bass
import concourse.mybir as mybir
from concourse.bass_utils import run_bass_kernel_spmd

F32 = mybir.dt.float32
BF16 = mybir.dt.bfloat16
AF = mybir.ActivationFunctionType
ALU = mybir.AluOpType
AX = mybir.AxisListType

NCORES = 8
D = 1024
DC = 8
H = 8
KVH = 2
HD = 128
CH = 512
NPROJ = 6656
EPS = 1e-6
SM_SCALE = 1.0 / math.sqrt(HD)
OQ, OK_, OV, OGA, OXL, OGL, OML = 0, 1024, 1280, 1536, 2560, 3584, 4608


class Prog:
    ENG = ("sp", "act", "dve", "pool", "pe")

    def __init__(self):
        self.ops = []
        self.last_writer = {}
        self.readers = {}
        self.last_on = {e: None for e in self.ENG}
        self.dma_since = set()
        self.pending = {e: set() for e in self.ENG}

    def op(self, eng, fn, reads=(), writes=(), dma=False):
        idx = len(self.ops)
        deps = {}
        for r in reads:
            w = self.last_writer.get(r)
            if w is not None:
                deps[w] = "raw"
        for r in writes:
            w = self.last_writer.get(r)
            if w is not None and w not in deps:
                deps[w] = "waw"
            for rd in self.readers.get(r, ()):
                if rd not in deps:
                    deps[rd] = "war"
        if self.pending[eng]:
            for p in self.pending[eng]:
                deps[p] = "raw"
            self.pending[eng] = set()
        for r in reads:
            self.readers.setdefault(r, []).append(idx)
        for r in writes:
            self.last_writer[r] = idx
            self.readers[r] = []
        self.ops.append(dict(eng=eng, fn=fn, dma=dma, deps=deps, idx=idx))
        self.last_on[eng] = idx
        if dma:
            self.dma_since.add(idx)
        return idx

    def barrier(self):
        s = set(self.dma_since)
        for e in self.ENG:
            if self.last_on[e] is not None:
                s.add(self.last_on[e])
        self.dma_since = set()
        self.pending = {e: set(s) for e in self.ENG}
        self.last_writer.clear()
        self.readers.clear()

    def emit(self, nc, ndma_sems=20):
        ops = self.ops
        for o in ops:
            keep = []
            for p, kind in o["deps"].items():
                po = ops[p]
                if (not po["dma"]) and (not o["dma"]) and po["eng"] == o["eng"]:
                    if not (kind == "raw" and o["eng"] in ("act", "dve", "pool")):
                        continue
                keep.append(p)
            o["deps"] = sorted(keep)
            o["signal"] = False
        for o in ops:
            for p in o["deps"]:
                ops[p]["signal"] = True
        cnt = {e: 0 for e in self.ENG}
        dma_i = {e: 0 for e in self.ENG}
        dma_cum = {}
        for o in ops:
            if o["dma"]:
                slot = (o["eng"], dma_i[o["eng"]] % ndma_sems)
                dma_i[o["eng"]] += 1
                prev = dma_cum.get(slot, 0)
                o["sem"], o["prev"], o["val"] = slot, prev, prev + 16
                dma_cum[slot] = prev + 16
            elif o["signal"]:
                cnt[o["eng"]] += 1
                o["sem"], o["val"] = o["eng"], cnt[o["eng"]]
        dma_queues = sorted({o["eng"] for o in ops if o["dma"]})
        with contextlib.ExitStack() as st:
            sems = {}
            for e in self.ENG:
                sems[e] = st.enter_context(nc.semaphore("s_" + e))
            for q in dma_queues:
                for i in range(ndma_sems):
                    sems[(q, i)] = st.enter_context(nc.semaphore("d_%s_%d" % (q, i)))
            block = st.enter_context(nc.Block())
            per_eng = {e: [o for o in ops if o["eng"] == e] for e in self.ENG}

            def run_engine(ename, eng):
                waited = {}
                for o in per_eng[ename]:
                    need = {}
                    for p in o["deps"]:
                        po = ops[p]
                        s = po["sem"]
                        if po["val"] > need.get(s, 0):
                            need[s] = po["val"]
                    if o["dma"] and o["prev"] > 0:
                        s = o["sem"]
                        if o["prev"] > need.get(s, 0):
                            need[s] = o["prev"]
                    for s, v in need.items():
                        if waited.get(s, 0) >= v:
                            continue
                        eng.wait_ge(sems[s], v)
                        waited[s] = v
                    ins = o["fn"](eng)
                    if o["dma"]:
                        ins.then_inc(sems[o["sem"]], 16)
                    elif o["signal"]:
                        ins.then_inc(sems[o["sem"]], 1)
                if ename == "sp":
                    for slot, v in dma_cum.items():
                        if waited.get(slot, 0) < v:
                            eng.wait_ge(sems[slot], v)

            @block.sync
            def _(e):
                run_engine("sp", e)

            @block.scalar
            def _(e):
                run_engine("act", e)

            @block.vector
            def _(e):
                run_engine("dve", e)

            @block.gpsimd
            def _(e):
                run_engine("pool", e)

            @block.tensor
            def _(e):
                run_engine("pe", e)


class Arena:
    def __init__(self, ap, nwords):
        self.ap = ap
        self.n = nwords
        self.off = 0
        self.mark = 0

    def alloc(self, shape, dt):
        n = 1
        for s in shape:
            n *= s
        words = n if dt == F32 else (n + 1) // 2
        words = (words + 7) // 8 * 8
        assert self.off + words <= self.n, ("arena overflow", self.off, words, self.n)
        v = self.ap[:, self.off:self.off + words]
        self.off += words
        if dt != F32:
            v = v.bitcast(dt)
        v = v[:, 0:n]
        if len(shape) == 2:
            v = v.rearrange("p (a b) -> p a b", a=shape[0])
        elif len(shape) == 3:
            v = v.rearrange("p (a b c) -> p a b c", a=shape[0], b=shape[1])
        return v

    def persist(self):
        self.mark = self.off

    def reset(self):
        self.off = self.mark


def build_nc(S_S, S_P):
    OWN = S_P // NCORES
    assert OWN % CH == 0 and S_S % CH == 0
    SMAX = max(S_S, S_P)
    NSLOT = OWN // CH
    nc = bass.Bass("TRN2", target_bir_lowering=False)

    def din(name, shape, dt=F32):
        return nc.dram_tensor(name, list(shape), dt, kind="ExternalInput").ap()

    xs = din("xs", [S_S, D])
    xp = din("xp", [S_P, D])
    xq = din("xq", [OWN, D])
    cs_s = din("cs_s", [S_S, HD])
    cs_p = din("cs_p", [S_P, HD])
    cs_q = din("cs_q", [OWN, HD])
    w_in = din("w_in", [D, NPROJ])
    w_br = din("w_br", [2, D, D])
    w_out = din("w_out", [D, D])
    w_gate = din("w_gate", [4, 8, 128, 128])
    chp = din("chp", [128, 8, 11])
    g_in = din("g_in", [1, D])
    g_fin = din("g_fin", [1, D])
    qn = din("qn", [1, HD])
    kn = din("kn", [1, HD])
    bm = din("bm", [128, 16])
    msk = din("msk", [128, NCORES])
    ident = din("ident", [128, 128])
    ys = nc.dram_tensor("ys", [S_S, D], F32, kind="ExternalOutput").ap()
    yq = nc.dram_tensor("yq", [OWN, D], F32, kind="ExternalOutput").ap()
    kT_scr = nc.dram_tensor("kT_scr", [KVH, 128, SMAX], BF16).ap()
    v_scr = nc.dram_tensor("v_scr", [KVH, 128, SMAX // 128, HD], BF16).ap()
    xl_scr = nc.dram_tensor("xl_scr", [128, 8, SMAX], F32).ap()
    hf_scr = nc.dram_tensor("hf_scr", [128, 8, SMAX], F32).ap()
    ls_scr = nc.dram_tensor("ls_scr", [128, 8, max(S_S, OWN)], F32).ap()
    ao_scr = nc.dram_tensor("ao_scr", [128, 8, max(S_S, OWN)], BF16).ap()

    PG = Prog()
    w_in_v = w_in.rearrange("(c p) n -> p c n", p=128)

    def OP(eng, reads, writes, f):
        PG.op(eng, f, reads, writes)

    def DMA(eng, out, in_, reads, writes):
        PG.op(eng, lambda e: e.dma_start(out=out, in_=in_), reads, writes, dma=True)

    with contextlib.ExitStack() as st:
        NW = 53000
        arena_t = st.enter_context(nc.sbuf_tensor("arena", [128, NW], F32))
        ps_t = st.enter_context(nc.psum_tensor("ps", [128, 4096], F32))
        AR = Arena(arena_t, NW)
        ps = ps_t
        psb = ps_t[:].bitcast(BF16)

        def bank(b, n=512):
            return ps[:, b * 512:b * 512 + n]

        def bankb(b, n=1024):
            return psb[:, b * 1024:b * 1024 + n]

        bank_ctr = [0]

        def next_bank():
            b = bank_ctr[0] % 8
            bank_ctr[0] += 1
            return b

        ident_b = AR.alloc([128], BF16)
        ones_b = AR.alloc([128], BF16)
        ones_f = AR.alloc([128], F32)
        gin_bc = AR.alloc([D], F32)
        gfin_bc = AR.alloc([D], F32)
        qn_bc = AR.alloc([HD], F32)
        kn_bc = AR.alloc([HD], F32)
        chp_t = AR.alloc([8, 11], F32)
        bm_t = AR.alloc([16], F32)
        bmh_t = AR.alloc([16], F32)
        msk_t = AR.alloc([NCORES], F32)
        lam_c = AR.alloc([16], F32)
        cl_h = AR.alloc([16], F32)
        cl_1 = AR.alloc([16], F32)
        brh = AR.alloc([16], F32)
        bih = AR.alloc([16], F32)
        tmpa = AR.alloc([16], F32)
        tmpb = AR.alloc([16], F32)
        tmpc = AR.alloc([16], F32)
        negB = AR.alloc([1], F32)
        mq = AR.alloc([1], F32)
        mk = AR.alloc([1], F32)
        absq = AR.alloc([HD], F32)
        wg = AR.alloc([4, 8, 128], BF16)
        AR.persist()

        DMA("pool", ident_b, ident[:, :], [], ["ident_b"])
        OP("dve", [], ["ones_b"], lambda e: e.memset(ones_b, 1.0))
        OP("dve", [], ["ones_f"], lambda e: e.memset(ones_f, 1.0))
        DMA("sp", gin_bc, g_in[0:1, :].partition_broadcast(128), [], ["gin_bc"])
        DMA("sp", gfin_bc, g_fin[0:1, :].partition_broadcast(128), [], ["gfin_bc"])
        DMA("sp", qn_bc, qn[0:1, :].partition_broadcast(128), [], ["qn_bc"])
        DMA("sp", kn_bc, kn[0:1, :].partition_broadcast(128), [], ["kn_bc"])
        DMA("sp", chp_t, chp[:, :, :], [], ["chp"])
        DMA("sp", bm_t, bm[:, :], [], ["bm"])
        DMA("sp", msk_t, msk[:, :], [], ["msk"])
        for gi in range(4):
            DMA("pool", wg[:, gi, :, :], w_gate[gi].rearrange("b i o -> i b o"), [], ["wg"])
        for d_ in range(2):
            OP("dve", ["chp"], ["lam_c"], lambda e, d_=d_: e.tensor_copy(out=lam_c[:, d_ * 8:(d_ + 1) * 8], in_=chp_t[:, :, 9 + d_]))
            OP("dve", ["chp"], ["brh"], lambda e, d_=d_: e.tensor_scalar(out=brh[:, d_ * 8:(d_ + 1) * 8], in0=chp_t[:, :, 5 + d_], scalar1=0.5, scalar2=None, op0=ALU.mult))
            OP("dve", ["chp"], ["bih"], lambda e, d_=d_: e.tensor_scalar(out=bih[:, d_ * 8:(d_ + 1) * 8], in0=chp_t[:, :, 7 + d_], scalar1=0.5, scalar2=None, op0=ALU.mult))
        OP("dve", ["bm"], ["bmh"], lambda e: e.tensor_scalar(out=bmh_t, in0=bm_t, scalar1=0.5, scalar2=None, op0=ALU.mult))
        OP("act", ["lam_c"], ["tmpa"], lambda e: e.activation(out=tmpa, in_=lam_c, func=AF.Exp, scale=-1.0))
        OP("dve", ["tmpa"], ["tmpb"], lambda e: e.tensor_scalar(out=tmpb, in0=tmpa, scalar1=1.0, scalar2=None, op0=ALU.add))
        OP("act", ["tmpb"], ["tmpc"], lambda e: e.activation(out=tmpc, in_=tmpb, func=AF.Ln))
        OP("dve", ["tmpb"], ["tmpb"], lambda e: e.tensor_scalar(out=tmpb, in0=tmpb, scalar1=-1.0, scalar2=1e-30, op0=ALU.add, op1=ALU.max))
        OP("dve", ["tmpb"], ["tmpb"], lambda e: e.reciprocal(out=tmpb, in_=tmpb))
        OP("dve", ["tmpa", "tmpc"], ["tmpc"], lambda e: e.tensor_tensor(out=tmpc, in0=tmpc, in1=tmpa, op=ALU.mult))
        OP("dve", ["tmpb", "tmpc"], ["tmpc"], lambda e: e.tensor_tensor(out=tmpc, in0=tmpc, in1=tmpb, op=ALU.mult))
        OP("dve", ["tmpc"], ["cl_h"], lambda e: e.tensor_scalar(out=cl_h, in0=tmpc, scalar1=-4.0, scalar2=None, op0=ALU.mult))
        OP("dve", ["tmpc"], ["cl_1"], lambda e: e.tensor_scalar(out=cl_1, in0=tmpc, scalar1=-8.0, scalar2=None, op0=ALU.mult))
        OP("act", ["qn_bc"], ["absq"], lambda e: e.activation(out=absq, in_=qn_bc, func=AF.Abs))
        OP("dve", ["absq"], ["mq"], lambda e: e.tensor_reduce(out=mq, in_=absq, axis=AX.X, op=ALU.max))
        OP("act", ["kn_bc", "mq"], ["absq"], lambda e: e.activation(out=absq, in_=kn_bc, func=AF.Abs))
        OP("dve", ["absq"], ["mk"], lambda e: e.tensor_reduce(out=mk, in_=absq, axis=AX.X, op=ALU.max))
        OP("dve", ["mq", "mk"], ["negB"], lambda e: e.tensor_tensor(out=negB, in0=mq, in1=mk, op=ALU.mult))
        OP("dve", ["negB"], ["negB"], lambda e: e.tensor_scalar(out=negB, in0=negB, scalar1=-math.sqrt(HD), scalar2=None, op0=ALU.mult))
        PG.barrier()

        def wload(dst, src_cols):
            DMA("pool", dst, src_cols, [], ["W"])

        def hT_stageA(bufs, x_ap, t0, tiles):
            for i in tiles:
                sl = i % bufs["depth"]
                xt, ssq, xn, junk = bufs["xt"][sl], bufs["ssq"][sl], bufs["xn"][sl], bufs["junk"]
                xtr = "xt%d" % sl
                DMA("sp", xt, x_ap[t0 + i * 128:t0 + (i + 1) * 128, :], [], [xtr])
                OP("act", [xtr], ["junk", "ssq%d" % sl], lambda e, xt=xt, ssq=ssq: e.activation(out=junk, in_=xt, func=AF.Square, accum_out=ssq))
                OP("act", ["ssq%d" % sl], ["ssq%d" % sl], lambda e, ssq=ssq: e.activation(out=ssq, in_=ssq, func=AF.Sqrt, scale=1.0 / D, bias=EPS))
                OP("dve", ["ssq%d" % sl], ["ssq%d" % sl], lambda e, ssq=ssq: e.reciprocal(out=ssq, in_=ssq))
                OP("dve", [xtr, "ssq%d" % sl], ["xn%d" % sl], lambda e, xt=xt, ssq=ssq, xn=xn: e.scalar_tensor_tensor(out=xn, in0=xt, scalar=ssq[:, 0:1], in1=gin_bc, op0=ALU.mult, op1=ALU.mult))

        def hT_stageB(bufs, tiles, hT, hT_res, banks=None):
            for i in tiles:
                sl = i % bufs["depth"]
                xn = bufs["xn"][sl]
                if banks is None:
                    b = next_bank()
                else:
                    b = banks[i % len(banks)]
                for c in range(DC):
                    OP("pe", ["xn%d" % sl], ["pb%d" % b], lambda e, c=c, b=b, xn=xn: e.transpose(out=bankb(b)[:, c * 128:(c + 1) * 128], in_=xn[:, c * 128:(c + 1) * 128], identity=ident_b))
                OP("act", ["pb%d" % b], [hT_res], lambda e, b=b, i=i: e.activation(out=hT[:, :, i * 128:(i + 1) * 128], in_=bankb(b).rearrange("p (c t) -> p c t", c=DC), func=AF.Copy))

        def make_hT(bufs, x_ap, t0, nt, hT, hT_res, banks=None):
            dp = bufs["depth"]
            for i0 in range(0, nt, dp):
                tiles = list(range(i0, min(i0 + dp, nt)))
                hT_stageA(bufs, x_ap, t0, tiles)
                hT_stageB(bufs, tiles, hT, hT_res, banks)

        def norm_rope(tb, src_ps, nt, nh, w_bc, cs_t, cs_res, src_res, out_b, out_res):
            nu = nt * nh
            n = nu * HD
            s = tb["sl"] % tb["nbuf"]
            tb["sl"] += 1
            qf, sq, ssq, qn_, t1, t2, t3, t4 = (tb[k][s] for k in ("qf", "sq", "ssq", "qn", "t1", "t2", "t3", "t4"))
            R = lambda k: "%s%d" % (k, s)
            OP("act", src_res, [R("qf")], lambda e: e.activation(out=qf[:, 0:n].rearrange("p (t x) -> p t x", t=nt), in_=src_ps, func=AF.Copy))
            OP("act", [R("qf")], [R("sq")], lambda e: e.activation(out=sq[:, 0:n], in_=qf[:, 0:n], func=AF.Square))
            OP("dve", [R("sq")], [R("qssq")], lambda e: e.tensor_reduce(out=ssq[:, 0:nu], in_=sq[:, 0:n].rearrange("p (h d) -> p h d", h=nu), axis=AX.X, op=ALU.add))
            OP("act", [R("qssq")], [R("qssq")], lambda e: e.activation(out=ssq[:, 0:nu], in_=ssq[:, 0:nu], func=AF.Sqrt, scale=1.0 / HD, bias=EPS))
            OP("dve", [R("qssq")], [R("qssq")], lambda e: e.reciprocal(out=ssq[:, 0:nu], in_=ssq[:, 0:nu]))
            for h in range(nu):
                OP("dve", [R("qf"), R("qssq")], [R("qn")], lambda e, h=h: e.scalar_tensor_tensor(out=qn_[:, h * HD:(h + 1) * HD], in0=qf[:, h * HD:(h + 1) * HD], scalar=ssq[:, h:h + 1], in1=w_bc, op0=ALU.mult, op1=ALU.mult))
            q4 = qn_[:, 0:n].rearrange("p (t h d) -> p t h d", t=nt, h=nh)
            x0, x1 = q4[:, :, :, 0:64], q4[:, :, :, 64:128]
            cosb = cs_t[:, :, 0:64].unsqueeze(2).to_broadcast([128, nt, nh, 64])
            sinb = cs_t[:, :, 64:128].unsqueeze(2).to_broadcast([128, nt, nh, 64])
            o4 = out_b[:, 0:n].rearrange("p (t h d) -> p t h d", t=nt, h=nh)

            def v4(t):
                return t[:, 0:nu * 64].rearrange("p (t h d) -> p t h d", t=nt, h=nh)
            OP("dve", [R("qn"), cs_res], [R("t1")], lambda e: e.tensor_tensor(out=v4(t1), in0=x0, in1=cosb, op=ALU.mult))
            OP("dve", [R("qn"), cs_res], [R("t2")], lambda e: e.tensor_tensor(out=v4(t2), in0=x1, in1=sinb, op=ALU.mult))
            OP("dve", [R("t1"), R("t2")], [out_res], lambda e: e.tensor_tensor(out=o4[:, :, :, 0:64], in0=v4(t1), in1=v4(t2), op=ALU.subtract))
            OP("dve", [R("qn"), cs_res], [R("t3")], lambda e: e.tensor_tensor(out=v4(t3), in0=x0, in1=sinb, op=ALU.mult))
            OP("dve", [R("qn"), cs_res], [R("t4")], lambda e: e.tensor_tensor(out=v4(t4), in0=x1, in1=cosb, op=ALU.mult))
            OP("dve", [R("t3"), R("t4")], [out_res], lambda e: e.tensor_tensor(out=o4[:, :, :, 64:128], in0=v4(t3), in1=v4(t4), op=ALU.add))

        def alloc_x_bufs(depth=2):
            return dict(depth=depth, xt=[AR.alloc([D], F32) for _ in range(depth)], ssq=[AR.alloc([1], F32) for _ in range(depth)],
                        xn=[AR.alloc([D], BF16) for _ in range(depth)], junk=AR.alloc([D], BF16))

        def alloc_rope_bufs(nbuf=2):
            n = 8 * HD
            d = dict(sl=0)
            for k, sz in (("qf", n), ("sq", n), ("ssq", 8), ("qn", n), ("t1", n // 2), ("t2", n // 2), ("t3", n // 2), ("t4", n // 2)):
                d[k] = [AR.alloc([sz], F32) for _ in range(nbuf)]
                if nbuf == 1:
                    d[k] = d[k] * 2
            if nbuf == 1:
                d["sl"] = 0
            d["nbuf"] = nbuf
            return d

        def phase_A1(x_all, S, cs_all):
            AR.reset()
            Wkv = AR.alloc([DC, 512], BF16)
            Wxl = AR.alloc([DC, 1024], BF16)
            wload(Wkv, w_in_v[:, :, OK_:OK_ + 512])
            wload(Wxl, w_in_v[:, :, OXL:OXL + 1024])
            xb = alloc_x_bufs(4)
            rb = alloc_rope_bufs(2)
            hTs = [AR.alloc([DC, CH], BF16) for _ in range(2)]
            cst = [AR.alloc([4, HD], F32) for _ in range(2)]
            kb = [AR.alloc([4 * KVH * HD], BF16) for _ in range(2)]
            kTst = [AR.alloc([KVH, CH], BF16) for _ in range(2)]
            vst = [AR.alloc([KVH, 4, HD], BF16) for _ in range(2)]
            xlst = [AR.alloc([8, CH], F32) for _ in range(2)]
            nchunk = S // CH
            T4 = [0, 1, 2, 3]
            HB = (2, 3, 4, 5)
            hT_stageA(xb, x_all, 0, T4)
            hT_stageB(xb, T4, hTs[0], "hT0", banks=HB)
            for j in range(nchunk):
                s2 = j % 2
                t0 = j * CH
                hT = hTs[s2]
                hres = "hT%d" % s2
                DMA("sp", cst[s2], cs_all[t0:t0 + CH, :].rearrange("(i p) d -> p i d", p=128), [], ["cs%d" % s2])
                if j + 1 < nchunk:
                    hT_stageA(xb, x_all, t0 + CH, T4)
                for i in range(4):
                    b = 2 + i
                    for c in range(DC):
                        OP("pe", [hres, "W"], ["pb%d" % b], lambda e, c=c, b=b, i=i, hT=hT: e.matmul(bank(b), lhsT=hT[:, c, i * 128:(i + 1) * 128], rhs=Wkv[:, c, :], start=(c == 0), stop=(c == DC - 1)))
                kvr = ["pb%d" % (2 + i) for i in range(4)]
                kv4 = ps[:, 2 * 512:6 * 512].rearrange("p (i x) -> p i x", i=4)
                OP("act", kvr, ["vst%d" % s2], lambda e, s2=s2, kv4=kv4: e.activation(out=vst[s2].rearrange("p g i d -> p i g d"), in_=kv4[:, :, 256:512].rearrange("p i (g d) -> p i g d", g=KVH), func=AF.Copy))
                norm_rope(rb, kv4[:, :, 0:256], 4, KVH, kn_bc, cst[s2], "cs%d" % s2, kvr, kb[s2], "kb%d" % s2)
                for eb in range(8):
                    b = 7 if eb % 2 == 0 else 6
                    for c in range(DC):
                        OP("pe", [hres, "W"], ["pb%d" % b], lambda e, c=c, b=b, eb=eb, hT=hT: e.matmul(bank(b), lhsT=Wxl[:, c, eb * 128:(eb + 1) * 128], rhs=hT[:, c, :], start=(c == 0), stop=(c == DC - 1)))
                    if eb % 2 == 0:
                        OP("act", ["pb%d" % b], ["xlst%d" % s2], lambda e, b=b, eb=eb, s2=s2: e.activation(out=xlst[s2][:, eb, :], in_=bank(b), func=AF.Copy))
                    else:
                        OP("dve", ["pb%d" % b], ["xlst%d" % s2], lambda e, b=b, eb=eb, s2=s2: e.tensor_copy(out=xlst[s2][:, eb, :], in_=bank(b)))
                DMA("sp", xl_scr[:, :, t0:t0 + CH], xlst[s2], ["xlst%d" % s2], [])
                kbk = j % 2
                for u in range(4 * KVH):
                    OP("pe", ["kb%d" % s2], ["pb%d" % kbk], lambda e, u=u, s2=s2, kbk=kbk: e.transpose(out=bankb(kbk)[:, u * 128:(u + 1) * 128], in_=kb[s2][:, u * HD:(u + 1) * HD], identity=ident_b))
                OP("dve", ["pb%d" % kbk], ["kTst%d" % s2], lambda e, s2=s2, kbk=kbk: e.tensor_copy(out=kTst[s2].rearrange("p g (i t) -> p i g t", i=4), in_=bankb(kbk).rearrange("p (i g t) -> p i g t", i=4, g=KVH)))
                for g in range(KVH):
                    DMA("sp", kT_scr[g, :, t0:t0 + CH], kTst[s2][:, g, :], ["kTst%d" % s2], [])
                    DMA("sp", v_scr[g, :, j * 4:(j + 1) * 4, :], vst[s2][:, g, :, :], ["vst%d" % s2], [])
                if j + 1 < nchunk:
                    hT_stageB(xb, T4, hTs[1 - s2], "hT%d" % (1 - s2), banks=HB)
            PG.barrier()

        def phase_LRU(S, direction, own_acc):
            AR.reset()
            pipe = not own_acc
            nb = 2 if pipe else 1
            ng = 2 if pipe else 1
            BG = 4
            xlh = [AR.alloc([8, CH + 3], F32) for _ in range(nb)]
            hst = [AR.alloc([8, CH], F32) for _ in range(2)]
            hfl = [AR.alloc([8, CH], F32) for _ in range(nb)] if direction == 1 else None
            acc = AR.alloc([8, OWN], F32) if own_acc else None
            xc = [AR.alloc([CH], F32) for _ in range(BG * ng)]
            xcb = [AR.alloc([CH], BF16) for _ in range(BG * ng)]
            thr = [AR.alloc([CH], F32) for _ in range(BG)]
            thi = [AR.alloc([CH], F32) for _ in range(BG)]
            a_t = [AR.alloc([BG, CH], F32) for _ in range(ng)]
            s_t = [AR.alloc([BG, CH], F32) for _ in range(ng)]
            w_t = [AR.alloc([BG, CH], F32) for _ in range(ng)]
            if own_acc:
                OP("dve", [], ["acc"], lambda e: e.memset(acc, 0.0))
            nchunk = S // CH
            order = list(range(nchunk)) if direction == 0 else list(range(nchunk - 1, -1, -1))

            def load(k):
                j = order[k]
                sl = k % nb
                t0 = j * CH
                lo = max(t0 - 2, 0)
                hi = min(t0 + CH + 1, S)
                if t0 == 0:
                    OP("dve", [], ["xlh%d" % sl], lambda e, sl=sl: e.memset(xlh[sl][:, :, 0:2], 0.0))
                if t0 + CH == S:
                    OP("dve", [], ["xlh%d" % sl], lambda e, sl=sl: e.memset(xlh[sl][:, :, CH + 2:CH + 3], 0.0))
                DMA("sp", xlh[sl][:, :, lo - (t0 - 2):hi - (t0 - 2)], xl_scr[:, :, lo:hi], [], ["xlh%d" % sl])
                if direction == 1:
                    DMA("sp", hfl[sl], hf_scr[:, :, t0:t0 + CH], [], ["hfl%d" % sl])

            items = [(k, g0) for k in range(nchunk) for g0 in range(0, 8, BG)]

            def S1(n):
                k, g0 = items[n]
                sl = k % nb
                for bi in range(BG):
                    eb = g0 + bi
                    xi = (n % ng) * BG + bi
                    cw = chp_t[:, eb, :]
                    src = xlh[sl][:, eb, :]
                    OP("act", ["xlh%d" % sl, "chp"], ["xc%d" % xi], lambda e, xi=xi, src=src, cw=cw: e.activation(out=xc[xi], in_=src[:, 0:CH], func=AF.Identity, scale=cw[:, 0:1], bias=cw[:, 4:5]))
                    for tap in range(1, 4):
                        OP("dve", ["xlh%d" % sl, "xc%d" % xi], ["xc%d" % xi], lambda e, xi=xi, src=src, cw=cw, tap=tap: e.scalar_tensor_tensor(out=xc[xi], in0=src[:, tap:tap + CH], scalar=cw[:, tap:tap + 1], in1=xc[xi], op0=ALU.mult, op1=ALU.add))

            def S2(n):
                k, g0 = items[n]
                for bi in range(BG):
                    eb = g0 + bi
                    xi = (n % ng) * BG + bi
                    OP("act", ["xc%d" % xi], ["xcb%d" % xi], lambda e, xi=xi: e.activation(out=xcb[xi], in_=xc[xi], func=AF.Copy))
                    OP("pe", ["xcb%d" % xi, "wg"], ["pb%d" % (2 * bi)], lambda e, bi=bi, eb=eb, xi=xi: e.matmul(bank(2 * bi), lhsT=wg[:, 0 + direction, eb, :], rhs=xcb[xi], start=True, stop=True))
                    OP("pe", ["xcb%d" % xi, "wg"], ["pb%d" % (2 * bi + 1)], lambda e, bi=bi, eb=eb, xi=xi: e.matmul(bank(2 * bi + 1), lhsT=wg[:, 2 + direction, eb, :], rhs=xcb[xi], start=True, stop=True))

            def S3a(n):
                k, g0 = items[n]
                gs = n % ng
                for bi in range(BG):
                    eb = g0 + bi
                    ci = direction * 8 + eb
                    OP("act", ["pb%d" % (2 * bi)], ["thr%d" % bi], lambda e, bi=bi, ci=ci: e.activation(out=thr[bi], in_=bank(2 * bi), func=AF.Tanh, scale=0.5, bias=brh[:, ci:ci + 1]))
                    OP("act", ["pb%d" % (2 * bi + 1)], ["thi%d" % bi], lambda e, bi=bi, ci=ci: e.activation(out=thi[bi], in_=bank(2 * bi + 1), func=AF.Tanh, scale=0.5, bias=bih[:, ci:ci + 1]))
                    OP("act", ["thr%d" % bi], ["a%d" % gs], lambda e, bi=bi, gs=gs, ci=ci: e.activation(out=a_t[gs][:, bi, :], in_=thr[bi], func=AF.Exp, scale=cl_h[:, ci:ci + 1], bias=cl_h[:, ci:ci + 1]))
                    OP("act", ["thr%d" % bi], ["s%d" % gs], lambda e, bi=bi, gs=gs, ci=ci: e.activation(out=s_t[gs][:, bi, :], in_=thr[bi], func=AF.Exp, scale=cl_1[:, ci:ci + 1], bias=cl_1[:, ci:ci + 1]))
                OP("act", ["s%d" % gs], ["s%d" % gs], lambda e, gs=gs: e.activation(out=s_t[gs], in_=s_t[gs], func=AF.Sqrt, scale=-1.0, bias=1.0))

            def S3b(n):
                k, g0 = items[n]
                gs = n % ng
                hs = k % 2
                sl = k % nb
                j = order[k]
                t0 = j * CH
                for bi in range(BG):
                    xi = (n % ng) * BG + bi
                    OP("dve", ["thi%d" % bi, "xc%d" % xi], ["w%d" % gs], lambda e, bi=bi, gs=gs, xi=xi: e.scalar_tensor_tensor(out=w_t[gs][:, bi, :], in0=thi[bi], scalar=1.0, in1=xc[xi], op0=ALU.add, op1=ALU.mult))
                OP("dve", ["s%d" % gs, "w%d" % gs], ["w%d" % gs], lambda e, gs=gs: e.scalar_tensor_tensor(out=w_t[gs], in0=s_t[gs], scalar=0.5, in1=w_t[gs], op0=ALU.mult, op1=ALU.mult))
                for bi in range(BG):
                    eb = g0 + bi
                    if k == 0:
                        init = 0.0
                        rd = []
                    else:
                        pcol = CH - 1 if direction == 0 else 0
                        init = hst[1 - hs][:, eb, pcol:pcol + 1]
                        rd = ["hst%d" % (1 - hs)]
                    if direction == 0:
                        OP("dve", ["a%d" % gs, "w%d" % gs] + rd, ["hst%d" % hs], lambda e, gs=gs, bi=bi, eb=eb, hs=hs, init=init: e.tensor_tensor_scan(out=hst[hs][:, eb, :], data0=a_t[gs][:, bi, :], data1=w_t[gs][:, bi, :], initial=init, op0=ALU.mult, op1=ALU.add))
                    else:
                        OP("dve", ["a%d" % gs, "w%d" % gs] + rd, ["hst%d" % hs], lambda e, gs=gs, bi=bi, eb=eb, hs=hs, init=init: e.tensor_tensor_scan(out=hst[hs][:, eb, ::-1], data0=a_t[gs][:, bi, ::-1], data1=w_t[gs][:, bi, ::-1], initial=init, op0=ALU.mult, op1=ALU.add))
                if g0 + BG == 8:
                    if direction == 0:
                        DMA("sp", hf_scr[:, :, t0:t0 + CH], hst[hs], ["hst%d" % hs], [])
                    else:
                        OP("dve", ["hst%d" % hs, "hfl%d" % sl], ["hfl%d" % sl], lambda e, hs=hs, sl=sl: e.tensor_tensor(out=hfl[sl], in0=hfl[sl], in1=hst[hs], op=ALU.add))
                        if own_acc:
                            slot = j % NSLOT
                            grp = j // NSLOT
                            OP("dve", ["hfl%d" % sl, "acc", "msk"], ["acc"], lambda e, sl=sl, slot=slot, grp=grp: e.scalar_tensor_tensor(out=acc[:, :, slot * CH:(slot + 1) * CH], in0=hfl[sl], scalar=msk_t[:, grp:grp + 1], in1=acc[:, :, slot * CH:(slot + 1) * CH], op0=ALU.mult, op1=ALU.add))
                        else:
                            DMA("sp", ls_scr[:, :, t0:t0 + CH], hfl[sl], ["hfl%d" % sl], [])

            load(0)
            if pipe:
                S1(0)
                S2(0)
                for n in range(len(items)):
                    k, g0 = items[n]
                    if g0 == 0 and k + 1 < nchunk:
                        load(k + 1)
                    if n + 1 < len(items):
                        S1(n + 1)
                    S3a(n)
                    if n + 1 < len(items):
                        S2(n + 1)
                    S3b(n)
            else:
                for n in range(len(items)):
                    k, g0 = items[n]
                    S1(n)
                    S2(n)
                    S3a(n)
                    S3b(n)
                    if g0 + BG == 8 and k + 1 < nchunk:
                        load(k + 1)
            if own_acc:
                DMA("sp", ls_scr[:, :, 0:OWN], acc, ["acc"], [])
            PG.barrier()

        def phase_B1(x_own, N_OWN, cs_own, S):
            AR.reset()
            QB = 1024 if N_OWN % 1024 == 0 else 512
            QH = QB // 512
            NKT = S // 128
            big = S > 8192
            Wq = AR.alloc([DC, 1024], BF16)
            Wga = AR.alloc([DC, 1024], BF16)
            wload(Wq, w_in_v[:, :, OQ:OQ + 1024])
            wload(Wga, w_in_v[:, :, OGA:OGA + 1024])
            KT = AR.alloc([S], BF16)
            VV = AR.alloc([NKT, HD], BF16)
            xb = alloc_x_bufs()
            rb = alloc_rope_bufs(1 if big else 2)
            hT = AR.alloc([DC, QB], BF16)
            qT = AR.alloc([H, QB], BF16)
            cst = [AR.alloc([1, HD], F32) for _ in range(2)]
            qb_ = [AR.alloc([H * HD], BF16) for _ in range(2)]
            pT = [AR.alloc([QB], BF16) for _ in range(3)]
            rs = AR.alloc([QB], F32)
            nsg = 1 if big else 2
            sg = [AR.alloc([QB], F32) for _ in range(nsg)]
            aob = [AR.alloc([QB], BF16)]
            tctr = 0
            cnt = [0]
            hctr = 0
            for qb in range(N_OWN // QB):
                q0 = qb * QB
                make_hT(xb, x_own, q0, QB // 128, hT, "hT")
                for i in range(QB // 128):
                    ts = tctr % 2
                    tctr += 1
                    b0 = 2 * (next_bank() % 4)
                    for hf in range(2):
                        for c in range(DC):
                            OP("pe", ["hT", "W"], ["pb%d" % (b0 + hf)], lambda e, c=c, b0=b0, hf=hf, i=i: e.matmul(bank(b0 + hf), lhsT=hT[:, c, i * 128:(i + 1) * 128], rhs=Wq[:, c, hf * 512:(hf + 1) * 512], start=(c == 0), stop=(c == DC - 1)))
                    DMA("sp", cst[ts], cs_own[q0 + i * 128:q0 + (i + 1) * 128, :].rearrange("(i p) d -> p i d", p=128), [], ["cs%d" % ts])
                    norm_rope(rb, ps[:, b0 * 512:b0 * 512 + 1024].rearrange("p (t x) -> p t x", t=1), 1, H, qn_bc, cst[ts], "cs%d" % ts, ["pb%d" % b0, "pb%d" % (b0 + 1)], qb_[ts], "qb%d" % ts)
                    b2 = next_bank()
                    for h in range(H):
                        OP("pe", ["qb%d" % ts], ["pb%d" % b2], lambda e, h=h, b2=b2, ts=ts: e.transpose(out=bankb(b2)[:, h * 128:(h + 1) * 128], in_=qb_[ts][:, h * HD:(h + 1) * HD], identity=ident_b))
                    OP("dve", ["pb%d" % b2], ["qT"], lambda e, b2=b2, i=i: e.tensor_copy(out=qT[:, :, i * 128:(i + 1) * 128], in_=bankb(b2).rearrange("p (h t) -> p h t", h=H)))
                for g in range(KVH):
                    DMA("sp", KT, kT_scr[g, :, 0:S], [], ["KT"])
                    DMA("sp", VV, v_scr[g, :, 0:NKT, :], [], ["VV"])
                    for hh in range(H // KVH):
                        h = g * (H // KVH) + hh
                        hsg = hctr % nsg
                        hctr += 1
                        for hf in range(QH):
                            for c in range(DC):
                                OP("pe", ["hT", "W"], ["pb%d" % hf], lambda e, c=c, hf=hf, h=h: e.matmul(bank(hf), lhsT=Wga[:, c, h * 128:(h + 1) * 128], rhs=hT[:, c, hf * 512:(hf + 1) * 512], start=(c == 0), stop=(c == DC - 1)))
                        OP("act", ["pb%d" % hf for hf in range(QH)], ["sg%d" % hsg], lambda e, hsg=hsg: e.activation(out=sg[hsg], in_=ps[:, 0:QB], func=AF.Silu))
                        slots = []

                        def emit_qk(kt, h=h):
                            sp_ = (cnt[0] % 2) * 2
                            pt = cnt[0] % 3
                            cnt[0] += 1
                            slots.append((sp_, pt))
                            for hf in range(QH):
                                OP("pe", ["KT", "qT"], ["pb%d" % (sp_ + hf)], lambda e, kt=kt, sp_=sp_, hf=hf, h=h: e.matmul(bank(sp_ + hf), lhsT=KT[:, kt * 128:(kt + 1) * 128], rhs=qT[:, h, hf * 512:(hf + 1) * 512], start=True, stop=True))

                        emit_qk(0)
                        for kt in range(NKT):
                            if kt + 1 < NKT:
                                emit_qk(kt + 1)
                            sp_, pt = slots[kt]
                            OP("act", ["pb%d" % (sp_ + hf) for hf in range(QH)] + ["negB"], ["pT%d" % pt], lambda e, sp_=sp_, pt=pt: e.activation(out=pT[pt], in_=ps[:, sp_ * 512:sp_ * 512 + QB], func=AF.Exp, scale=SM_SCALE, bias=negB[:, 0:1]))
                            for hf in range(QH):
                                OP("pe", ["pT%d" % pt, "VV"], ["pb%d" % (4 + hf)], lambda e, kt=kt, pt=pt, hf=hf: e.matmul(bank(4 + hf), lhsT=VV[:, kt, :], rhs=pT[pt][:, hf * 512:(hf + 1) * 512], start=(kt == 0), stop=(kt == NKT - 1)))
                            if kt == 0:
                                OP("dve", ["pT%d" % pt], ["rs"], lambda e, pt=pt: e.tensor_copy(out=rs, in_=pT[pt]))
                            else:
                                OP("dve", ["pT%d" % pt, "rs"], ["rs"], lambda e, pt=pt: e.tensor_tensor(out=rs, in0=rs, in1=pT[pt], op=ALU.add))
                        accr = ["pb%d" % (4 + hf) for hf in range(QH)]
                        sumr = ["pb%d" % (6 + hf) for hf in range(QH)]
                        for hf in range(QH):
                            OP("pe", ["rs", "ones_f"], ["pb%d" % (6 + hf)], lambda e, hf=hf: e.matmul(bank(6 + hf), lhsT=ones_f, rhs=rs[:, hf * 512:(hf + 1) * 512], start=True, stop=True))
                        OP("dve", sumr, ["rs"], lambda e: e.reciprocal(out=rs, in_=ps[:, 6 * 512:6 * 512 + QB]))
                        OP("dve", accr + ["rs"], ["rs"], lambda e: e.tensor_tensor(out=rs, in0=ps[:, 4 * 512:4 * 512 + QB], in1=rs, op=ALU.mult))
                        OP("dve", ["rs", "sg%d" % hsg], ["aob0"], lambda e, hsg=hsg: e.tensor_tensor(out=aob[0], in0=rs, in1=sg[hsg], op=ALU.mult))
                        DMA("sp", ao_scr[:, h, q0:q0 + QB], aob[0], ["aob0"], [])
            PG.barrier()

        def phase_B2(x_own, N_OWN, y_out):
            AR.reset()
            Wgl = AR.alloc([DC, 1024], BF16)
            Wml = AR.alloc([DC, 2048], BF16)
            WbA = AR.alloc([DC, 1024], BF16)
            WbL = AR.alloc([DC, 1024], BF16)
            Wo = AR.alloc([DC, 1024], BF16)
            wload(Wgl, w_in_v[:, :, OGL:OGL + 1024])
            wload(Wml[:, :, 0:1024], w_in_v[:, :, OML:OML + 1024])
            wload(Wml[:, :, 1024:2048], w_in_v[:, :, OML + 1024:OML + 2048])
            wload(WbA, w_br[0].rearrange("(c p) n -> p c n", p=128))
            wload(WbL, w_br[1].rearrange("(c p) n -> p c n", p=128))
            wload(Wo, w_out.rearrange("(c p) n -> p c n", p=128))
            xb = alloc_x_bufs()
            hT = AR.alloc([DC, CH], BF16)
            hres = "hT"
            lsl = [AR.alloc([CH], F32) for _ in range(2)]
            aol = AR.alloc([8, CH], BF16)
            loT = AR.alloc([8, CH], BF16)
            mT = AR.alloc([8, CH], BF16)
            sgl = [AR.alloc([CH], F32) for _ in range(2)]
            ga = [AR.alloc([CH], F32) for _ in range(2)]
            gl = [AR.alloc([CH], F32) for _ in range(2)]
            t1 = [AR.alloc([CH], F32) for _ in range(2)]
            t2 = [AR.alloc([CH], F32) for _ in range(2)]
            yt = [AR.alloc([D], F32)]
            yo = [AR.alloc([D], F32)]
            yss = [AR.alloc([1], F32)]
            xres = [AR.alloc([D], F32)]
            junk2 = xb["junk"]
            nchunk = N_OWN // CH
            ectr = 0
            yctr = 0
            for j in range(nchunk):
                t0 = j * CH
                DMA("sp", aol, ao_scr[:, :, t0:t0 + CH], [], ["aol"])
                make_hT(xb, x_own, t0, 4, hT, hres)
                for eb in range(8):
                    es = ectr % 2
                    ectr += 1
                    DMA("sp", lsl[es], ls_scr[:, eb, t0:t0 + CH], [], ["lsl%d" % es])
                    b = next_bank()
                    for c in range(DC):
                        OP("pe", [hres, "W"], ["pb%d" % b], lambda e, c=c, b=b, eb=eb: e.matmul(bank(b), lhsT=Wgl[:, c, eb * 128:(eb + 1) * 128], rhs=hT[:, c, :], start=(c == 0), stop=(c == DC - 1)))
                    OP("act", ["pb%d" % b], ["sgl%d" % es], lambda e, b=b, es=es: e.activation(out=sgl[es], in_=bank(b), func=AF.Silu))
                    OP("dve", ["sgl%d" % es, "lsl%d" % es], ["loT"], lambda e, es=es, eb=eb: e.tensor_tensor(out=loT[:, eb, :], in0=lsl[es], in1=sgl[es], op=ALU.mult))
                for eb in range(8):
                    es = ectr % 2
                    ectr += 1
                    bA, bL, bga, bgl = next_bank(), next_bank(), next_bank(), next_bank()
                    for c in range(DC):
                        OP("pe", ["aol", "W"], ["pb%d" % bA], lambda e, c=c, bA=bA, eb=eb: e.matmul(bank(bA), lhsT=WbA[:, c, eb * 128:(eb + 1) * 128], rhs=aol[:, c, :], start=(c == 0), stop=(c == DC - 1)))
                    for c in range(DC):
                        OP("pe", ["loT", "W"], ["pb%d" % bL], lambda e, c=c, bL=bL, eb=eb: e.matmul(bank(bL), lhsT=WbL[:, c, eb * 128:(eb + 1) * 128], rhs=loT[:, c, :], start=(c == 0), stop=(c == DC - 1)))
                    for c in range(DC):
                        OP("pe", [hres, "W"], ["pb%d" % bga], lambda e, c=c, bga=bga, eb=eb: e.matmul(bank(bga), lhsT=Wml[:, c, eb * 128:(eb + 1) * 128], rhs=hT[:, c, :], start=(c == 0), stop=(c == DC - 1)))
                    for c in range(DC):
                        OP("pe", [hres, "W"], ["pb%d" % bgl], lambda e, c=c, bgl=bgl, eb=eb: e.matmul(bank(bgl), lhsT=Wml[:, c, 1024 + eb * 128:1024 + (eb + 1) * 128], rhs=hT[:, c, :], start=(c == 0), stop=(c == DC - 1)))
                    OP("act", ["pb%d" % bga, "bm"], ["ga%d" % es], lambda e, bga=bga, es=es, eb=eb: e.activation(out=ga[es], in_=bank(bga), func=AF.Sigmoid, bias=bm_t[:, eb:eb + 1]))
                    OP("act", ["pb%d" % bgl, "bm"], ["gl%d" % es], lambda e, bgl=bgl, es=es, eb=eb: e.activation(out=gl[es], in_=bank(bgl), func=AF.Sigmoid, bias=bm_t[:, 8 + eb:9 + eb]))
                    OP("dve", ["pb%d" % bA, "ga%d" % es], ["t1%d" % es], lambda e, bA=bA, es=es: e.tensor_tensor(out=t1[es], in0=bank(bA), in1=ga[es], op=ALU.mult))
                    OP("dve", ["pb%d" % bL, "gl%d" % es], ["t2%d" % es], lambda e, bL=bL, es=es: e.tensor_tensor(out=t2[es], in0=bank(bL), in1=gl[es], op=ALU.mult))
                    OP("pool", ["t1%d" % es, "t2%d" % es], ["mT"], lambda e, es=es, eb=eb: e.tensor_tensor(out=mT[:, eb, :], in0=t1[es], in1=t2[es], op=ALU.add))
                for i in range(4):
                    ys_ = 0
                    b0 = 2 * (next_bank() % 4)
                    DMA("sp", xres[ys_], x_own[t0 + i * 128:t0 + (i + 1) * 128, :], [], ["xres%d" % ys_])
                    for hf in range(2):
                        for c in range(DC):
                            OP("pe", ["mT", "W"], ["pb%d" % (b0 + hf)], lambda e, c=c, b0=b0, hf=hf, i=i: e.matmul(bank(b0 + hf), lhsT=mT[:, c, i * 128:(i + 1) * 128], rhs=Wo[:, c, hf * 512:(hf + 1) * 512], start=(c == 0), stop=(c == DC - 1)))
                    OP("dve", ["pb%d" % b0, "pb%d" % (b0 + 1), "xres%d" % ys_], ["yt%d" % ys_], lambda e, b0=b0, ys_=ys_: e.tensor_tensor(out=yt[ys_], in0=ps[:, b0 * 512:b0 * 512 + 1024], in1=xres[ys_], op=ALU.add))
                    OP("act", ["yt%d" % ys_], ["junk2", "yss%d" % ys_], lambda e, ys_=ys_: e.activation(out=junk2, in_=yt[ys_], func=AF.Square, accum_out=yss[ys_]))
                    OP("act", ["yss%d" % ys_], ["yss%d" % ys_], lambda e, ys_=ys_: e.activation(out=yss[ys_], in_=yss[ys_], func=AF.Sqrt, scale=1.0 / D, bias=EPS))
                    OP("dve", ["yss%d" % ys_], ["yss%d" % ys_], lambda e, ys_=ys_: e.reciprocal(out=yss[ys_], in_=yss[ys_]))
                    OP("dve", ["yt%d" % ys_, "yss%d" % ys_], ["yo%d" % ys_], lambda e, ys_=ys_: e.scalar_tensor_tensor(out=yo[ys_], in0=yt[ys_], scalar=yss[ys_][:, 0:1], in1=gfin_bc, op0=ALU.mult, op1=ALU.mult))
                    DMA("sp", y_out[t0 + i * 128:t0 + (i + 1) * 128, :], yo[ys_], ["yo%d" % ys_], [])
            PG.barrier()

        for (x_all, S, cs_all, x_own, n_own, cs_own, y_out, is_p) in (
            (xs, S_S, cs_s, xs, S_S, cs_s, ys, False),
            (xp, S_P, cs_p, xq, OWN, cs_q, yq, True),
        ):
            phase_A1(x_all, S, cs_all)
            phase_LRU(S, 0, False)
            phase_LRU(S, 1, is_p)
            phase_B1(x_own, n_own, cs_own, S)
            phase_B2(x_own, n_own, y_out)
        PG.emit(nc)
    return nc


def _rope_table(S):
    grid_w = 64
    pos = np.arange(S)
    rows = (pos // grid_w).astype(np.float32)
    cols = (pos % grid_w).astype(np.float32)
    n = HD // 4
    inv = (np.float32(10000.0) ** (-(np.arange(n, dtype=np.float32) / np.float32(n)))).astype(np.float32)
    ang = np.concatenate([rows[:, None] * inv[None, :], cols[:, None] * inv[None, :]], axis=-1).astype(np.float32)
    return np.concatenate([np.cos(ang), np.sin(ang)], axis=-1).astype(np.float32)


_NC_CACHE = {}


def kernel(x_prompt, x_sample, norm_in, w_in, b_merge, q_norm, k_norm, conv_w, conv_b,
           w_rgate, b_rgate, w_igate, b_igate, lam, w_branch, w_out, norm_final):
    f = lambda a: np.ascontiguousarray(np.asarray(a, dtype=np.float32))
    x_prompt, x_sample = f(x_prompt), f(x_sample)
    S_P = x_prompt.shape[1]
    S_S = x_sample.shape[1]
    OWN = S_P // NCORES
    assert x_prompt.shape[0] == 1 and x_sample.shape[0] == NCORES
    perm = np.concatenate([np.arange(0, HD, 2), np.arange(1, HD, 2)])
    cols = np.arange(NPROJ)
    for hh in range(H):
        cols[OQ + hh * HD:OQ + (hh + 1) * HD] = OQ + hh * HD + perm
    for hh in range(KVH):
        cols[OK_ + hh * HD:OK_ + (hh + 1) * HD] = OK_ + hh * HD + perm
    w_in_p = f(f(w_in)[0][:, cols])
    qn = f(f(q_norm)[0][perm][None, :])
    kn = f(f(k_norm)[0][perm][None, :])
    plist = [f(conv_w)[0][0], f(conv_w)[0][1], f(conv_w)[0][2], f(conv_w)[0][3], f(conv_b)[0],
             f(b_rgate)[0][0], f(b_rgate)[0][1], f(b_igate)[0][0], f(b_igate)[0][1], f(lam)[0][0], f(lam)[0][1]]
    chp = f(np.stack(plist, axis=-1).reshape(8, 128, 11).transpose(1, 0, 2))
    w_gate = f(np.stack([f(w_rgate)[0][0], f(w_rgate)[0][1], f(w_igate)[0][0], f(w_igate)[0][1]], axis=0))
    bm = f(f(b_merge)[0].reshape(16, 128).T)
    cs_p = _rope_table(S_P)
    cs_s = _rope_table(S_S)
    ident = np.eye(128, dtype=np.float32)
    key = (S_S, S_P)
    if key not in _NC_CACHE:
        _NC_CACHE[key] = build_nc(S_S, S_P)
    nc = _NC_CACHE[key]
    xp2 = x_prompt[0]
    in_maps = []
    for c in range(NCORES):
        m = np.zeros((128, NCORES), np.float32)
        m[:, c] = 1.0
        in_maps.append(dict(
            xs=x_sample[c], xp=xp2, xq=f(xp2[c * OWN:(c + 1) * OWN]),
            cs_s=cs_s, cs_p=cs_p, cs_q=f(cs_p[c * OWN:(c + 1) * OWN]),
            w_in=w_in_p, w_br=f(w_branch)[0], w_out=f(w_out)[0], w_gate=w_gate, chp=chp,
            g_in=f(norm_in), g_fin=f(norm_final)[None, :], qn=qn, kn=kn, bm=bm, msk=m, ident=ident))
    res = run_bass_kernel_spmd(nc, in_maps, core_ids=list(range(NCORES)))
    y_s = np.stack([np.asarray(res.results[c]["ys"], dtype=np.float32) for c in range(NCORES)], axis=0)
    y_p = np.concatenate([np.asarray(res.results[c]["yq"], dtype=np.float32) for c in range(NCORES)], axis=0)[None]
    return (y_p, y_s)
```

```python
import contextlib
import math
import numpy as np
import concourse.bass as bass
import concourse.mybir as mybir
from concourse.bass_utils import run_bass_kernel_spmd

F32 = mybir.dt.float32
BF16 = mybir.dt.bfloat16
AF = mybir.ActivationFunctionType
ALU = mybir.AluOpType
AX = mybir.AxisListType

NCORES = 8
D = 1024
DC = 8
H = 8
KVH = 2
HD = 128
CH = 512
NPROJ = 6656
EPS = 1e-6
SM_SCALE = 1.0 / math.sqrt(HD)
OQ, OK_, OV, OGA, OXL, OGL, OML = 0, 1024, 1280, 1536, 2560, 3584, 4608


class Prog:
    ENG = ("sp", "act", "dve", "pool", "pe")

    def __init__(self):
        self.ops = []
        self.last_writer = {}
        self.readers = {}
        self.last_on = {e: None for e in self.ENG}
        self.dma_since = set()
        self.pending = {e: set() for e in self.ENG}

    def op(self, eng, fn, reads=(), writes=(), dma=False):
        idx = len(self.ops)
        deps = {}
        for r in reads:
            w = self.last_writer.get(r)
            if w is not None:
                deps[w] = "raw"
        for r in writes:
            w = self.last_writer.get(r)
            if w is not None and w not in deps:
                deps[w] = "waw"
            for rd in self.readers.get(r, ()):
                if rd not in deps:
                    deps[rd] = "war"
        if self.pending[eng]:
            for p in self.pending[eng]:
                deps[p] = "raw"
            self.pending[eng] = set()
        for r in reads:
            self.readers.setdefault(r, []).append(idx)
        for r in writes:
            self.last_writer[r] = idx
            self.readers[r] = []
        self.ops.append(dict(eng=eng, fn=fn, dma=dma, deps=deps, idx=idx))
        self.last_on[eng] = idx
        if dma:
            self.dma_since.add(idx)
        return idx

    def barrier(self):
        s = set(self.dma_since)
        for e in self.ENG:
            if self.last_on[e] is not None:
                s.add(self.last_on[e])
        self.dma_since = set()
        self.pending = {e: set(s) for e in self.ENG}
        self.last_writer.clear()
        self.readers.clear()

    def emit(self, nc, ndma_sems=20):
        ops = self.ops
        for o in ops:
            keep = []
            for p, kind in o["deps"].items():
                po = ops[p]
                if (not po["dma"]) and (not o["dma"]) and po["eng"] == o["eng"]:
                    if not (kind == "raw" and o["eng"] in ("act", "dve", "pool")):
                        continue
                keep.append(p)
            o["deps"] = sorted(keep)
            o["signal"] = False
        for o in ops:
            for p in o["deps"]:
                ops[p]["signal"] = True
        cnt = {e: 0 for e in self.ENG}
        dma_i = {e: 0 for e in self.ENG}
        dma_cum = {}
        for o in ops:
            if o["dma"]:
                slot = (o["eng"], dma_i[o["eng"]] % ndma_sems)
                dma_i[o["eng"]] += 1
                prev = dma_cum.get(slot, 0)
                o["sem"], o["prev"], o["val"] = slot, prev, prev + 16
                dma_cum[slot] = prev + 16
            elif o["signal"]:
                cnt[o["eng"]] += 1
                o["sem"], o["val"] = o["eng"], cnt[o["eng"]]
        dma_queues = sorted({o["eng"] for o in ops if o["dma"]})
        with contextlib.ExitStack() as st:
            sems = {}
            for e in self.ENG:
                sems[e] = st.enter_context(nc.semaphore("s_" + e))
            for q in dma_queues:
                for i in range(ndma_sems):
                    sems[(q, i)] = st.enter_context(nc.semaphore("d_%s_%d" % (q, i)))
            block = st.enter_context(nc.Block())
            per_eng = {e: [o for o in ops if o["eng"] == e] for e in self.ENG}

            def run_engine(ename, eng):
                waited = {}
                for o in per_eng[ename]:
                    need = {}
                    for p in o["deps"]:
                        po = ops[p]
                        s = po["sem"]
                        if po["val"] > need.get(s, 0):
                            need[s] = po["val"]
                    if o["dma"] and o["prev"] > 0:
                        s = o["sem"]
                        if o["prev"] > need.get(s, 0):
                            need[s] = o["prev"]
                    for s, v in need.items():
                        if waited.get(s, 0) >= v:
                            continue
                        eng.wait_ge(sems[s], v)
                        waited[s] = v
                    ins = o["fn"](eng)
                    if o["dma"]:
                        ins.then_inc(sems[o["sem"]], 16)
                    elif o["signal"]:
                        ins.then_inc(sems[o["sem"]], 1)
                if ename == "sp":
                    for slot, v in dma_cum.items():
                        if waited.get(slot, 0) < v:
                            eng.wait_ge(sems[slot], v)

            @block.sync
            def _(e):
                run_engine("sp", e)

            @block.scalar
            def _(e):
                run_engine("act", e)

            @block.vector
            def _(e):
                run_engine("dve", e)

            @block.gpsimd
            def _(e):
                run_engine("pool", e)

            @block.tensor
            def _(e):
                run_engine("pe", e)


class Arena:
    def __init__(self, ap, nwords):
        self.ap = ap
        self.n = nwords
        self.off = 0
        self.mark = 0

    def alloc(self, shape, dt):
        n = 1
        for s in shape:
            n *= s
        words = n if dt == F32 else (n + 1) // 2
        words = (words + 7) // 8 * 8
        assert self.off + words <= self.n, ("arena overflow", self.off, words, self.n)
        v = self.ap[:, self.off:self.off + words]
        self.off += words
        if dt != F32:
            v = v.bitcast(dt)
        v = v[:, 0:n]
        if len(shape) == 2:
            v = v.rearrange("p (a b) -> p a b", a=shape[0])
        elif len(shape) == 3:
            v = v.rearrange("p (a b c) -> p a b c", a=shape[0], b=shape[1])
        return v

    def persist(self):
        self.mark = self.off

    def reset(self):
        self.off = self.mark


def build_nc(S_S, S_P):
    OWN = S_P // NCORES
    assert OWN % CH == 0 and S_S % CH == 0
    SMAX = max(S_S, S_P)
    NSLOT = OWN // CH
    nc = bass.Bass("TRN2", target_bir_lowering=False)

    def din(name, shape, dt=F32):
        return nc.dram_tensor(name, list(shape), dt, kind="ExternalInput").ap()

    xs = din("xs", [S_S, D])
    xp = din("xp", [S_P, D])
    xq = din("xq", [OWN, D])
    cs_s = din("cs_s", [S_S, HD])
    cs_p = din("cs_p", [S_P, HD])
    cs_q = din("cs_q", [OWN, HD])
    w_in = din("w_in", [D, NPROJ])
    w_br = din("w_br", [2, D, D])
    w_out = din("w_out", [D, D])
    w_gate = din("w_gate", [4, 8, 128, 128])
    chp = din("chp", [128, 8, 11])
    g_in = din("g_in", [1, D])
    g_fin = din("g_fin", [1, D])
    qn = din("qn", [1, HD])
    kn = din("kn", [1, HD])
    bm = din("bm", [128, 16])
    msk = din("msk", [128, NCORES])
    ident = din("ident", [128, 128])
    ys = nc.dram_tensor("ys", [S_S, D], F32, kind="ExternalOutput").ap()
    yq = nc.dram_tensor("yq", [OWN, D], F32, kind="ExternalOutput").ap()
    kT_scr = nc.dram_tensor("kT_scr", [KVH, 128, SMAX], BF16).ap()
    v_scr = nc.dram_tensor("v_scr", [KVH, 128, SMAX // 128, HD], BF16).ap()
    xl_scr = nc.dram_tensor("xl_scr", [128, 8, SMAX], F32).ap()
    hf_scr = nc.dram_tensor("hf_scr", [128, 8, SMAX], F32).ap()
    ls_scr = nc.dram_tensor("ls_scr", [128, 8, max(S_S, OWN)], F32).ap()
    ao_scr = nc.dram_tensor("ao_scr", [128, 8, max(S_S, OWN)], BF16).ap()

    PG = Prog()
    w_in_v = w_in.rearrange("(c p) n -> p c n", p=128)

    def OP(eng, reads, writes, f):
        PG.op(eng, f, reads, writes)

    def DMA(eng, out, in_, reads, writes):
        PG.op(eng, lambda e: e.dma_start(out=out, in_=in_), reads, writes, dma=True)

    with contextlib.ExitStack() as st:
        NW = 53000
        arena_t = st.enter_context(nc.sbuf_tensor("arena", [128, NW], F32))
        ps_t = st.enter_context(nc.psum_tensor("ps", [128, 4096], F32))
        AR = Arena(arena_t, NW)
        ps = ps_t
        psb = ps_t[:].bitcast(BF16)

        def bank(b, n=512):
            return ps[:, b * 512:b * 512 + n]

        def bankb(b, n=1024):
            return psb[:, b * 1024:b * 1024 + n]

        bank_ctr = [0]

        def next_bank():
            b = bank_ctr[0] % 8
            bank_ctr[0] += 1
            return b

        ident_b = AR.alloc([128], BF16)
        ones_b = AR.alloc([128], BF16)
        ones_f = AR.alloc([128], F32)
        gin_bc = AR.alloc([D], F32)
        gfin_bc = AR.alloc([D], F32)
        qn_bc = AR.alloc([HD], F32)
        kn_bc = AR.alloc([HD], F32)
        chp_t = AR.alloc([8, 11], F32)
        bm_t = AR.alloc([16], F32)
        bmh_t = AR.alloc([16], F32)
        msk_t = AR.alloc([NCORES], F32)
        lam_c = AR.alloc([16], F32)
        cl_h = AR.alloc([16], F32)
        cl_1 = AR.alloc([16], F32)
        brh = AR.alloc([16], F32)
        bih = AR.alloc([16], F32)
        tmpa = AR.alloc([16], F32)
        tmpb = AR.alloc([16], F32)
        tmpc = AR.alloc([16], F32)
        negB = AR.alloc([1], F32)
        mq = AR.alloc([1], F32)
        mk = AR.alloc([1], F32)
        absq = AR.alloc([HD], F32)
        wg = AR.alloc([4, 8, 128], BF16)
        AR.persist()

        DMA("pool", ident_b, ident[:, :], [], ["ident_b"])
        OP("dve", [], ["ones_b"], lambda e: e.memset(ones_b, 1.0))
        OP("dve", [], ["ones_f"], lambda e: e.memset(ones_f, 1.0))
        DMA("sp", gin_bc, g_in[0:1, :].partition_broadcast(128), [], ["gin_bc"])
        DMA("sp", gfin_bc, g_fin[0:1, :].partition_broadcast(128), [], ["gfin_bc"])
        DMA("sp", qn_bc, qn[0:1, :].partition_broadcast(128), [], ["qn_bc"])
        DMA("sp", kn_bc, kn[0:1, :].partition_broadcast(128), [], ["kn_bc"])
        DMA("sp", chp_t, chp[:, :, :], [], ["chp"])
        DMA("sp", bm_t, bm[:, :], [], ["bm"])
        DMA("sp", msk_t, msk[:, :], [], ["msk"])
        for gi in range(4):
            DMA("pool", wg[:, gi, :, :], w_gate[gi].rearrange("b i o -> i b o"), [], ["wg"])
        for d_ in range(2):
            OP("dve", ["chp"], ["lam_c"], lambda e, d_=d_: e.tensor_copy(out=lam_c[:, d_ * 8:(d_ + 1) * 8], in_=chp_t[:, :, 9 + d_]))
            OP("dve", ["chp"], ["brh"], lambda e, d_=d_: e.tensor_scalar(out=brh[:, d_ * 8:(d_ + 1) * 8], in0=chp_t[:, :, 5 + d_], scalar1=0.5, scalar2=None, op0=ALU.mult))
            OP("dve", ["chp"], ["bih"], lambda e, d_=d_: e.tensor_scalar(out=bih[:, d_ * 8:(d_ + 1) * 8], in0=chp_t[:, :, 7 + d_], scalar1=0.5, scalar2=None, op0=ALU.mult))
        OP("dve", ["bm"], ["bmh"], lambda e: e.tensor_scalar(out=bmh_t, in0=bm_t, scalar1=0.5, scalar2=None, op0=ALU.mult))
        OP("act", ["lam_c"], ["tmpa"], lambda e: e.activation(out=tmpa, in_=lam_c, func=AF.Exp, scale=-1.0))
        OP("dve", ["tmpa"], ["tmpb"], lambda e: e.tensor_scalar(out=tmpb, in0=tmpa, scalar1=1.0, scalar2=None, op0=ALU.add))
        OP("act", ["tmpb"], ["tmpc"], lambda e: e.activation(out=tmpc, in_=tmpb, func=AF.Ln))
        OP("dve", ["tmpb"], ["tmpb"], lambda e: e.tensor_scalar(out=tmpb, in0=tmpb, scalar1=-1.0, scalar2=1e-30, op0=ALU.add, op1=ALU.max))
        OP("dve", ["tmpb"], ["tmpb"], lambda e: e.reciprocal(out=tmpb, in_=tmpb))
        OP("dve", ["tmpa", "tmpc"], ["tmpc"], lambda e: e.tensor_tensor(out=tmpc, in0=tmpc, in1=tmpa, op=ALU.mult))
        OP("dve", ["tmpb", "tmpc"], ["tmpc"], lambda e: e.tensor_tensor(out=tmpc, in0=tmpc, in1=tmpb, op=ALU.mult))
        OP("dve", ["tmpc"], ["cl_h"], lambda e: e.tensor_scalar(out=cl_h, in0=tmpc, scalar1=-4.0, scalar2=None, op0=ALU.mult))
        OP("dve", ["tmpc"], ["cl_1"], lambda e: e.tensor_scalar(out=cl_1, in0=tmpc, scalar1=-8.0, scalar2=None, op0=ALU.mult))
        OP("act", ["qn_bc"], ["absq"], lambda e: e.activation(out=absq, in_=qn_bc, func=AF.Abs))
        OP("dve", ["absq"], ["mq"], lambda e: e.tensor_reduce(out=mq, in_=absq, axis=AX.X, op=ALU.max))
        OP("act", ["kn_bc", "mq"], ["absq"], lambda e: e.activation(out=absq, in_=kn_bc, func=AF.Abs))
        OP("dve", ["absq"], ["mk"], lambda e: e.tensor_reduce(out=mk, in_=absq, axis=AX.X, op=ALU.max))
        OP("dve", ["mq", "mk"], ["negB"], lambda e: e.tensor_tensor(out=negB, in0=mq, in1=mk, op=ALU.mult))
        OP("dve", ["negB"], ["negB"], lambda e: e.tensor_scalar(out=negB, in0=negB, scalar1=-math.sqrt(HD), scalar2=None, op0=ALU.mult))
        PG.barrier()

        def wload(dst, src_cols):
            DMA("pool", dst, src_cols, [], ["W"])

        def hT_stageA(bufs, x_ap, t0, tiles):
            for i in tiles:
                sl = i % bufs["depth"]
                xt, ssq, xn, junk = bufs["xt"][sl], bufs["ssq"][sl], bufs["xn"][sl], bufs["junk"]
                xtr = "xt%d" % sl
                DMA("sp", xt, x_ap[t0 + i * 128:t0 + (i + 1) * 128, :], [], [xtr])
                OP("act", [xtr], ["junk", "ssq%d" % sl], lambda e, xt=xt, ssq=ssq: e.activation(out=junk, in_=xt, func=AF.Square, accum_out=ssq))
                OP("act", ["ssq%d" % sl], ["ssq%d" % sl], lambda e, ssq=ssq: e.activation(out=ssq, in_=ssq, func=AF.Sqrt, scale=1.0 / D, bias=EPS))
                OP("dve", ["ssq%d" % sl], ["ssq%d" % sl], lambda e, ssq=ssq: e.reciprocal(out=ssq, in_=ssq))
                OP("dve", [xtr, "ssq%d" % sl], ["xn%d" % sl], lambda e, xt=xt, ssq=ssq, xn=xn: e.scalar_tensor_tensor(out=xn, in0=xt, scalar=ssq[:, 0:1], in1=gin_bc, op0=ALU.mult, op1=ALU.mult))

        def hT_stageB(bufs, tiles, hT, hT_res, banks=None):
            for i in tiles:
                sl = i % bufs["depth"]
                xn = bufs["xn"][sl]
                if banks is None:
                    b = next_bank()
                else:
                    b = banks[i % len(banks)]
                for c in range(DC):
                    OP("pe", ["xn%d" % sl], ["pb%d" % b], lambda e, c=c, b=b, xn=xn: e.transpose(out=bankb(b)[:, c * 128:(c + 1) * 128], in_=xn[:, c * 128:(c + 1) * 128], identity=ident_b))
                OP("act", ["pb%d" % b], [hT_res], lambda e, b=b, i=i: e.activation(out=hT[:, :, i * 128:(i + 1) * 128], in_=bankb(b).rearrange("p (c t) -> p c t", c=DC), func=AF.Copy))

        def make_hT(bufs, x_ap, t0, nt, hT, hT_res, banks=None):
            dp = bufs["depth"]
            for i0 in range(0, nt, dp):
                tiles = list(range(i0, min(i0 + dp, nt)))
                hT_stageA(bufs, x_ap, t0, tiles)
                hT_stageB(bufs, tiles, hT, hT_res, banks)

        def norm_rope(tb, src_ps, nt, nh, w_bc, cs_t, cs_res, src_res, out_b, out_res):
            nu = nt * nh
            n = nu * HD
            s = tb["sl"] % tb["nbuf"]
            tb["sl"] += 1
            qf, sq, ssq, qn_, t1, t2, t3, t4 = (tb[k][s] for k in ("qf", "sq", "ssq", "qn", "t1", "t2", "t3", "t4"))
            R = lambda k: "%s%d" % (k, s)
            OP("act", src_res, [R("qf")], lambda e: e.activation(out=qf[:, 0:n].rearrange("p (t x) -> p t x", t=nt), in_=src_ps, func=AF.Copy))
            OP("act", [R("qf")], [R("sq")], lambda e: e.activation(out=sq[:, 0:n], in_=qf[:, 0:n], func=AF.Square))
            OP("dve", [R("sq")], [R("qssq")], lambda e: e.tensor_reduce(out=ssq[:, 0:nu], in_=sq[:, 0:n].rearrange("p (h d) -> p h d", h=nu), axis=AX.X, op=ALU.add))
            OP("act", [R("qssq")], [R("qssq")], lambda e: e.activation(out=ssq[:, 0:nu], in_=ssq[:, 0:nu], func=AF.Sqrt, scale=1.0 / HD, bias=EPS))
            OP("dve", [R("qssq")], [R("qssq")], lambda e: e.reciprocal(out=ssq[:, 0:nu], in_=ssq[:, 0:nu]))
            for h in range(nu):
                OP("dve", [R("qf"), R("qssq")], [R("qn")], lambda e, h=h: e.scalar_tensor_tensor(out=qn_[:, h * HD:(h + 1) * HD], in0=qf[:, h * HD:(h + 1) * HD], scalar=ssq[:, h:h + 1], in1=w_bc, op0=ALU.mult, op1=ALU.mult))
            q4 = qn_[:, 0:n].rearrange("p (t h d) -> p t h d", t=nt, h=nh)
            x0, x1 = q4[:, :, :, 0:64], q4[:, :, :, 64:128]
            cosb = cs_t[:, :, 0:64].unsqueeze(2).to_broadcast([128, nt, nh, 64])
            sinb = cs_t[:, :, 64:128].unsqueeze(2).to_broadcast([128, nt, nh, 64])
            o4 = out_b[:, 0:n].rearrange("p (t h d) -> p t h d", t=nt, h=nh)

            def v4(t):
                return t[:, 0:nu * 64].rearrange("p (t h d) -> p t h d", t=nt, h=nh)
            OP("dve", [R("qn"), cs_res], [R("t1")], lambda e: e.tensor_tensor(out=v4(t1), in0=x0, in1=cosb, op=ALU.mult))
            OP("dve", [R("qn"), cs_res], [R("t2")], lambda e: e.tensor_tensor(out=v4(t2), in0=x1, in1=sinb, op=ALU.mult))
            OP("dve", [R("t1"), R("t2")], [out_res], lambda e: e.tensor_tensor(out=o4[:, :, :, 0:64], in0=v4(t1), in1=v4(t2), op=ALU.subtract))
            OP("dve", [R("qn"), cs_res], [R("t3")], lambda e: e.tensor_tensor(out=v4(t3), in0=x0, in1=sinb, op=ALU.mult))
            OP("dve", [R("qn"), cs_res], [R("t4")], lambda e: e.tensor_tensor(out=v4(t4), in0=x1, in1=cosb, op=ALU.mult))
            OP("dve", [R("t3"), R("t4")], [out_res], lambda e: e.tensor_tensor(out=o4[:, :, :, 64:128], in0=v4(t3), in1=v4(t4), op=ALU.add))

        def alloc_x_bufs(depth=2):
            return dict(depth=depth, xt=[AR.alloc([D], F32) for _ in range(depth)], ssq=[AR.alloc([1], F32) for _ in range(depth)],
                        xn=[AR.alloc([D], BF16) for _ in range(depth)], junk=AR.alloc([D], BF16))

        def alloc_rope_bufs(nbuf=2):
            n = 8 * HD
            d = dict(sl=0)
            for k, sz in (("qf", n), ("sq", n), ("ssq", 8), ("qn", n), ("t1", n // 2), ("t2", n // 2), ("t3", n // 2), ("t4", n // 2)):
                d[k] = [AR.alloc([sz], F32) for _ in range(nbuf)]
                if nbuf == 1:
                    d[k] = d[k] * 2
            if nbuf == 1:
                d["sl"] = 0
            d["nbuf"] = nbuf
            return d

        def phase_A1(x_all, S, cs_all):
            AR.reset()
            Wkv = AR.alloc([DC, 512], BF16)
            Wxl = AR.alloc([DC, 1024], BF16)
            wload(Wkv, w_in_v[:, :, OK_:OK_ + 512])
            wload(Wxl, w_in_v[:, :, OXL:OXL + 1024])
            xb = alloc_x_bufs(4)
            rb = alloc_rope_bufs(2)
            hTs = [AR.alloc([DC, CH], BF16) for _ in range(2)]
            cst = [AR.alloc([4, HD], F32) for _ in range(2)]
            kb = [AR.alloc([4 * KVH * HD], BF16) for _ in range(2)]
            kTst = [AR.alloc([KVH, CH], BF16) for _ in range(2)]
            vst = [AR.alloc([KVH, 4, HD], BF16) for _ in range(2)]
            xlst = [AR.alloc([8, CH], F32) for _ in range(2)]
            nchunk = S // CH
            T4 = [0, 1, 2, 3]
            HB = (2, 3, 4, 5)
            hT_stageA(xb, x_all, 0, T4)
            hT_stageB(xb, T4, hTs[0], "hT0", banks=HB)
            for j in range(nchunk):
                s2 = j % 2
                t0 = j * CH
                hT = hTs[s2]
                hres = "hT%d" % s2
                DMA("sp", cst[s2], cs_all[t0:t0 + CH, :].rearrange("(i p) d -> p i d", p=128), [], ["cs%d" % s2])
                if j + 1 < nchunk:
                    hT_stageA(xb, x_all, t0 + CH, T4)
                for i in range(4):
                    b = 2 + i
                    for c in range(DC):
                        OP("pe", [hres, "W"], ["pb%d" % b], lambda e, c=c, b=b, i=i, hT=hT: e.matmul(bank(b), lhsT=hT[:, c, i * 128:(i + 1) * 128], rhs=Wkv[:, c, :], start=(c == 0), stop=(c == DC - 1)))
                kvr = ["pb%d" % (2 + i) for i in range(4)]
                kv4 = ps[:, 2 * 512:6 * 512].rearrange("p (i x) -> p i x", i=4)
                OP("act", kvr, ["vst%d" % s2], lambda e, s2=s2, kv4=kv4: e.activation(out=vst[s2].rearrange("p g i d -> p i g d"), in_=kv4[:, :, 256:512].rearrange("p i (g d) -> p i g d", g=KVH), func=AF.Copy))
                norm_rope(rb, kv4[:, :, 0:256], 4, KVH, kn_bc, cst[s2], "cs%d" % s2, kvr, kb[s2], "kb%d" % s2)
                for eb in range(8):
                    b = 7 if eb % 2 == 0 else 6
                    for c in range(DC):
                        OP("pe", [hres, "W"], ["pb%d" % b], lambda e, c=c, b=b, eb=eb, hT=hT: e.matmul(bank(b), lhsT=Wxl[:, c, eb * 128:(eb + 1) * 128], rhs=hT[:, c, :], start=(c == 0), stop=(c == DC - 1)))
                    if eb % 2 == 0:
                        OP("act", ["pb%d" % b], ["xlst%d" % s2], lambda e, b=b, eb=eb, s2=s2: e.activation(out=xlst[s2][:, eb, :], in_=bank(b), func=AF.Copy))
                    else:
                        OP("dve", ["pb%d" % b], ["xlst%d" % s2], lambda e, b=b, eb=eb, s2=s2: e.tensor_copy(out=xlst[s2][:, eb, :], in_=bank(b)))
                DMA("sp", xl_scr[:, :, t0:t0 + CH], xlst[s2], ["xlst%d" % s2], [])
                kbk = j % 2
                for u in range(4 * KVH):
                    OP("pe", ["kb%d" % s2], ["pb%d" % kbk], lambda e, u=u, s2=s2, kbk=kbk: e.transpose(out=bankb(kbk)[:, u * 128:(u + 1) * 128], in_=kb[s2][:, u * HD:(u + 1) * HD], identity=ident_b))
                OP("dve", ["pb%d" % kbk], ["kTst%d" % s2], lambda e, s2=s2, kbk=kbk: e.tensor_copy(out=kTst[s2].rearrange("p g (i t) -> p i g t", i=4), in_=bankb(kbk).rearrange("p (i g t) -> p i g t", i=4, g=KVH)))
                for g in range(KVH):
                    DMA("sp", kT_scr[g, :, t0:t0 + CH], kTst[s2][:, g, :], ["kTst%d" % s2], [])
                    DMA("sp", v_scr[g, :, j * 4:(j + 1) * 4, :], vst[s2][:, g, :, :], ["vst%d" % s2], [])
                if j + 1 < nchunk:
                    hT_stageB(xb, T4, hTs[1 - s2], "hT%d" % (1 - s2), banks=HB)
            PG.barrier()

        def phase_LRU(S, direction, own_acc):
            AR.reset()
            pipe = not own_acc
            nb = 2 if pipe else 1
            ng = 2 if pipe else 1
            BG = 4
            xlh = [AR.alloc([8, CH + 3], F32) for _ in range(nb)]
            hst = [AR.alloc([8, CH], F32) for _ in range(2)]
            hfl = [AR.alloc([8, CH], F32) for _ in range(nb)] if direction == 1 else None
            acc = AR.alloc([8, OWN], F32) if own_acc else None
            xc = [AR.alloc([CH], F32) for _ in range(BG * ng)]
            xcb = [AR.alloc([CH], BF16) for _ in range(BG * ng)]
            thr = [AR.alloc([CH], F32) for _ in range(BG)]
            thi = [AR.alloc([CH], F32) for _ in range(BG)]
            a_t = [AR.alloc([BG, CH], F32) for _ in range(ng)]
            s_t = [AR.alloc([BG, CH], F32) for _ in range(ng)]
            w_t = [AR.alloc([BG, CH], F32) for _ in range(ng)]
            if own_acc:
                OP("dve", [], ["acc"], lambda e: e.memset(acc, 0.0))
            nchunk = S // CH
            order = list(range(nchunk)) if direction == 0 else list(range(nchunk - 1, -1, -1))

            def load(k):
                j = order[k]
                sl = k % nb
                t0 = j * CH
                lo = max(t0 - 2, 0)
                hi = min(t0 + CH + 1, S)
                if t0 == 0:
                    OP("dve", [], ["xlh%d" % sl], lambda e, sl=sl: e.memset(xlh[sl][:, :, 0:2], 0.0))
                if t0 + CH == S:
                    OP("dve", [], ["xlh%d" % sl], lambda e, sl=sl: e.memset(xlh[sl][:, :, CH + 2:CH + 3], 0.0))
                DMA("sp", xlh[sl][:, :, lo - (t0 - 2):hi - (t0 - 2)], xl_scr[:, :, lo:hi], [], ["xlh%d" % sl])
                if direction == 1:
                    DMA("sp", hfl[sl], hf_scr[:, :, t0:t0 + CH], [], ["hfl%d" % sl])

            items = [(k, g0) for k in range(nchunk) for g0 in range(0, 8, BG)]

            def S1(n):
                k, g0 = items[n]
                sl = k % nb
                for bi in range(BG):
                    eb = g0 + bi
                    xi = (n % ng) * BG + bi
                    cw = chp_t[:, eb, :]
                    src = xlh[sl][:, eb, :]
                    OP("act", ["xlh%d" % sl, "chp"], ["xc%d" % xi], lambda e, xi=xi, src=src, cw=cw: e.activation(out=xc[xi], in_=src[:, 0:CH], func=AF.Identity, scale=cw[:, 0:1], bias=cw[:, 4:5]))
                    for tap in range(1, 4):
                        OP("dve", ["xlh%d" % sl, "xc%d" % xi], ["xc%d" % xi], lambda e, xi=xi, src=src, cw=cw, tap=tap: e.scalar_tensor_tensor(out=xc[xi], in0=src[:, tap:tap + CH], scalar=cw[:, tap:tap + 1], in1=xc[xi], op0=ALU.mult, op1=ALU.add))

            def S2(n):
                k, g0 = items[n]
                for bi in range(BG):
                    eb = g0 + bi
                    xi = (n % ng) * BG + bi
                    OP("pool", ["xc%d" % xi], ["xcb%d" % xi], lambda e, xi=xi: e.tensor_copy(out=xcb[xi], in_=xc[xi]))
                    OP("pe", ["xcb%d" % xi, "wg"], ["pb%d" % (2 * bi)], lambda e, bi=bi, eb=eb, xi=xi: e.matmul(bank(2 * bi), lhsT=wg[:, 0 + direction, eb, :], rhs=xcb[xi], start=True, stop=True))
                    OP("pe", ["xcb%d" % xi, "wg"], ["pb%d" % (2 * bi + 1)], lambda e, bi=bi, eb=eb, xi=xi: e.matmul(bank(2 * bi + 1), lhsT=wg[:, 2 + direction, eb, :], rhs=xcb[xi], start=True, stop=True))

            def S3a(n):
                k, g0 = items[n]
                gs = n % ng
                for bi in range(BG):
                    eb = g0 + bi
                    ci = direction * 8 + eb
                    OP("act", ["pb%d" % (2 * bi)], ["thr%d" % bi], lambda e, bi=bi, ci=ci: e.activation(out=thr[bi], in_=bank(2 * bi), func=AF.Tanh, scale=0.5, bias=brh[:, ci:ci + 1]))
                    OP("act", ["pb%d" % (2 * bi + 1)], ["thi%d" % bi], lambda e, bi=bi, ci=ci: e.activation(out=thi[bi], in_=bank(2 * bi + 1), func=AF.Tanh, scale=0.5, bias=bih[:, ci:ci + 1]))
                    OP("act", ["thr%d" % bi], ["a%d" % gs], lambda e, bi=bi, gs=gs, ci=ci: e.activation(out=a_t[gs][:, bi, :], in_=thr[bi], func=AF.Exp, scale=cl_h[:, ci:ci + 1], bias=cl_h[:, ci:ci + 1]))
                    OP("act", ["thr%d" % bi], ["s%d" % gs], lambda e, bi=bi, gs=gs, ci=ci: e.activation(out=s_t[gs][:, bi, :], in_=thr[bi], func=AF.Exp, scale=cl_1[:, ci:ci + 1], bias=cl_1[:, ci:ci + 1]))
                OP("act", ["s%d" % gs], ["s%d" % gs], lambda e, gs=gs: e.activation(out=s_t[gs], in_=s_t[gs], func=AF.Sqrt, scale=-1.0, bias=1.0))

            def S3b(n):
                k, g0 = items[n]
                gs = n % ng
                hs = k % 2
                sl = k % nb
                j = order[k]
                t0 = j * CH
                for bi in range(BG):
                    xi = (n % ng) * BG + bi
                    OP("dve", ["thi%d" % bi, "xc%d" % xi], ["w%d" % gs], lambda e, bi=bi, gs=gs, xi=xi: e.scalar_tensor_tensor(out=w_t[gs][:, bi, :], in0=thi[bi], scalar=1.0, in1=xc[xi], op0=ALU.add, op1=ALU.mult))
                OP("dve", ["s%d" % gs, "w%d" % gs], ["w%d" % gs], lambda e, gs=gs: e.scalar_tensor_tensor(out=w_t[gs], in0=s_t[gs], scalar=0.5, in1=w_t[gs], op0=ALU.mult, op1=ALU.mult))
                for bi in range(BG):
                    eb = g0 + bi
                    if k == 0:
                        init = 0.0
                        rd = []
                    else:
                        pcol = CH - 1 if direction == 0 else 0
                        init = hst[1 - hs][:, eb, pcol:pcol + 1]
                        rd = ["hst%d" % (1 - hs)]
                    if direction == 0:
                        OP("dve", ["a%d" % gs, "w%d" % gs] + rd, ["hst%d" % hs], lambda e, gs=gs, bi=bi, eb=eb, hs=hs, init=init: e.tensor_tensor_scan(out=hst[hs][:, eb, :], data0=a_t[gs][:, bi, :], data1=w_t[gs][:, bi, :], initial=init, op0=ALU.mult, op1=ALU.add))
                    else:
                        OP("dve", ["a%d" % gs, "w%d" % gs] + rd, ["hst%d" % hs], lambda e, gs=gs, bi=bi, eb=eb, hs=hs, init=init: e.tensor_tensor_scan(out=hst[hs][:, eb, ::-1], data0=a_t[gs][:, bi, ::-1], data1=w_t[gs][:, bi, ::-1], initial=init, op0=ALU.mult, op1=ALU.add))
                if g0 + BG == 8:
                    if direction == 0:
                        DMA("sp", hf_scr[:, :, t0:t0 + CH], hst[hs], ["hst%d" % hs], [])
                    else:
                        OP("dve", ["hst%d" % hs, "hfl%d" % sl], ["hfl%d" % sl], lambda e, hs=hs, sl=sl: e.tensor_tensor(out=hfl[sl], in0=hfl[sl], in1=hst[hs], op=ALU.add))
                        if own_acc:
                            slot = j % NSLOT
                            grp = j // NSLOT
                            OP("dve", ["hfl%d" % sl, "acc", "msk"], ["acc"], lambda e, sl=sl, slot=slot, grp=grp: e.scalar_tensor_tensor(out=acc[:, :, slot * CH:(slot + 1) * CH], in0=hfl[sl], scalar=msk_t[:, grp:grp + 1], in1=acc[:, :, slot * CH:(slot + 1) * CH], op0=ALU.mult, op1=ALU.add))
                        else:
                            DMA("sp", ls_scr[:, :, t0:t0 + CH], hfl[sl], ["hfl%d" % sl], [])

            load(0)
            if pipe:
                S1(0)
                S2(0)
                for n in range(len(items)):
                    k, g0 = items[n]
                    if g0 == 0 and k + 1 < nchunk:
                        load(k + 1)
                    if n + 1 < len(items):
                        S1(n + 1)
                    S3a(n)
                    if n + 1 < len(items):
                        S2(n + 1)
                    S3b(n)
            else:
                for n in range(len(items)):
                    k, g0 = items[n]
                    S1(n)
                    S2(n)
                    S3a(n)
                    S3b(n)
                    if g0 + BG == 8 and k + 1 < nchunk:
                        load(k + 1)
            if own_acc:
                DMA("sp", ls_scr[:, :, 0:OWN], acc, ["acc"], [])
            PG.barrier()

        def phase_B1(x_own, N_OWN, cs_own, S):
            AR.reset()
            QB = 1024 if N_OWN % 1024 == 0 else 512
            QH = QB // 512
            NKT = S // 128
            big = S > 8192
            Wq = AR.alloc([DC, 1024], BF16)
            Wga = AR.alloc([DC, 1024], BF16)
            wload(Wq, w_in_v[:, :, OQ:OQ + 1024])
            wload(Wga, w_in_v[:, :, OGA:OGA + 1024])
            resident = not big
            if resident:
                KT2 = AR.alloc([KVH, S], BF16)
                VV2 = AR.alloc([KVH, NKT, HD], BF16)
                for g in range(KVH):
                    DMA("sp", KT2[:, g, :], kT_scr[g, :, 0:S], [], ["KT"])
                    DMA("sp", VV2[:, g, :, :], v_scr[g, :, 0:NKT, :], [], ["VV"])
            else:
                KT1 = AR.alloc([S], BF16)
                VV1 = AR.alloc([NKT, HD], BF16)
            xb = alloc_x_bufs()
            rb = alloc_rope_bufs(1)
            hT = AR.alloc([DC, QB], BF16)
            qT = AR.alloc([H, QB], BF16)
            cst = [AR.alloc([1, HD], F32) for _ in range(2)]
            qb_ = [AR.alloc([H * HD], BF16) for _ in range(2)]
            pT = [AR.alloc([QB], BF16) for _ in range(3)]
            rs = AR.alloc([QB], F32)
            nsg = 1
            XP = 384 if QB == 1024 else 192
            XD = QB - XP
            sg = [AR.alloc([QB], F32) for _ in range(nsg)]
            aob = [AR.alloc([QB], BF16)]
            tctr = 0
            cnt = [0]
            hctr = 0
            for qb in range(N_OWN // QB):
                q0 = qb * QB
                make_hT(xb, x_own, q0, QB // 128, hT, "hT")
                for i in range(QB // 128):
                    ts = tctr % 2
                    tctr += 1
                    b0 = 2 * (next_bank() % 4)
                    for hf in range(2):
                        for c in range(DC):
                            OP("pe", ["hT", "W"], ["pb%d" % (b0 + hf)], lambda e, c=c, b0=b0, hf=hf, i=i: e.matmul(bank(b0 + hf), lhsT=hT[:, c, i * 128:(i + 1) * 128], rhs=Wq[:, c, hf * 512:(hf + 1) * 512], start=(c == 0), stop=(c == DC - 1)))
                    DMA("sp", cst[ts], cs_own[q0 + i * 128:q0 + (i + 1) * 128, :].rearrange("(i p) d -> p i d", p=128), [], ["cs%d" % ts])
                    norm_rope(rb, ps[:, b0 * 512:b0 * 512 + 1024].rearrange("p (t x) -> p t x", t=1), 1, H, qn_bc, cst[ts], "cs%d" % ts, ["pb%d" % b0, "pb%d" % (b0 + 1)], qb_[ts], "qb%d" % ts)
                    b2 = next_bank()
                    for h in range(H):
                        OP("pe", ["qb%d" % ts], ["pb%d" % b2], lambda e, h=h, b2=b2, ts=ts: e.transpose(out=bankb(b2)[:, h * 128:(h + 1) * 128], in_=qb_[ts][:, h * HD:(h + 1) * HD], identity=ident_b))
                    OP("dve", ["pb%d" % b2], ["qT"], lambda e, b2=b2, i=i: e.tensor_copy(out=qT[:, :, i * 128:(i + 1) * 128], in_=bankb(b2).rearrange("p (h t) -> p h t", h=H)))
                for g in range(KVH):
                    if resident:
                        KT = KT2[:, g, :]
                        VV = VV2[:, g, :, :]
                    else:
                        KT, VV = KT1, VV1
                        DMA("sp", KT, kT_scr[g, :, 0:S], [], ["KT"])
                        DMA("sp", VV, v_scr[g, :, 0:NKT, :], [], ["VV"])
                    for hh in range(H // KVH):
                        h = g * (H // KVH) + hh
                        hsg = hctr % nsg
                        hctr += 1
                        for hf in range(QH):
                            for c in range(DC):
                                OP("pe", ["hT", "W"], ["pb%d" % hf], lambda e, c=c, hf=hf, h=h: e.matmul(bank(hf), lhsT=Wga[:, c, h * 128:(h + 1) * 128], rhs=hT[:, c, hf * 512:(hf + 1) * 512], start=(c == 0), stop=(c == DC - 1)))
                        OP("act", ["pb%d" % hf for hf in range(QH)], ["sg%d" % hsg], lambda e, hsg=hsg: e.activation(out=sg[hsg], in_=ps[:, 0:QB], func=AF.Silu))
                        slots = []

                        def emit_qk(kt, h=h, KT=KT):
                            sp_ = (cnt[0] % 2) * 2
                            pt = cnt[0] % 3
                            cnt[0] += 1
                            slots.append((sp_, pt))
                            for hf in range(QH):
                                OP("pe", ["KT", "qT"], ["pb%d" % (sp_ + hf)], lambda e, kt=kt, sp_=sp_, hf=hf, h=h, KT=KT: e.matmul(bank(sp_ + hf), lhsT=KT[:, kt * 128:(kt + 1) * 128], rhs=qT[:, h, hf * 512:(hf + 1) * 512], start=True, stop=True))

                        emit_qk(0)
                        for kt in range(NKT):
                            if kt + 1 < NKT:
                                emit_qk(kt + 1)
                            sp_, pt = slots[kt]
                            OP("act", ["pb%d" % (sp_ + hf) for hf in range(QH)] + ["negB"], ["pT%d" % pt], lambda e, sp_=sp_, pt=pt: e.activation(out=pT[pt], in_=ps[:, sp_ * 512:sp_ * 512 + QB], func=AF.Exp, scale=SM_SCALE, bias=negB[:, 0:1]))
                            for hf in range(QH):
                                OP("pe", ["pT%d" % pt, "VV"], ["pb%d" % (4 + hf)], lambda e, kt=kt, pt=pt, hf=hf, VV=VV: e.matmul(bank(4 + hf), lhsT=VV[:, kt, :], rhs=pT[pt][:, hf * 512:(hf + 1) * 512], start=(kt == 0), stop=(kt == NKT - 1)))
                            OP("pe", ["pT%d" % pt, "ones_b"], ["pb%d" % (6 + QH - 1)], lambda e, kt=kt, pt=pt: e.matmul(ps[:, 6 * 512 + XD:6 * 512 + QB], lhsT=ones_b, rhs=pT[pt][:, XD:QB], start=(kt == 0), stop=(kt == NKT - 1)))
                            if kt == 0:
                                OP("dve", ["pT%d" % pt], ["rs"], lambda e, pt=pt: e.tensor_copy(out=rs[:, 0:XD], in_=pT[pt][:, 0:XD]))
                            else:
                                OP("dve", ["pT%d" % pt, "rs"], ["rs"], lambda e, pt=pt: e.tensor_tensor(out=rs[:, 0:XD], in0=rs[:, 0:XD], in1=pT[pt][:, 0:XD], op=ALU.add))
                        accr = ["pb%d" % (4 + hf) for hf in range(QH)]
                        sumr = ["pb%d" % (6 + hf) for hf in range(QH)]
                        for hf in range(QH):
                            c0, c1 = hf * 512, min((hf + 1) * 512, XD)
                            if c1 > c0:
                                OP("pe", ["rs", "ones_f"], ["pb%d" % (6 + hf)], lambda e, c0=c0, c1=c1: e.matmul(ps[:, 6 * 512 + c0:6 * 512 + c1], lhsT=ones_f, rhs=rs[:, c0:c1], start=True, stop=True))
                        OP("dve", sumr, ["rs"], lambda e: e.reciprocal(out=rs, in_=ps[:, 6 * 512:6 * 512 + QB]))
                        OP("dve", accr + ["rs"], ["rs"], lambda e: e.tensor_tensor(out=rs, in0=ps[:, 4 * 512:4 * 512 + QB], in1=rs, op=ALU.mult))
                        OP("dve", ["rs", "sg%d" % hsg], ["aob0"], lambda e, hsg=hsg: e.tensor_tensor(out=aob[0], in0=rs, in1=sg[hsg], op=ALU.mult))
                        DMA("sp", ao_scr[:, h, q0:q0 + QB], aob[0], ["aob0"], [])
            PG.barrier()

        def phase_B2(x_own, N_OWN, y_out):
            AR.reset()
            Wgl = AR.alloc([DC, 1024], BF16)
            Wml = AR.alloc([DC, 2048], BF16)
            WbA = AR.alloc([DC, 1024], BF16)
            WbL = AR.alloc([DC, 1024], BF16)
            Wo = AR.alloc([DC, 1024], BF16)
            wload(Wgl, w_in_v[:, :, OGL:OGL + 1024])
            wload(Wml[:, :, 0:1024], w_in_v[:, :, OML:OML + 1024])
            wload(Wml[:, :, 1024:2048], w_in_v[:, :, OML + 1024:OML + 2048])
            wload(WbA, w_br[0].rearrange("(c p) n -> p c n", p=128))
            wload(WbL, w_br[1].rearrange("(c p) n -> p c n", p=128))
            wload(Wo, w_out.rearrange("(c p) n -> p c n", p=128))
            xb = alloc_x_bufs()
            hT = AR.alloc([DC, CH], BF16)
            hres = "hT"
            lsl = [AR.alloc([CH], F32) for _ in range(2)]
            aol = AR.alloc([8, CH], BF16)
            loT = AR.alloc([8, CH], BF16)
            mT = AR.alloc([8, CH], BF16)
            sgl = [AR.alloc([CH], F32) for _ in range(2)]
            ga = [AR.alloc([CH], F32) for _ in range(2)]
            gl = [AR.alloc([CH], F32) for _ in range(2)]
            t1 = [AR.alloc([CH], F32) for _ in range(2)]
            t2 = [AR.alloc([CH], F32) for _ in range(2)]
            yt = [AR.alloc([D], F32)]
            yo = [AR.alloc([D], F32)]
            yss = [AR.alloc([1], F32)]
            xres = [AR.alloc([D], F32)]
            junk2 = xb["junk"]
            nchunk = N_OWN // CH
            ectr = 0
            yctr = 0
            for j in range(nchunk):
                t0 = j * CH
                DMA("sp", aol, ao_scr[:, :, t0:t0 + CH], [], ["aol"])
                make_hT(xb, x_own, t0, 4, hT, hres)
                for eb in range(8):
                    es = ectr % 2
                    ectr += 1
                    DMA("sp", lsl[es], ls_scr[:, eb, t0:t0 + CH], [], ["lsl%d" % es])
                    b = next_bank()
                    for c in range(DC):
                        OP("pe", [hres, "W"], ["pb%d" % b], lambda e, c=c, b=b, eb=eb: e.matmul(bank(b), lhsT=Wgl[:, c, eb * 128:(eb + 1) * 128], rhs=hT[:, c, :], start=(c == 0), stop=(c == DC - 1)))
                    OP("act", ["pb%d" % b], ["sgl%d" % es], lambda e, b=b, es=es: e.activation(out=sgl[es], in_=bank(b), func=AF.Silu))
                    OP("dve", ["sgl%d" % es, "lsl%d" % es], ["loT"], lambda e, es=es, eb=eb: e.tensor_tensor(out=loT[:, eb, :], in0=lsl[es], in1=sgl[es], op=ALU.mult))
                for eb in range(8):
                    es = ectr % 2
                    ectr += 1
                    bA, bL, bga, bgl = next_bank(), next_bank(), next_bank(), next_bank()
                    for c in range(DC):
                        OP("pe", ["aol", "W"], ["pb%d" % bA], lambda e, c=c, bA=bA, eb=eb: e.matmul(bank(bA), lhsT=WbA[:, c, eb * 128:(eb + 1) * 128], rhs=aol[:, c, :], start=(c == 0), stop=(c == DC - 1)))
                    for c in range(DC):
                        OP("pe", ["loT", "W"], ["pb%d" % bL], lambda e, c=c, bL=bL, eb=eb: e.matmul(bank(bL), lhsT=WbL[:, c, eb * 128:(eb + 1) * 128], rhs=loT[:, c, :], start=(c == 0), stop=(c == DC - 1)))
                    for c in range(DC):
                        OP("pe", [hres, "W"], ["pb%d" % bga], lambda e, c=c, bga=bga, eb=eb: e.matmul(bank(bga), lhsT=Wml[:, c, eb * 128:(eb + 1) * 128], rhs=hT[:, c, :], start=(c == 0), stop=(c == DC - 1)))
                    for c in range(DC):
                        OP("pe", [hres, "W"], ["pb%d" % bgl], lambda e, c=c, bgl=bgl, eb=eb: e.matmul(bank(bgl), lhsT=Wml[:, c, 1024 + eb * 128:1024 + (eb + 1) * 128], rhs=hT[:, c, :], start=(c == 0), stop=(c == DC - 1)))
                    OP("act", ["pb%d" % bga, "bm"], ["ga%d" % es], lambda e, bga=bga, es=es, eb=eb: e.activation(out=ga[es], in_=bank(bga), func=AF.Sigmoid, bias=bm_t[:, eb:eb + 1]))
                    OP("act", ["pb%d" % bgl, "bm"], ["gl%d" % es], lambda e, bgl=bgl, es=es, eb=eb: e.activation(out=gl[es], in_=bank(bgl), func=AF.Sigmoid, bias=bm_t[:, 8 + eb:9 + eb]))
                    OP("dve", ["pb%d" % bA, "ga%d" % es], ["t1%d" % es], lambda e, bA=bA, es=es: e.tensor_tensor(out=t1[es], in0=bank(bA), in1=ga[es], op=ALU.mult))
                    OP("dve", ["pb%d" % bL, "gl%d" % es], ["t2%d" % es], lambda e, bL=bL, es=es: e.tensor_tensor(out=t2[es], in0=bank(bL), in1=gl[es], op=ALU.mult))
                    OP("pool", ["t1%d" % es, "t2%d" % es], ["mT"], lambda e, es=es, eb=eb: e.tensor_tensor(out=mT[:, eb, :], in0=t1[es], in1=t2[es], op=ALU.add))
                for i in range(4):
                    ys_ = 0
                    b0 = 2 * (next_bank() % 4)
                    DMA("sp", xres[ys_], x_own[t0 + i * 128:t0 + (i + 1) * 128, :], [], ["xres%d" % ys_])
                    for hf in range(2):
                        for c in range(DC):
                            OP("pe", ["mT", "W"], ["pb%d" % (b0 + hf)], lambda e, c=c, b0=b0, hf=hf, i=i: e.matmul(bank(b0 + hf), lhsT=mT[:, c, i * 128:(i + 1) * 128], rhs=Wo[:, c, hf * 512:(hf + 1) * 512], start=(c == 0), stop=(c == DC - 1)))
                    OP("dve", ["pb%d" % b0, "pb%d" % (b0 + 1), "xres%d" % ys_], ["yt%d" % ys_], lambda e, b0=b0, ys_=ys_: e.tensor_tensor(out=yt[ys_], in0=ps[:, b0 * 512:b0 * 512 + 1024], in1=xres[ys_], op=ALU.add))
                    OP("act", ["yt%d" % ys_], ["junk2", "yss%d" % ys_], lambda e, ys_=ys_: e.activation(out=junk2, in_=yt[ys_], func=AF.Square, accum_out=yss[ys_]))
                    OP("act", ["yss%d" % ys_], ["yss%d" % ys_], lambda e, ys_=ys_: e.activation(out=yss[ys_], in_=yss[ys_], func=AF.Sqrt, scale=1.0 / D, bias=EPS))
                    OP("dve", ["yss%d" % ys_], ["yss%d" % ys_], lambda e, ys_=ys_: e.reciprocal(out=yss[ys_], in_=yss[ys_]))
                    OP("dve", ["yt%d" % ys_, "yss%d" % ys_], ["yo%d" % ys_], lambda e, ys_=ys_: e.scalar_tensor_tensor(out=yo[ys_], in0=yt[ys_], scalar=yss[ys_][:, 0:1], in1=gfin_bc, op0=ALU.mult, op1=ALU.mult))
                    DMA("sp", y_out[t0 + i * 128:t0 + (i + 1) * 128, :], yo[ys_], ["yo%d" % ys_], [])
            PG.barrier()

        for (x_all, S, cs_all, x_own, n_own, cs_own, y_out, is_p) in (
            (xs, S_S, cs_s, xs, S_S, cs_s, ys, False),
            (xp, S_P, cs_p, xq, OWN, cs_q, yq, True),
        ):
            phase_A1(x_all, S, cs_all)
            phase_LRU(S, 0, False)
            phase_LRU(S, 1, is_p)
            phase_B1(x_own, n_own, cs_own, S)
            phase_B2(x_own, n_own, y_out)
        PG.emit(nc)
    return nc


def _rope_table(S):
    grid_w = 64
    pos = np.arange(S)
    rows = (pos // grid_w).astype(np.float32)
    cols = (pos % grid_w).astype(np.float32)
    n = HD // 4
    inv = (np.float32(10000.0) ** (-(np.arange(n, dtype=np.float32) / np.float32(n)))).astype(np.float32)
    ang = np.concatenate([rows[:, None] * inv[None, :], cols[:, None] * inv[None, :]], axis=-1).astype(np.float32)
    return np.concatenate([np.cos(ang), np.sin(ang)], axis=-1).astype(np.float32)


_NC_CACHE = {}


def kernel(x_prompt, x_sample, norm_in, w_in, b_merge, q_norm, k_norm, conv_w, conv_b,
           w_rgate, b_rgate, w_igate, b_igate, lam, w_branch, w_out, norm_final):
    f = lambda a: np.ascontiguousarray(np.asarray(a, dtype=np.float32))
    x_prompt, x_sample = f(x_prompt), f(x_sample)
    S_P = x_prompt.shape[1]
    S_S = x_sample.shape[1]
    OWN = S_P // NCORES
    assert x_prompt.shape[0] == 1 and x_sample.shape[0] == NCORES
    perm = np.concatenate([np.arange(0, HD, 2), np.arange(1, HD, 2)])
    cols = np.arange(NPROJ)
    for hh in range(H):
        cols[OQ + hh * HD:OQ + (hh + 1) * HD] = OQ + hh * HD + perm
    for hh in range(KVH):
        cols[OK_ + hh * HD:OK_ + (hh + 1) * HD] = OK_ + hh * HD + perm
    w_in_p = f(f(w_in)[0][:, cols])
    qn = f(f(q_norm)[0][perm][None, :])
    kn = f(f(k_norm)[0][perm][None, :])
    plist = [f(conv_w)[0][0], f(conv_w)[0][1], f(conv_w)[0][2], f(conv_w)[0][3], f(conv_b)[0],
             f(b_rgate)[0][0], f(b_rgate)[0][1], f(b_igate)[0][0], f(b_igate)[0][1], f(lam)[0][0], f(lam)[0][1]]
    chp = f(np.stack(plist, axis=-1).reshape(8, 128, 11).transpose(1, 0, 2))
    w_gate = f(np.stack([f(w_rgate)[0][0], f(w_rgate)[0][1], f(w_igate)[0][0], f(w_igate)[0][1]], axis=0))
    bm = f(f(b_merge)[0].reshape(16, 128).T)
    cs_p = _rope_table(S_P)
    cs_s = _rope_table(S_S)
    ident = np.eye(128, dtype=np.float32)
    key = (S_S, S_P)
    if key not in _NC_CACHE:
        _NC_CACHE[key] = build_nc(S_S, S_P)
    nc = _NC_CACHE[key]
    xp2 = x_prompt[0]
    in_maps = []
    for c in range(NCORES):
        m = np.zeros((128, NCORES), np.float32)
        m[:, c] = 1.0
        in_maps.append(dict(
            xs=x_sample[c], xp=xp2, xq=f(xp2[c * OWN:(c + 1) * OWN]),
            cs_s=cs_s, cs_p=cs_p, cs_q=f(cs_p[c * OWN:(c + 1) * OWN]),
            w_in=w_in_p, w_br=f(w_branch)[0], w_out=f(w_out)[0], w_gate=w_gate, chp=chp,
            g_in=f(norm_in), g_fin=f(norm_final)[None, :], qn=qn, kn=kn, bm=bm, msk=m, ident=ident))
    res = run_bass_kernel_spmd(nc, in_maps, core_ids=list(range(NCORES)))
    y_s = np.stack([np.asarray(res.results[c]["ys"], dtype=np.float32) for c in range(NCORES)], axis=0)
    y_p = np.concatenate([np.asarray(res.results[c]["yq"], dtype=np.float32) for c in range(NCORES)], axis=0)[None]
    return (y_p, y_s)
```

```python
import contextlib
import math
import numpy as np
import concourse.bass as bass
import concourse.mybir as mybir
from concourse.bass_utils import run_bass_kernel_spmd

F32 = mybir.dt.float32
BF16 = mybir.dt.bfloat16
AF = mybir.ActivationFunctionType
ALU = mybir.AluOpType
AX = mybir.AxisListType

NCORES = 8
D = 1024
DC = 8
H = 8
KVH = 2
HD = 128
CH = 512
NPROJ = 6656
EPS = 1e-6
SM_SCALE = 1.0 / math.sqrt(HD)
OQ, OK_, OV, OGA, OXL, OGL, OML = 0, 1024, 1280, 1536, 2560, 3584, 4608


class Prog:
    ENG = ("sp", "act", "dve", "pool", "pe")

    def __init__(self):
        self.ops = []
        self.last_writer = {}
        self.readers = {}
        self.last_on = {e: None for e in self.ENG}
        self.dma_since = set()
        self.pending = {e: set() for e in self.ENG}

    def op(self, eng, fn, reads=(), writes=(), dma=False):
        idx = len(self.ops)
        deps = {}
        for r in reads:
            w = self.last_writer.get(r)
            if w is not None:
                deps[w] = "raw"
        for r in writes:
            w = self.last_writer.get(r)
            if w is not None and w not in deps:
                deps[w] = "waw"
            for rd in self.readers.get(r, ()):
                if rd not in deps:
                    deps[rd] = "war"
        if self.pending[eng]:
            for p in self.pending[eng]:
                deps[p] = "raw"
            self.pending[eng] = set()
        for r in reads:
            self.readers.setdefault(r, []).append(idx)
        for r in writes:
            self.last_writer[r] = idx
            self.readers[r] = []
        self.ops.append(dict(eng=eng, fn=fn, dma=dma, deps=deps, idx=idx))
        self.last_on[eng] = idx
        if dma:
            self.dma_since.add(idx)
        return idx

    def barrier(self):
        s = set(self.dma_since)
        for e in self.ENG:
            if self.last_on[e] is not None:
                s.add(self.last_on[e])
        self.dma_since = set()
        self.pending = {e: set(s) for e in self.ENG}
        self.last_writer.clear()
        self.readers.clear()

    def emit(self, nc, ndma_sems=20):
        ops = self.ops
        for o in ops:
            keep = []
            for p, kind in o["deps"].items():
                po = ops[p]
                if (not po["dma"]) and (not o["dma"]) and po["eng"] == o["eng"]:
                    if not (kind == "raw" and o["eng"] in ("act", "dve", "pool")):
                        continue
                keep.append(p)
            o["deps"] = sorted(keep)
            o["signal"] = False
        for o in ops:
            for p in o["deps"]:
                ops[p]["signal"] = True
        cnt = {e: 0 for e in self.ENG}
        dma_i = {e: 0 for e in self.ENG}
        dma_cum = {}
        for o in ops:
            if o["dma"]:
                slot = (o["eng"], dma_i[o["eng"]] % ndma_sems)
                dma_i[o["eng"]] += 1
                prev = dma_cum.get(slot, 0)
                o["sem"], o["prev"], o["val"] = slot, prev, prev + 16
                dma_cum[slot] = prev + 16
            elif o["signal"]:
                cnt[o["eng"]] += 1
                o["sem"], o["val"] = o["eng"], cnt[o["eng"]]
        dma_queues = sorted({o["eng"] for o in ops if o["dma"]})
        with contextlib.ExitStack() as st:
            sems = {}
            for e in self.ENG:
                sems[e] = st.enter_context(nc.semaphore("s_" + e))
            for q in dma_queues:
                for i in range(ndma_sems):
                    sems[(q, i)] = st.enter_context(nc.semaphore("d_%s_%d" % (q, i)))
            block = st.enter_context(nc.Block())
            per_eng = {e: [o for o in ops if o["eng"] == e] for e in self.ENG}

            def run_engine(ename, eng):
                waited = {}
                for o in per_eng[ename]:
                    need = {}
                    for p in o["deps"]:
                        po = ops[p]
                        s = po["sem"]
                        if po["val"] > need.get(s, 0):
                            need[s] = po["val"]
                    if o["dma"] and o["prev"] > 0:
                        s = o["sem"]
                        if o["prev"] > need.get(s, 0):
                            need[s] = o["prev"]
                    for s, v in need.items():
                        if waited.get(s, 0) >= v:
                            continue
                        eng.wait_ge(sems[s], v)
                        waited[s] = v
                    ins = o["fn"](eng)
                    if o["dma"]:
                        ins.then_inc(sems[o["sem"]], 16)
                    elif o["signal"]:
                        ins.then_inc(sems[o["sem"]], 1)
                if ename == "sp":
                    for slot, v in dma_cum.items():
                        if waited.get(slot, 0) < v:
                            eng.wait_ge(sems[slot], v)

            @block.sync
            def _(e):
                run_engine("sp", e)

            @block.scalar
            def _(e):
                run_engine("act", e)

            @block.vector
            def _(e):
                run_engine("dve", e)

            @block.gpsimd
            def _(e):
                run_engine("pool", e)

            @block.tensor
            def _(e):
                run_engine("pe", e)


class Arena:
    def __init__(self, ap, nwords):
        self.ap = ap
        self.n = nwords
        self.off = 0
        self.mark = 0

    def alloc(self, shape, dt):
        n = 1
        for s in shape:
            n *= s
        words = n if dt == F32 else (n + 1) // 2
        words = (words + 7) // 8 * 8
        assert self.off + words <= self.n, ("arena overflow", self.off, words, self.n)
        v = self.ap[:, self.off:self.off + words]
        self.off += words
        if dt != F32:
            v = v.bitcast(dt)
        v = v[:, 0:n]
        if len(shape) == 2:
            v = v.rearrange("p (a b) -> p a b", a=shape[0])
        elif len(shape) == 3:
            v = v.rearrange("p (a b c) -> p a b c", a=shape[0], b=shape[1])
        return v

    def persist(self):
        self.mark = self.off

    def reset(self):
        self.off = self.mark


def build_nc(S_S, S_P):
    OWN = S_P // NCORES
    assert OWN % CH == 0 and S_S % CH == 0
    SMAX = max(S_S, S_P)
    NSLOT = OWN // CH
    nc = bass.Bass("TRN2", target_bir_lowering=False)

    def din(name, shape, dt=F32):
        return nc.dram_tensor(name, list(shape), dt, kind="ExternalInput").ap()

    xs = din("xs", [S_S, D])
    xp = din("xp", [S_P, D])
    xq = din("xq", [OWN, D])
    cs_s = din("cs_s", [S_S, HD])
    cs_p = din("cs_p", [S_P, HD])
    cs_q = din("cs_q", [OWN, HD])
    w_in = din("w_in", [D, NPROJ])
    w_br = din("w_br", [2, D, D])
    w_out = din("w_out", [D, D])
    w_gate = din("w_gate", [4, 8, 128, 128])
    chp = din("chp", [128, 8, 11])
    g_in = din("g_in", [1, D])
    g_fin = din("g_fin", [1, D])
    qn = din("qn", [1, HD])
    kn = din("kn", [1, HD])
    bm = din("bm", [128, 16])
    msk = din("msk", [128, NCORES])
    ident = din("ident", [128, 128])
    ys = nc.dram_tensor("ys", [S_S, D], F32, kind="ExternalOutput").ap()
    yq = nc.dram_tensor("yq", [OWN, D], F32, kind="ExternalOutput").ap()
    kT_scr = nc.dram_tensor("kT_scr", [KVH, 128, SMAX], BF16).ap()
    v_scr = nc.dram_tensor("v_scr", [KVH, 128, SMAX // 128, HD], BF16).ap()
    xl_scr = nc.dram_tensor("xl_scr", [128, 8, SMAX], F32).ap()
    hf_scr = nc.dram_tensor("hf_scr", [128, 8, SMAX], F32).ap()
    ls_scr = nc.dram_tensor("ls_scr", [128, 8, max(S_S, OWN)], F32).ap()
    ao_scr = nc.dram_tensor("ao_scr", [128, 8, max(S_S, OWN)], BF16).ap()
    hT_scr = nc.dram_tensor("hT_scr", [128, 8, max(S_S, OWN)], BF16).ap()
    qT_scr = nc.dram_tensor("qT_scr", [128, 8, max(S_S, OWN)], BF16).ap()

    PG = Prog()
    w_in_v = w_in.rearrange("(c p) n -> p c n", p=128)

    def OP(eng, reads, writes, f):
        PG.op(eng, f, reads, writes)

    def DMA(eng, out, in_, reads, writes):
        PG.op(eng, lambda e: e.dma_start(out=out, in_=in_), reads, writes, dma=True)

    with contextlib.ExitStack() as st:
        NW = 53000
        arena_t = st.enter_context(nc.sbuf_tensor("arena", [128, NW], F32))
        ps_t = st.enter_context(nc.psum_tensor("ps", [128, 4096], F32))
        AR = Arena(arena_t, NW)
        ps = ps_t
        psb = ps_t[:].bitcast(BF16)

        def bank(b, n=512):
            return ps[:, b * 512:b * 512 + n]

        def bankb(b, n=1024):
            return psb[:, b * 1024:b * 1024 + n]

        bank_ctr = [0]

        def next_bank():
            b = bank_ctr[0] % 8
            bank_ctr[0] += 1
            return b

        ident_b = AR.alloc([128], BF16)
        ones_b = AR.alloc([128], BF16)
        ones_f = AR.alloc([128], F32)
        gin_bc = AR.alloc([D], F32)
        gfin_bc = AR.alloc([D], F32)
        qn_bc = AR.alloc([HD], F32)
        kn_bc = AR.alloc([HD], F32)
        chp_t = AR.alloc([8, 11], F32)
        bm_t = AR.alloc([16], F32)
        bmh_t = AR.alloc([16], F32)
        msk_t = AR.alloc([NCORES], F32)
        lam_c = AR.alloc([16], F32)
        cl_h = AR.alloc([16], F32)
        cl_1 = AR.alloc([16], F32)
        brh = AR.alloc([16], F32)
        bih = AR.alloc([16], F32)
        tmpa = AR.alloc([16], F32)
        tmpb = AR.alloc([16], F32)
        tmpc = AR.alloc([16], F32)
        negB = AR.alloc([1], F32)
        mq = AR.alloc([1], F32)
        mk = AR.alloc([1], F32)
        absq = AR.alloc([HD], F32)
        wg = AR.alloc([4, 8, 128], BF16)
        AR.persist()

        DMA("pool", ident_b, ident[:, :], [], ["ident_b"])
        OP("dve", [], ["ones_b"], lambda e: e.memset(ones_b, 1.0))
        OP("dve", [], ["ones_f"], lambda e: e.memset(ones_f, 1.0))
        DMA("sp", gin_bc, g_in[0:1, :].partition_broadcast(128), [], ["gin_bc"])
        DMA("sp", gfin_bc, g_fin[0:1, :].partition_broadcast(128), [], ["gfin_bc"])
        DMA("sp", qn_bc, qn[0:1, :].partition_broadcast(128), [], ["qn_bc"])
        DMA("sp", kn_bc, kn[0:1, :].partition_broadcast(128), [], ["kn_bc"])
        DMA("sp", chp_t, chp[:, :, :], [], ["chp"])
        DMA("sp", bm_t, bm[:, :], [], ["bm"])
        DMA("sp", msk_t, msk[:, :], [], ["msk"])
        for gi in range(4):
            DMA("pool", wg[:, gi, :, :], w_gate[gi].rearrange("b i o -> i b o"), [], ["wg"])
        for d_ in range(2):
            OP("dve", ["chp"], ["lam_c"], lambda e, d_=d_: e.tensor_copy(out=lam_c[:, d_ * 8:(d_ + 1) * 8], in_=chp_t[:, :, 9 + d_]))
            OP("dve", ["chp"], ["brh"], lambda e, d_=d_: e.tensor_scalar(out=brh[:, d_ * 8:(d_ + 1) * 8], in0=chp_t[:, :, 5 + d_], scalar1=0.5, scalar2=None, op0=ALU.mult))
            OP("dve", ["chp"], ["bih"], lambda e, d_=d_: e.tensor_scalar(out=bih[:, d_ * 8:(d_ + 1) * 8], in0=chp_t[:, :, 7 + d_], scalar1=0.5, scalar2=None, op0=ALU.mult))
        OP("dve", ["bm"], ["bmh"], lambda e: e.tensor_scalar(out=bmh_t, in0=bm_t, scalar1=0.5, scalar2=None, op0=ALU.mult))
        OP("act", ["lam_c"], ["tmpa"], lambda e: e.activation(out=tmpa, in_=lam_c, func=AF.Exp, scale=-1.0))
        OP("dve", ["tmpa"], ["tmpb"], lambda e: e.tensor_scalar(out=tmpb, in0=tmpa, scalar1=1.0, scalar2=None, op0=ALU.add))
        OP("act", ["tmpb"], ["tmpc"], lambda e: e.activation(out=tmpc, in_=tmpb, func=AF.Ln))
        OP("dve", ["tmpb"], ["tmpb"], lambda e: e.tensor_scalar(out=tmpb, in0=tmpb, scalar1=-1.0, scalar2=1e-30, op0=ALU.add, op1=ALU.max))
        OP("dve", ["tmpb"], ["tmpb"], lambda e: e.reciprocal(out=tmpb, in_=tmpb))
        OP("dve", ["tmpa", "tmpc"], ["tmpc"], lambda e: e.tensor_tensor(out=tmpc, in0=tmpc, in1=tmpa, op=ALU.mult))
        OP("dve", ["tmpb", "tmpc"], ["tmpc"], lambda e: e.tensor_tensor(out=tmpc, in0=tmpc, in1=tmpb, op=ALU.mult))
        OP("dve", ["tmpc"], ["cl_h"], lambda e: e.tensor_scalar(out=cl_h, in0=tmpc, scalar1=-4.0, scalar2=None, op0=ALU.mult))
        OP("dve", ["tmpc"], ["cl_1"], lambda e: e.tensor_scalar(out=cl_1, in0=tmpc, scalar1=-8.0, scalar2=None, op0=ALU.mult))
        OP("act", ["qn_bc"], ["absq"], lambda e: e.activation(out=absq, in_=qn_bc, func=AF.Abs))
        OP("dve", ["absq"], ["mq"], lambda e: e.tensor_reduce(out=mq, in_=absq, axis=AX.X, op=ALU.max))
        OP("act", ["kn_bc", "mq"], ["absq"], lambda e: e.activation(out=absq, in_=kn_bc, func=AF.Abs))
        OP("dve", ["absq"], ["mk"], lambda e: e.tensor_reduce(out=mk, in_=absq, axis=AX.X, op=ALU.max))
        OP("dve", ["mq", "mk"], ["negB"], lambda e: e.tensor_tensor(out=negB, in0=mq, in1=mk, op=ALU.mult))
        OP("dve", ["negB"], ["negB"], lambda e: e.tensor_scalar(out=negB, in0=negB, scalar1=-math.sqrt(HD), scalar2=None, op0=ALU.mult))
        PG.barrier()

        def wload(dst, src_cols):
            DMA("pool", dst, src_cols, [], ["W"])

        def hT_stageA(bufs, x_ap, t0, tiles):
            for i in tiles:
                sl = i % bufs["depth"]
                xt, ssq, xn, junk = bufs["xt"][sl], bufs["ssq"][sl], bufs["xn"][sl], bufs["junk"]
                xtr = "xt%d" % sl
                DMA("sp", xt, x_ap[t0 + i * 128:t0 + (i + 1) * 128, :], [], [xtr])
                OP("act", [xtr], ["junk", "ssq%d" % sl], lambda e, xt=xt, ssq=ssq: e.activation(out=junk, in_=xt, func=AF.Square, accum_out=ssq))
                OP("act", ["ssq%d" % sl], ["ssq%d" % sl], lambda e, ssq=ssq: e.activation(out=ssq, in_=ssq, func=AF.Sqrt, scale=1.0 / D, bias=EPS))
                OP("dve", ["ssq%d" % sl], ["ssq%d" % sl], lambda e, ssq=ssq: e.reciprocal(out=ssq, in_=ssq))
                OP("dve", [xtr, "ssq%d" % sl], ["xn%d" % sl], lambda e, xt=xt, ssq=ssq, xn=xn: e.scalar_tensor_tensor(out=xn, in0=xt, scalar=ssq[:, 0:1], in1=gin_bc, op0=ALU.mult, op1=ALU.mult))

        def hT_stageB(bufs, tiles, hT, hT_res, banks=None):
            for i in tiles:
                sl = i % bufs["depth"]
                xn = bufs["xn"][sl]
                if banks is None:
                    b = next_bank()
                else:
                    b = banks[i % len(banks)]
                for c in range(DC):
                    OP("pe", ["xn%d" % sl], ["pb%d" % b], lambda e, c=c, b=b, xn=xn: e.transpose(out=bankb(b)[:, c * 128:(c + 1) * 128], in_=xn[:, c * 128:(c + 1) * 128], identity=ident_b))
                OP("act", ["pb%d" % b], [hT_res], lambda e, b=b, i=i: e.activation(out=hT[:, :, i * 128:(i + 1) * 128], in_=bankb(b).rearrange("p (c t) -> p c t", c=DC), func=AF.Copy))

        def make_hT(bufs, x_ap, t0, nt, hT, hT_res, banks=None):
            dp = bufs["depth"]
            for i0 in range(0, nt, dp):
                tiles = list(range(i0, min(i0 + dp, nt)))
                hT_stageA(bufs, x_ap, t0, tiles)
                hT_stageB(bufs, tiles, hT, hT_res, banks)

        def norm_rope(tb, src_ps, nt, nh, w_bc, cs_t, cs_res, src_res, out_b, out_res):
            nu = nt * nh
            n = nu * HD
            s = tb["sl"] % tb["nbuf"]
            tb["sl"] += 1
            qf, sq, ssq, qn_, t1, t2, t3, t4 = (tb[k][s] for k in ("qf", "sq", "ssq", "qn", "t1", "t2", "t3", "t4"))
            R = lambda k: "%s%d" % (k, s)
            OP("act", src_res, [R("qf")], lambda e: e.activation(out=qf[:, 0:n].rearrange("p (t x) -> p t x", t=nt), in_=src_ps, func=AF.Copy))
            OP("act", [R("qf")], [R("sq")], lambda e: e.activation(out=sq[:, 0:n], in_=qf[:, 0:n], func=AF.Square))
            OP("dve", [R("sq")], [R("qssq")], lambda e: e.tensor_reduce(out=ssq[:, 0:nu], in_=sq[:, 0:n].rearrange("p (h d) -> p h d", h=nu), axis=AX.X, op=ALU.add))
            OP("act", [R("qssq")], [R("qssq")], lambda e: e.activation(out=ssq[:, 0:nu], in_=ssq[:, 0:nu], func=AF.Sqrt, scale=1.0 / HD, bias=EPS))
            OP("dve", [R("qssq")], [R("qssq")], lambda e: e.reciprocal(out=ssq[:, 0:nu], in_=ssq[:, 0:nu]))
            for h in range(nu):
                OP("dve", [R("qf"), R("qssq")], [R("qn")], lambda e, h=h: e.scalar_tensor_tensor(out=qn_[:, h * HD:(h + 1) * HD], in0=qf[:, h * HD:(h + 1) * HD], scalar=ssq[:, h:h + 1], in1=w_bc, op0=ALU.mult, op1=ALU.mult))
            q4 = qn_[:, 0:n].rearrange("p (t h d) -> p t h d", t=nt, h=nh)
            x0, x1 = q4[:, :, :, 0:64], q4[:, :, :, 64:128]
            cosb = cs_t[:, :, 0:64].unsqueeze(2).to_broadcast([128, nt, nh, 64])
            sinb = cs_t[:, :, 64:128].unsqueeze(2).to_broadcast([128, nt, nh, 64])
            o4 = out_b[:, 0:n].rearrange("p (t h d) -> p t h d", t=nt, h=nh)

            def v4(t):
                return t[:, 0:nu * 64].rearrange("p (t h d) -> p t h d", t=nt, h=nh)
            OP("dve", [R("qn"), cs_res], [R("t1")], lambda e: e.tensor_tensor(out=v4(t1), in0=x0, in1=cosb, op=ALU.mult))
            OP("dve", [R("qn"), cs_res], [R("t2")], lambda e: e.tensor_tensor(out=v4(t2), in0=x1, in1=sinb, op=ALU.mult))
            OP("dve", [R("t1"), R("t2")], [out_res], lambda e: e.tensor_tensor(out=o4[:, :, :, 0:64], in0=v4(t1), in1=v4(t2), op=ALU.subtract))
            OP("dve", [R("qn"), cs_res], [R("t3")], lambda e: e.tensor_tensor(out=v4(t3), in0=x0, in1=sinb, op=ALU.mult))
            OP("dve", [R("qn"), cs_res], [R("t4")], lambda e: e.tensor_tensor(out=v4(t4), in0=x1, in1=cosb, op=ALU.mult))
            OP("dve", [R("t3"), R("t4")], [out_res], lambda e: e.tensor_tensor(out=o4[:, :, :, 64:128], in0=v4(t3), in1=v4(t4), op=ALU.add))

        def alloc_x_bufs(depth=2):
            return dict(depth=depth, xt=[AR.alloc([D], F32) for _ in range(depth)], ssq=[AR.alloc([1], F32) for _ in range(depth)],
                        xn=[AR.alloc([D], BF16) for _ in range(depth)], junk=AR.alloc([D], BF16))

        def alloc_rope_bufs(nbuf=2):
            n = 8 * HD
            d = dict(sl=0)
            for k, sz in (("qf", n), ("sq", n), ("ssq", 8), ("qn", n), ("t1", n // 2), ("t2", n // 2), ("t3", n // 2), ("t4", n // 2)):
                d[k] = [AR.alloc([sz], F32) for _ in range(nbuf)]
                if nbuf == 1:
                    d[k] = d[k] * 2
            if nbuf == 1:
                d["sl"] = 0
            d["nbuf"] = nbuf
            return d

        def phase_A1(x_all, S, cs_all):
            AR.reset()
            Wkv = AR.alloc([DC, 512], BF16)
            Wxl = AR.alloc([DC, 1024], BF16)
            wload(Wkv, w_in_v[:, :, OK_:OK_ + 512])
            wload(Wxl, w_in_v[:, :, OXL:OXL + 1024])
            xb = alloc_x_bufs(4)
            rb = alloc_rope_bufs(2)
            hTs = [AR.alloc([DC, CH], BF16) for _ in range(2)]
            cst = [AR.alloc([4, HD], F32) for _ in range(2)]
            kb = [AR.alloc([4 * KVH * HD], BF16) for _ in range(2)]
            kTst = [AR.alloc([KVH, CH], BF16) for _ in range(2)]
            vst = [AR.alloc([KVH, 4, HD], BF16) for _ in range(2)]
            xlst = [AR.alloc([8, CH], F32) for _ in range(2)]
            nchunk = S // CH
            T4 = [0, 1, 2, 3]
            HB = (2, 3, 4, 5)
            hT_stageA(xb, x_all, 0, T4)
            hT_stageB(xb, T4, hTs[0], "hT0", banks=HB)
            for j in range(nchunk):
                s2 = j % 2
                t0 = j * CH
                hT = hTs[s2]
                hres = "hT%d" % s2
                DMA("sp", cst[s2], cs_all[t0:t0 + CH, :].rearrange("(i p) d -> p i d", p=128), [], ["cs%d" % s2])
                if j + 1 < nchunk:
                    hT_stageA(xb, x_all, t0 + CH, T4)
                for i in range(4):
                    b = 2 + i
                    for c in range(DC):
                        OP("pe", [hres, "W"], ["pb%d" % b], lambda e, c=c, b=b, i=i, hT=hT: e.matmul(bank(b), lhsT=hT[:, c, i * 128:(i + 1) * 128], rhs=Wkv[:, c, :], start=(c == 0), stop=(c == DC - 1)))
                kvr = ["pb%d" % (2 + i) for i in range(4)]
                kv4 = ps[:, 2 * 512:6 * 512].rearrange("p (i x) -> p i x", i=4)
                OP("act", kvr, ["vst%d" % s2], lambda e, s2=s2, kv4=kv4: e.activation(out=vst[s2].rearrange("p g i d -> p i g d"), in_=kv4[:, :, 256:512].rearrange("p i (g d) -> p i g d", g=KVH), func=AF.Copy))
                norm_rope(rb, kv4[:, :, 0:256], 4, KVH, kn_bc, cst[s2], "cs%d" % s2, kvr, kb[s2], "kb%d" % s2)
                for eb in range(8):
                    b = 7 if eb % 2 == 0 else 6
                    for c in range(DC):
                        OP("pe", [hres, "W"], ["pb%d" % b], lambda e, c=c, b=b, eb=eb, hT=hT: e.matmul(bank(b), lhsT=Wxl[:, c, eb * 128:(eb + 1) * 128], rhs=hT[:, c, :], start=(c == 0), stop=(c == DC - 1)))
                    if eb % 2 == 0:
                        OP("act", ["pb%d" % b], ["xlst%d" % s2], lambda e, b=b, eb=eb, s2=s2: e.activation(out=xlst[s2][:, eb, :], in_=bank(b), func=AF.Copy))
                    else:
                        OP("dve", ["pb%d" % b], ["xlst%d" % s2], lambda e, b=b, eb=eb, s2=s2: e.tensor_copy(out=xlst[s2][:, eb, :], in_=bank(b)))
                DMA("sp", xl_scr[:, :, t0:t0 + CH], xlst[s2], ["xlst%d" % s2], [])
                kbk = j % 2
                for u in range(4 * KVH):
                    OP("pe", ["kb%d" % s2], ["pb%d" % kbk], lambda e, u=u, s2=s2, kbk=kbk: e.transpose(out=bankb(kbk)[:, u * 128:(u + 1) * 128], in_=kb[s2][:, u * HD:(u + 1) * HD], identity=ident_b))
                OP("dve", ["pb%d" % kbk], ["kTst%d" % s2], lambda e, s2=s2, kbk=kbk: e.tensor_copy(out=kTst[s2].rearrange("p g (i t) -> p i g t", i=4), in_=bankb(kbk).rearrange("p (i g t) -> p i g t", i=4, g=KVH)))
                for g in range(KVH):
                    DMA("sp", kT_scr[g, :, t0:t0 + CH], kTst[s2][:, g, :], ["kTst%d" % s2], [])
                    DMA("sp", v_scr[g, :, j * 4:(j + 1) * 4, :], vst[s2][:, g, :, :], ["vst%d" % s2], [])
                if j + 1 < nchunk:
                    hT_stageB(xb, T4, hTs[1 - s2], "hT%d" % (1 - s2), banks=HB)
            PG.barrier()

        def phase_LRU(S, direction, own_acc):
            AR.reset()
            pipe = not own_acc
            nb = 2 if pipe else 1
            ng = 2 if pipe else 1
            BG = 4
            xlh = [AR.alloc([8, CH + 3], F32) for _ in range(nb)]
            hst = [AR.alloc([8, CH], F32) for _ in range(2)]
            hfl = [AR.alloc([8, CH], F32) for _ in range(nb)] if direction == 1 else None
            acc = AR.alloc([8, OWN], F32) if own_acc else None
            xc = [AR.alloc([CH], F32) for _ in range(BG * ng)]
            xcb = [AR.alloc([CH], BF16) for _ in range(BG * ng)]
            thr = [AR.alloc([CH], F32) for _ in range(BG)]
            thi = [AR.alloc([CH], F32) for _ in range(BG)]
            a_t = [AR.alloc([BG, CH], F32) for _ in range(ng)]
            s_t = [AR.alloc([BG, CH], F32) for _ in range(ng)]
            w_t = [AR.alloc([BG, CH], F32) for _ in range(ng)]
            if own_acc:
                OP("dve", [], ["acc"], lambda e: e.memset(acc, 0.0))
            nchunk = S // CH
            order = list(range(nchunk)) if direction == 0 else list(range(nchunk - 1, -1, -1))

            def load(k):
                j = order[k]
                sl = k % nb
                t0 = j * CH
                lo = max(t0 - 2, 0)
                hi = min(t0 + CH + 1, S)
                if t0 == 0:
                    OP("dve", [], ["xlh%d" % sl], lambda e, sl=sl: e.memset(xlh[sl][:, :, 0:2], 0.0))
                if t0 + CH == S:
                    OP("dve", [], ["xlh%d" % sl], lambda e, sl=sl: e.memset(xlh[sl][:, :, CH + 2:CH + 3], 0.0))
                DMA("sp", xlh[sl][:, :, lo - (t0 - 2):hi - (t0 - 2)], xl_scr[:, :, lo:hi], [], ["xlh%d" % sl])
                if direction == 1:
                    DMA("sp", hfl[sl], hf_scr[:, :, t0:t0 + CH], [], ["hfl%d" % sl])

            items = [(k, g0) for k in range(nchunk) for g0 in range(0, 8, BG)]

            def S1(n):
                k, g0 = items[n]
                sl = k % nb
                for bi in range(BG):
                    eb = g0 + bi
                    xi = (n % ng) * BG + bi
                    cw = chp_t[:, eb, :]
                    src = xlh[sl][:, eb, :]
                    OP("act", ["xlh%d" % sl, "chp"], ["xc%d" % xi], lambda e, xi=xi, src=src, cw=cw: e.activation(out=xc[xi], in_=src[:, 0:CH], func=AF.Identity, scale=cw[:, 0:1], bias=cw[:, 4:5]))
                    for tap in range(1, 4):
                        OP("dve", ["xlh%d" % sl, "xc%d" % xi], ["xc%d" % xi], lambda e, xi=xi, src=src, cw=cw, tap=tap: e.scalar_tensor_tensor(out=xc[xi], in0=src[:, tap:tap + CH], scalar=cw[:, tap:tap + 1], in1=xc[xi], op0=ALU.mult, op1=ALU.add))

            def S2(n):
                k, g0 = items[n]
                for bi in range(BG):
                    eb = g0 + bi
                    xi = (n % ng) * BG + bi
                    OP("pool", ["xc%d" % xi], ["xcb%d" % xi], lambda e, xi=xi: e.tensor_copy(out=xcb[xi], in_=xc[xi]))
                    OP("pe", ["xcb%d" % xi, "wg"], ["pb%d" % (2 * bi)], lambda e, bi=bi, eb=eb, xi=xi: e.matmul(bank(2 * bi), lhsT=wg[:, 0 + direction, eb, :], rhs=xcb[xi], start=True, stop=True))
                    OP("pe", ["xcb%d" % xi, "wg"], ["pb%d" % (2 * bi + 1)], lambda e, bi=bi, eb=eb, xi=xi: e.matmul(bank(2 * bi + 1), lhsT=wg[:, 2 + direction, eb, :], rhs=xcb[xi], start=True, stop=True))

            def S3a(n):
                k, g0 = items[n]
                gs = n % ng
                for bi in range(BG):
                    eb = g0 + bi
                    ci = direction * 8 + eb
                    OP("act", ["pb%d" % (2 * bi)], ["thr%d" % bi], lambda e, bi=bi, ci=ci: e.activation(out=thr[bi], in_=bank(2 * bi), func=AF.Tanh, scale=0.5, bias=brh[:, ci:ci + 1]))
                    OP("act", ["pb%d" % (2 * bi + 1)], ["thi%d" % bi], lambda e, bi=bi, ci=ci: e.activation(out=thi[bi], in_=bank(2 * bi + 1), func=AF.Tanh, scale=0.5, bias=bih[:, ci:ci + 1]))
                    OP("act", ["thr%d" % bi], ["a%d" % gs], lambda e, bi=bi, gs=gs, ci=ci: e.activation(out=a_t[gs][:, bi, :], in_=thr[bi], func=AF.Exp, scale=cl_h[:, ci:ci + 1], bias=cl_h[:, ci:ci + 1]))
                    OP("act", ["thr%d" % bi], ["s%d" % gs], lambda e, bi=bi, gs=gs, ci=ci: e.activation(out=s_t[gs][:, bi, :], in_=thr[bi], func=AF.Exp, scale=cl_1[:, ci:ci + 1], bias=cl_1[:, ci:ci + 1]))
                OP("act", ["s%d" % gs], ["s%d" % gs], lambda e, gs=gs: e.activation(out=s_t[gs], in_=s_t[gs], func=AF.Sqrt, scale=-1.0, bias=1.0))

            def S3b(n):
                k, g0 = items[n]
                gs = n % ng
                hs = k % 2
                sl = k % nb
                j = order[k]
                t0 = j * CH
                for bi in range(BG):
                    xi = (n % ng) * BG + bi
                    OP("dve", ["thi%d" % bi, "xc%d" % xi], ["w%d" % gs], lambda e, bi=bi, gs=gs, xi=xi: e.scalar_tensor_tensor(out=w_t[gs][:, bi, :], in0=thi[bi], scalar=1.0, in1=xc[xi], op0=ALU.add, op1=ALU.mult))
                OP("dve", ["s%d" % gs, "w%d" % gs], ["w%d" % gs], lambda e, gs=gs: e.scalar_tensor_tensor(out=w_t[gs], in0=s_t[gs], scalar=0.5, in1=w_t[gs], op0=ALU.mult, op1=ALU.mult))
                for bi in range(BG):
                    eb = g0 + bi
                    if k == 0:
                        init = 0.0
                        rd = []
                    else:
                        pcol = CH - 1 if direction == 0 else 0
                        init = hst[1 - hs][:, eb, pcol:pcol + 1]
                        rd = ["hst%d" % (1 - hs)]
                    if direction == 0:
                        OP("dve", ["a%d" % gs, "w%d" % gs] + rd, ["hst%d" % hs], lambda e, gs=gs, bi=bi, eb=eb, hs=hs, init=init: e.tensor_tensor_scan(out=hst[hs][:, eb, :], data0=a_t[gs][:, bi, :], data1=w_t[gs][:, bi, :], initial=init, op0=ALU.mult, op1=ALU.add))
                    else:
                        OP("dve", ["a%d" % gs, "w%d" % gs] + rd, ["hst%d" % hs], lambda e, gs=gs, bi=bi, eb=eb, hs=hs, init=init: e.tensor_tensor_scan(out=hst[hs][:, eb, ::-1], data0=a_t[gs][:, bi, ::-1], data1=w_t[gs][:, bi, ::-1], initial=init, op0=ALU.mult, op1=ALU.add))
                if g0 + BG == 8:
                    if direction == 0:
                        DMA("sp", hf_scr[:, :, t0:t0 + CH], hst[hs], ["hst%d" % hs], [])
                    else:
                        OP("dve", ["hst%d" % hs, "hfl%d" % sl], ["hfl%d" % sl], lambda e, hs=hs, sl=sl: e.tensor_tensor(out=hfl[sl], in0=hfl[sl], in1=hst[hs], op=ALU.add))
                        if own_acc:
                            slot = j % NSLOT
                            grp = j // NSLOT
                            OP("dve", ["hfl%d" % sl, "acc", "msk"], ["acc"], lambda e, sl=sl, slot=slot, grp=grp: e.scalar_tensor_tensor(out=acc[:, :, slot * CH:(slot + 1) * CH], in0=hfl[sl], scalar=msk_t[:, grp:grp + 1], in1=acc[:, :, slot * CH:(slot + 1) * CH], op0=ALU.mult, op1=ALU.add))
                        else:
                            DMA("sp", ls_scr[:, :, t0:t0 + CH], hfl[sl], ["hfl%d" % sl], [])

            load(0)
            if pipe:
                S1(0)
                S2(0)
                for n in range(len(items)):
                    k, g0 = items[n]
                    if g0 == 0 and k + 1 < nchunk:
                        load(k + 1)
                    if n + 1 < len(items):
                        S1(n + 1)
                    S3a(n)
                    if n + 1 < len(items):
                        S2(n + 1)
                    S3b(n)
            else:
                for n in range(len(items)):
                    k, g0 = items[n]
                    S1(n)
                    S2(n)
                    S3a(n)
                    S3b(n)
                    if g0 + BG == 8 and k + 1 < nchunk:
                        load(k + 1)
            if own_acc:
                DMA("sp", ls_scr[:, :, 0:OWN], acc, ["acc"], [])
            PG.barrier()

        def phase_B0(x_own, N_OWN, cs_own):
            AR.reset()
            Wq = AR.alloc([DC, 1024], BF16)
            wload(Wq, w_in_v[:, :, OQ:OQ + 1024])
            xb = alloc_x_bufs(4)
            rb = alloc_rope_bufs(2)
            hTs = [AR.alloc([DC, CH], BF16) for _ in range(2)]
            cst = [AR.alloc([4, HD], F32) for _ in range(2)]
            qb_ = [AR.alloc([H * HD], BF16) for _ in range(2)]
            qTst = [AR.alloc([H, CH], BF16) for _ in range(2)]
            nchunk = N_OWN // CH
            T4 = [0, 1, 2, 3]
            HB = (2, 3, 4, 5)
            hT_stageA(xb, x_own, 0, T4)
            hT_stageB(xb, T4, hTs[0], "hT0", banks=HB)
            tctr = [0]
            for j in range(nchunk):
                s2 = j % 2
                t0 = j * CH
                hT = hTs[s2]
                hres = "hT%d" % s2
                DMA("sp", cst[s2], cs_own[t0:t0 + CH, :].rearrange("(i p) d -> p i d", p=128), [], ["cs%d" % s2])
                if j + 1 < nchunk:
                    hT_stageA(xb, x_own, t0 + CH, T4)
                DMA("sp", hT_scr[:, :, t0:t0 + CH], hT, [hres], [])
                info = {}

                def proj(i, hT=hT, hres=hres, s2=s2):
                    ts = tctr[0] % 2
                    tctr[0] += 1
                    b0 = 0 if ts == 0 else 6
                    info[i] = ts
                    for hf in range(2):
                        for c in range(DC):
                            OP("pe", [hres, "W"], ["pb%d" % (b0 + hf)], lambda e, c=c, b0=b0, hf=hf, i=i: e.matmul(bank(b0 + hf), lhsT=hT[:, c, i * 128:(i + 1) * 128], rhs=Wq[:, c, hf * 512:(hf + 1) * 512], start=(c == 0), stop=(c == DC - 1)))
                    norm_rope(rb, ps[:, b0 * 512:b0 * 512 + 1024].rearrange("p (t x) -> p t x", t=1), 1, H, qn_bc, cst[s2][:, i:i + 1, :], "cs%d" % s2, ["pb%d" % b0, "pb%d" % (b0 + 1)], qb_[ts], "qb%d" % ts)

                def trans(i, s2=s2):
                    ts = info[i]
                    b2 = 2 + i
                    for h in range(H):
                        OP("pe", ["qb%d" % ts], ["pb%d" % b2], lambda e, h=h, b2=b2, ts=ts: e.transpose(out=bankb(b2)[:, h * 128:(h + 1) * 128], in_=qb_[ts][:, h * HD:(h + 1) * HD], identity=ident_b))
                    OP("dve", ["pb%d" % b2], ["qTst%d" % s2], lambda e, b2=b2, i=i, s2=s2: e.tensor_copy(out=qTst[s2][:, :, i * 128:(i + 1) * 128], in_=bankb(b2).rearrange("p (h t) -> p h t", h=H)))

                proj(0)
                for i in range(4):
                    if i + 1 < 4:
                        proj(i + 1)
                    trans(i)
                DMA("sp", qT_scr[:, :, t0:t0 + CH], qTst[s2], ["qTst%d" % s2], [])
                if j + 1 < nchunk:
                    hT_stageB(xb, T4, hTs[1 - s2], "hT%d" % (1 - s2), banks=HB)
            PG.barrier()

        def phase_B1(N_OWN, S):
            AR.reset()
            QB = 1024 if N_OWN % 1024 == 0 else 512
            QH = QB // 512
            NKT = S // 128
            big = S > 8192
            Wga = AR.alloc([DC, 1024], BF16)
            wload(Wga, w_in_v[:, :, OGA:OGA + 1024])
            resident = not big
            if resident:
                KT2 = AR.alloc([KVH, S], BF16)
                VV2 = AR.alloc([KVH, NKT, HD], BF16)
                for g in range(KVH):
                    DMA("sp", KT2[:, g, :], kT_scr[g, :, 0:S], [], ["KT"])
                    DMA("sp", VV2[:, g, :, :], v_scr[g, :, 0:NKT, :], [], ["VV"])
            else:
                KT1 = AR.alloc([S], BF16)
                VV1 = AR.alloc([NKT, HD], BF16)
            hTb = [AR.alloc([DC, QB], BF16) for _ in range(2)]
            qTb = [AR.alloc([H, QB], BF16) for _ in range(2)]
            pT = [AR.alloc([QB], BF16) for _ in range(3)]
            rs = AR.alloc([QB], F32)
            sg = [AR.alloc([QB], F32) for _ in range(2)]
            aob = [AR.alloc([QB], BF16) for _ in range(2)]
            XP = 384 if QB == 1024 else 192
            XD = QB - XP
            cnt = [0]
            hctr = 0
            nqb = N_OWN // QB

            def loadq(qb):
                s2 = qb % 2
                DMA("sp", hTb[s2], hT_scr[:, :, qb * QB:(qb + 1) * QB], [], ["hTb%d" % s2])
                DMA("sp", qTb[s2], qT_scr[:, :, qb * QB:(qb + 1) * QB], [], ["qTb%d" % s2])

            loadq(0)
            for qb in range(nqb):
                q0 = qb * QB
                s2 = qb % 2
                hT, qT = hTb[s2], qTb[s2]
                hres, qres = "hTb%d" % s2, "qTb%d" % s2
                if qb + 1 < nqb:
                    loadq(qb + 1)
                for g in range(KVH):
                    if resident:
                        KT = KT2[:, g, :]
                        VV = VV2[:, g, :, :]
                    else:
                        KT, VV = KT1, VV1
                        DMA("sp", KT, kT_scr[g, :, 0:S], [], ["KT"])
                        DMA("sp", VV, v_scr[g, :, 0:NKT, :], [], ["VV"])
                    for hh in range(H // KVH):
                        h = g * (H // KVH) + hh
                        hsg = hctr % 2
                        hctr += 1
                        for hf in range(QH):
                            for c in range(DC):
                                OP("pe", [hres, "W"], ["pb%d" % hf], lambda e, c=c, hf=hf, h=h, hT=hT: e.matmul(bank(hf), lhsT=Wga[:, c, h * 128:(h + 1) * 128], rhs=hT[:, c, hf * 512:(hf + 1) * 512], start=(c == 0), stop=(c == DC - 1)))
                        OP("act", ["pb%d" % hf for hf in range(QH)], ["sg%d" % hsg], lambda e, hsg=hsg: e.activation(out=sg[hsg], in_=ps[:, 0:QB], func=AF.Silu))
                        slots = []

                        def emit_qk(kt, h=h, KT=KT, qT=qT, qres=qres):
                            sp_ = (cnt[0] % 2) * 2
                            pt = cnt[0] % 3
                            cnt[0] += 1
                            slots.append((sp_, pt))
                            for hf in range(QH):
                                OP("pe", ["KT", qres], ["pb%d" % (sp_ + hf)], lambda e, kt=kt, sp_=sp_, hf=hf: e.matmul(bank(sp_ + hf), lhsT=KT[:, kt * 128:(kt + 1) * 128], rhs=qT[:, h, hf * 512:(hf + 1) * 512], start=True, stop=True))

                        emit_qk(0)
                        for kt in range(NKT):
                            if kt + 1 < NKT:
                                emit_qk(kt + 1)
                            sp_, pt = slots[kt]
                            OP("act", ["pb%d" % (sp_ + hf) for hf in range(QH)] + ["negB"], ["pT%d" % pt], lambda e, sp_=sp_, pt=pt: e.activation(out=pT[pt], in_=ps[:, sp_ * 512:sp_ * 512 + QB], func=AF.Exp, scale=SM_SCALE, bias=negB[:, 0:1]))
                            for hf in range(QH):
                                OP("pe", ["pT%d" % pt, "VV"], ["pb%d" % (4 + hf)], lambda e, kt=kt, pt=pt, hf=hf, VV=VV: e.matmul(bank(4 + hf), lhsT=VV[:, kt, :], rhs=pT[pt][:, hf * 512:(hf + 1) * 512], start=(kt == 0), stop=(kt == NKT - 1)))
                            OP("pe", ["pT%d" % pt, "ones_b"], ["pb%d" % (6 + QH - 1)], lambda e, kt=kt, pt=pt: e.matmul(ps[:, 6 * 512 + XD:6 * 512 + QB], lhsT=ones_b, rhs=pT[pt][:, XD:QB], start=(kt == 0), stop=(kt == NKT - 1)))
                            if kt == 0:
                                OP("dve", ["pT%d" % pt], ["rs"], lambda e, pt=pt: e.tensor_copy(out=rs[:, 0:XD], in_=pT[pt][:, 0:XD]))
                            else:
                                OP("dve", ["pT%d" % pt, "rs"], ["rs"], lambda e, pt=pt: e.tensor_tensor(out=rs[:, 0:XD], in0=rs[:, 0:XD], in1=pT[pt][:, 0:XD], op=ALU.add))
                        accr = ["pb%d" % (4 + hf) for hf in range(QH)]
                        sumr = ["pb%d" % (6 + hf) for hf in range(QH)]
                        for hf in range(QH):
                            c0, c1 = hf * 512, min((hf + 1) * 512, XD)
                            if c1 > c0:
                                OP("pe", ["rs", "ones_f"], ["pb%d" % (6 + hf)], lambda e, c0=c0, c1=c1: e.matmul(ps[:, 6 * 512 + c0:6 * 512 + c1], lhsT=ones_f, rhs=rs[:, c0:c1], start=True, stop=True))
                        OP("dve", sumr, ["rs"], lambda e: e.reciprocal(out=rs, in_=ps[:, 6 * 512:6 * 512 + QB]))
                        OP("dve", accr + ["rs"], ["rs"], lambda e: e.tensor_tensor(out=rs, in0=ps[:, 4 * 512:4 * 512 + QB], in1=rs, op=ALU.mult))
                        OP("dve", ["rs", "sg%d" % hsg], ["aob%d" % hsg], lambda e, hsg=hsg: e.tensor_tensor(out=aob[hsg], in0=rs, in1=sg[hsg], op=ALU.mult))
                        DMA("sp", ao_scr[:, h, q0:q0 + QB], aob[hsg], ["aob%d" % hsg], [])
            PG.barrier()

        def phase_B2(x_own, N_OWN, y_out):
            AR.reset()
            Wgl = AR.alloc([DC, 1024], BF16)
            Wml = AR.alloc([DC, 2048], BF16)
            WbA = AR.alloc([DC, 1024], BF16)
            WbL = AR.alloc([DC, 1024], BF16)
            Wo = AR.alloc([DC, 1024], BF16)
            wload(Wgl, w_in_v[:, :, OGL:OGL + 1024])
            wload(Wml[:, :, 0:1024], w_in_v[:, :, OML:OML + 1024])
            wload(Wml[:, :, 1024:2048], w_in_v[:, :, OML + 1024:OML + 2048])
            wload(WbA, w_br[0].rearrange("(c p) n -> p c n", p=128))
            wload(WbL, w_br[1].rearrange("(c p) n -> p c n", p=128))
            wload(Wo, w_out.rearrange("(c p) n -> p c n", p=128))
            hTs2 = [AR.alloc([DC, CH], BF16) for _ in range(2)]
            junkb = AR.alloc([D], BF16)
            lsl = [AR.alloc([CH], F32) for _ in range(2)]
            aol = AR.alloc([8, CH], BF16)
            loT = AR.alloc([8, CH], BF16)
            mT = AR.alloc([8, CH], BF16)
            sgl = [AR.alloc([CH], F32) for _ in range(2)]
            ga = [AR.alloc([CH], F32) for _ in range(2)]
            gl = [AR.alloc([CH], F32) for _ in range(2)]
            t1 = [AR.alloc([CH], F32) for _ in range(2)]
            t2 = [AR.alloc([CH], F32) for _ in range(2)]
            yt = [AR.alloc([D], F32)]
            yo = [AR.alloc([D], F32)]
            yss = [AR.alloc([1], F32)]
            xres = [AR.alloc([D], F32)]
            junk2 = junkb
            nchunk = N_OWN // CH
            ectr = 0
            yctr = 0
            for j in range(nchunk):
                t0 = j * CH
                DMA("sp", aol, ao_scr[:, :, t0:t0 + CH], [], ["aol"])
                hT = hTs2[j % 2]
                hres = "hT%d" % (j % 2)
                if j == 0:
                    DMA("sp", hTs2[0], hT_scr[:, :, 0:CH], [], ["hT0"])
                if j + 1 < nchunk:
                    DMA("sp", hTs2[(j + 1) % 2], hT_scr[:, :, t0 + CH:t0 + 2 * CH], [], ["hT%d" % ((j + 1) % 2)])
                for eb in range(8):
                    es = ectr % 2
                    ectr += 1
                    DMA("sp", lsl[es], ls_scr[:, eb, t0:t0 + CH], [], ["lsl%d" % es])
                    b = next_bank()
                    for c in range(DC):
                        OP("pe", [hres, "W"], ["pb%d" % b], lambda e, c=c, b=b, eb=eb, hT=hT: e.matmul(bank(b), lhsT=Wgl[:, c, eb * 128:(eb + 1) * 128], rhs=hT[:, c, :], start=(c == 0), stop=(c == DC - 1)))
                    OP("act", ["pb%d" % b], ["sgl%d" % es], lambda e, b=b, es=es: e.activation(out=sgl[es], in_=bank(b), func=AF.Silu))
                    OP("dve", ["sgl%d" % es, "lsl%d" % es], ["loT"], lambda e, es=es, eb=eb: e.tensor_tensor(out=loT[:, eb, :], in0=lsl[es], in1=sgl[es], op=ALU.mult))
                for eb in range(8):
                    es = ectr % 2
                    ectr += 1
                    bA, bL, bga, bgl = next_bank(), next_bank(), next_bank(), next_bank()
                    for c in range(DC):
                        OP("pe", ["aol", "W"], ["pb%d" % bA], lambda e, c=c, bA=bA, eb=eb: e.matmul(bank(bA), lhsT=WbA[:, c, eb * 128:(eb + 1) * 128], rhs=aol[:, c, :], start=(c == 0), stop=(c == DC - 1)))
                    for c in range(DC):
                        OP("pe", ["loT", "W"], ["pb%d" % bL], lambda e, c=c, bL=bL, eb=eb: e.matmul(bank(bL), lhsT=WbL[:, c, eb * 128:(eb + 1) * 128], rhs=loT[:, c, :], start=(c == 0), stop=(c == DC - 1)))
                    for c in range(DC):
                        OP("pe", [hres, "W"], ["pb%d" % bga], lambda e, c=c, bga=bga, eb=eb, hT=hT: e.matmul(bank(bga), lhsT=Wml[:, c, eb * 128:(eb + 1) * 128], rhs=hT[:, c, :], start=(c == 0), stop=(c == DC - 1)))
                    for c in range(DC):
                        OP("pe", [hres, "W"], ["pb%d" % bgl], lambda e, c=c, bgl=bgl, eb=eb, hT=hT: e.matmul(bank(bgl), lhsT=Wml[:, c, 1024 + eb * 128:1024 + (eb + 1) * 128], rhs=hT[:, c, :], start=(c == 0), stop=(c == DC - 1)))
                    OP("act", ["pb%d" % bga, "bm"], ["ga%d" % es], lambda e, bga=bga, es=es, eb=eb: e.activation(out=ga[es], in_=bank(bga), func=AF.Sigmoid, bias=bm_t[:, eb:eb + 1]))
                    OP("act", ["pb%d" % bgl, "bm"], ["gl%d" % es], lambda e, bgl=bgl, es=es, eb=eb: e.activation(out=gl[es], in_=bank(bgl), func=AF.Sigmoid, bias=bm_t[:, 8 + eb:9 + eb]))
                    OP("dve", ["pb%d" % bA, "ga%d" % es], ["t1%d" % es], lambda e, bA=bA, es=es: e.tensor_tensor(out=t1[es], in0=bank(bA), in1=ga[es], op=ALU.mult))
                    OP("dve", ["pb%d" % bL, "gl%d" % es], ["t2%d" % es], lambda e, bL=bL, es=es: e.tensor_tensor(out=t2[es], in0=bank(bL), in1=gl[es], op=ALU.mult))
                    OP("pool", ["t1%d" % es, "t2%d" % es], ["mT"], lambda e, es=es, eb=eb: e.tensor_tensor(out=mT[:, eb, :], in0=t1[es], in1=t2[es], op=ALU.add))
                for i in range(4):
                    ys_ = 0
                    b0 = 2 * (next_bank() % 4)
                    DMA("sp", xres[ys_], x_own[t0 + i * 128:t0 + (i + 1) * 128, :], [], ["xres%d" % ys_])
                    for hf in range(2):
                        for c in range(DC):
                            OP("pe", ["mT", "W"], ["pb%d" % (b0 + hf)], lambda e, c=c, b0=b0, hf=hf, i=i: e.matmul(bank(b0 + hf), lhsT=mT[:, c, i * 128:(i + 1) * 128], rhs=Wo[:, c, hf * 512:(hf + 1) * 512], start=(c == 0), stop=(c == DC - 1)))
                    OP("dve", ["pb%d" % b0, "pb%d" % (b0 + 1), "xres%d" % ys_], ["yt%d" % ys_], lambda e, b0=b0, ys_=ys_: e.tensor_tensor(out=yt[ys_], in0=ps[:, b0 * 512:b0 * 512 + 1024], in1=xres[ys_], op=ALU.add))
                    OP("act", ["yt%d" % ys_], ["junk2", "yss%d" % ys_], lambda e, ys_=ys_: e.activation(out=junk2, in_=yt[ys_], func=AF.Square, accum_out=yss[ys_]))
                    OP("act", ["yss%d" % ys_], ["yss%d" % ys_], lambda e, ys_=ys_: e.activation(out=yss[ys_], in_=yss[ys_], func=AF.Sqrt, scale=1.0 / D, bias=EPS))
                    OP("dve", ["yss%d" % ys_], ["yss%d" % ys_], lambda e, ys_=ys_: e.reciprocal(out=yss[ys_], in_=yss[ys_]))
                    OP("dve", ["yt%d" % ys_, "yss%d" % ys_], ["yo%d" % ys_], lambda e, ys_=ys_: e.scalar_tensor_tensor(out=yo[ys_], in0=yt[ys_], scalar=yss[ys_][:, 0:1], in1=gfin_bc, op0=ALU.mult, op1=ALU.mult))
                    DMA("sp", y_out[t0 + i * 128:t0 + (i + 1) * 128, :], yo[ys_], ["yo%d" % ys_], [])
            PG.barrier()

        for (x_all, S, cs_all, x_own, n_own, cs_own, y_out, is_p) in (
            (xs, S_S, cs_s, xs, S_S, cs_s, ys, False),
            (xp, S_P, cs_p, xq, OWN, cs_q, yq, True),
        ):
            phase_A1(x_all, S, cs_all)
            phase_LRU(S, 0, False)
            phase_LRU(S, 1, is_p)
            phase_B0(x_own, n_own, cs_own)
            phase_B1(n_own, S)
            phase_B2(x_own, n_own, y_out)
        PG.emit(nc)
    return nc


def _rope_table(S):
    grid_w = 64
    pos = np.arange(S)
    rows = (pos // grid_w).astype(np.float32)
    cols = (pos % grid_w).astype(np.float32)
    n = HD // 4
    inv = (np.float32(10000.0) ** (-(np.arange(n, dtype=np.float32) / np.float32(n)))).astype(np.float32)
    ang = np.concatenate([rows[:, None] * inv[None, :], cols[:, None] * inv[None, :]], axis=-1).astype(np.float32)
    return np.concatenate([np.cos(ang), np.sin(ang)], axis=-1).astype(np.float32)


_NC_CACHE = {}


def kernel(x_prompt, x_sample, norm_in, w_in, b_merge, q_norm, k_norm, conv_w, conv_b,
           w_rgate, b_rgate, w_igate, b_igate, lam, w_branch, w_out, norm_final):
    f = lambda a: np.ascontiguousarray(np.asarray(a, dtype=np.float32))
    x_prompt, x_sample = f(x_prompt), f(x_sample)
    S_P = x_prompt.shape[1]
    S_S = x_sample.shape[1]
    OWN = S_P // NCORES
    assert x_prompt.shape[0] == 1 and x_sample.shape[0] == NCORES
    perm = np.concatenate([np.arange(0, HD, 2), np.arange(1, HD, 2)])
    cols = np.arange(NPROJ)
    for hh in range(H):
        cols[OQ + hh * HD:OQ + (hh + 1) * HD] = OQ + hh * HD + perm
    for hh in range(KVH):
        cols[OK_ + hh * HD:OK_ + (hh + 1) * HD] = OK_ + hh * HD + perm
    w_in_p = f(f(w_in)[0][:, cols])
    qn = f(f(q_norm)[0][perm][None, :])
    kn = f(f(k_norm)[0][perm][None, :])
    plist = [f(conv_w)[0][0], f(conv_w)[0][1], f(conv_w)[0][2], f(conv_w)[0][3], f(conv_b)[0],
             f(b_rgate)[0][0], f(b_rgate)[0][1], f(b_igate)[0][0], f(b_igate)[0][1], f(lam)[0][0], f(lam)[0][1]]
    chp = f(np.stack(plist, axis=-1).reshape(8, 128, 11).transpose(1, 0, 2))
    w_gate = f(np.stack([f(w_rgate)[0][0], f(w_rgate)[0][1], f(w_igate)[0][0], f(w_igate)[0][1]], axis=0))
    bm = f(f(b_merge)[0].reshape(16, 128).T)
    cs_p = _rope_table(S_P)
    cs_s = _rope_table(S_S)
    ident = np.eye(128, dtype=np.float32)
    key = (S_S, S_P)
    if key not in _NC_CACHE:
        _NC_CACHE[key] = build_nc(S_S, S_P)
    nc = _NC_CACHE[key]
    xp2 = x_prompt[0]
    in_maps = []
    for c in range(NCORES):
        m = np.zeros((128, NCORES), np.float32)
        m[:, c] = 1.0
        in_maps.append(dict(
            xs=x_sample[c], xp=xp2, xq=f(xp2[c * OWN:(c + 1) * OWN]),
            cs_s=cs_s, cs_p=cs_p, cs_q=f(cs_p[c * OWN:(c + 1) * OWN]),
            w_in=w_in_p, w_br=f(w_branch)[0], w_out=f(w_out)[0], w_gate=w_gate, chp=chp,
            g_in=f(norm_in), g_fin=f(norm_final)[None, :], qn=qn, kn=kn, bm=bm, msk=m, ident=ident))
    res = run_bass_kernel_spmd(nc, in_maps, core_ids=list(range(NCORES)))
    y_s = np.stack([np.asarray(res.results[c]["ys"], dtype=np.float32) for c in range(NCORES)], axis=0)
    y_p = np.concatenate([np.asarray(res.results[c]["yq"], dtype=np.float32) for c in range(NCORES)], axis=0)[None]
    return (y_p, y_s)
```

```python
import contextlib
import math
import numpy as np
import concourse.bass as bass
import concourse.mybir as mybir
from concourse.bass_utils import run_bass_kernel_spmd

F32 = mybir.dt.float32
BF16 = mybir.dt.bfloat16
AF = mybir.ActivationFunctionType
ALU = mybir.AluOpType
AX = mybir.AxisListType

NCORES = 8
D = 1024
DC = 8
H = 8
KVH = 2
HD = 128
CH = 512
NPROJ = 6656
EPS = 1e-6
SM_SCALE = 1.0 / math.sqrt(HD)
OQ, OK_, OV, OGA, OXL, OGL, OML = 0, 1024, 1280, 1536, 2560, 3584, 4608


class Prog:
    ENG = ("sp", "act", "dve", "pool", "pe")

    def __init__(self):
        self.ops = []
        self.last_writer = {}
        self.readers = {}
        self.last_on = {e: None for e in self.ENG}
        self.dma_since = set()
        self.pending = {e: set() for e in self.ENG}

    def op(self, eng, fn, reads=(), writes=(), dma=False):
        idx = len(self.ops)
        deps = {}
        for r in reads:
            w = self.last_writer.get(r)
            if w is not None:
                deps[w] = "raw"
        for r in writes:
            w = self.last_writer.get(r)
            if w is not None and w not in deps:
                deps[w] = "waw"
            for rd in self.readers.get(r, ()):
                if rd not in deps:
                    deps[rd] = "war"
        if self.pending[eng]:
            for p in self.pending[eng]:
                deps[p] = "raw"
            self.pending[eng] = set()
        for r in reads:
            self.readers.setdefault(r, []).append(idx)
        for r in writes:
            self.last_writer[r] = idx
            self.readers[r] = []
        self.ops.append(dict(eng=eng, fn=fn, dma=dma, deps=deps, idx=idx))
        self.last_on[eng] = idx
        if dma:
            self.dma_since.add(idx)
        return idx

    def barrier(self):
        s = set(self.dma_since)
        for e in self.ENG:
            if self.last_on[e] is not None:
                s.add(self.last_on[e])
        self.dma_since = set()
        self.pending = {e: set(s) for e in self.ENG}
        self.last_writer.clear()
        self.readers.clear()

    def emit(self, nc, ndma_sems=20):
        ops = self.ops
        for o in ops:
            keep = []
            for p, kind in o["deps"].items():
                po = ops[p]
                if (not po["dma"]) and (not o["dma"]) and po["eng"] == o["eng"]:
                    if not (kind == "raw" and o["eng"] in ("act", "dve", "pool")):
                        continue
                keep.append(p)
            o["deps"] = sorted(keep)
            o["signal"] = False
        for o in ops:
            for p in o["deps"]:
                ops[p]["signal"] = True
        cnt = {e: 0 for e in self.ENG}
        dma_i = {e: 0 for e in self.ENG}
        dma_cum = {}
        for o in ops:
            if o["dma"]:
                slot = (o["eng"], dma_i[o["eng"]] % ndma_sems)
                dma_i[o["eng"]] += 1
                prev = dma_cum.get(slot, 0)
                o["sem"], o["prev"], o["val"] = slot, prev, prev + 16
                dma_cum[slot] = prev + 16
            elif o["signal"]:
                cnt[o["eng"]] += 1
                o["sem"], o["val"] = o["eng"], cnt[o["eng"]]
        dma_queues = sorted({o["eng"] for o in ops if o["dma"]})
        with contextlib.ExitStack() as st:
            sems = {}
            for e in self.ENG:
                sems[e] = st.enter_context(nc.semaphore("s_" + e))
            for q in dma_queues:
                for i in range(ndma_sems):
                    sems[(q, i)] = st.enter_context(nc.semaphore("d_%s_%d" % (q, i)))
            block = st.enter_context(nc.Block())
            per_eng = {e: [o for o in ops if o["eng"] == e] for e in self.ENG}

            def run_engine(ename, eng):
                waited = {}
                for o in per_eng[ename]:
                    need = {}
                    for p in o["deps"]:
                        po = ops[p]
                        s = po["sem"]
                        if po["val"] > need.get(s, 0):
                            need[s] = po["val"]
                    if o["dma"] and o["prev"] > 0:
                        s = o["sem"]
                        if o["prev"] > need.get(s, 0):
                            need[s] = o["prev"]
                    for s, v in need.items():
                        if waited.get(s, 0) >= v:
                            continue
                        eng.wait_ge(sems[s], v)
                        waited[s] = v
                    ins = o["fn"](eng)
                    if o["dma"]:
                        ins.then_inc(sems[o["sem"]], 16)
                    elif o["signal"]:
                        ins.then_inc(sems[o["sem"]], 1)
                if ename == "sp":
                    for slot, v in dma_cum.items():
                        if waited.get(slot, 0) < v:
                            eng.wait_ge(sems[slot], v)

            @block.sync
            def _(e):
                run_engine("sp", e)

            @block.scalar
            def _(e):
                run_engine("act", e)

            @block.vector
            def _(e):
                run_engine("dve", e)

            @block.gpsimd
            def _(e):
                run_engine("pool", e)

            @block.tensor
            def _(e):
                run_engine("pe", e)


class Arena:
    def __init__(self, ap, nwords):
        self.ap = ap
        self.n = nwords
        self.off = 0
        self.mark = 0

    def alloc(self, shape, dt):
        n = 1
        for s in shape:
            n *= s
        words = n if dt == F32 else (n + 1) // 2
        words = (words + 7) // 8 * 8
        assert self.off + words <= self.n, ("arena overflow", self.off, words, self.n)
        v = self.ap[:, self.off:self.off + words]
        self.off += words
        if dt != F32:
            v = v.bitcast(dt)
        v = v[:, 0:n]
        if len(shape) == 2:
            v = v.rearrange("p (a b) -> p a b", a=shape[0])
        elif len(shape) == 3:
            v = v.rearrange("p (a b c) -> p a b c", a=shape[0], b=shape[1])
        return v

    def persist(self):
        self.mark = self.off

    def reset(self):
        self.off = self.mark


def build_nc(S_S, S_P):
    OWN = S_P // NCORES
    assert OWN % CH == 0 and S_S % CH == 0
    SMAX = max(S_S, S_P)
    NSLOT = OWN // CH
    nc = bass.Bass("TRN2", target_bir_lowering=False)

    def din(name, shape, dt=F32):
        return nc.dram_tensor(name, list(shape), dt, kind="ExternalInput").ap()

    xs = din("xs", [S_S, D])
    xp = din("xp", [S_P, D])
    xq = din("xq", [OWN, D])
    cs_s = din("cs_s", [S_S, HD])
    cs_p = din("cs_p", [S_P, HD])
    cs_q = din("cs_q", [OWN, HD])
    w_in = din("w_in", [D, NPROJ])
    w_br = din("w_br", [2, D, D])
    w_out = din("w_out", [D, D])
    w_gate = din("w_gate", [4, 8, 128, 128])
    chp = din("chp", [128, 8, 11])
    g_in = din("g_in", [1, D])
    g_fin = din("g_fin", [1, D])
    qn = din("qn", [1, HD])
    kn = din("kn", [1, HD])
    bm = din("bm", [128, 16])
    msk = din("msk", [128, NCORES])
    ident = din("ident", [128, 128])
    ys = nc.dram_tensor("ys", [S_S, D], F32, kind="ExternalOutput").ap()
    yq = nc.dram_tensor("yq", [OWN, D], F32, kind="ExternalOutput").ap()
    kT_scr = nc.dram_tensor("kT_scr", [KVH, 128, SMAX], BF16).ap()
    v_scr = nc.dram_tensor("v_scr", [KVH, 128, SMAX // 128, HD], BF16).ap()
    xl_scr = nc.dram_tensor("xl_scr", [128, 8, SMAX], F32).ap()
    hf_scr = nc.dram_tensor("hf_scr", [128, 8, SMAX], F32).ap()
    ls_scr = nc.dram_tensor("ls_scr", [128, 8, max(S_S, OWN)], F32).ap()
    ao_scr = nc.dram_tensor("ao_scr", [128, 8, max(S_S, OWN)], BF16).ap()
    hT_scr = nc.dram_tensor("hT_scr", [128, 8, max(S_S, OWN)], BF16).ap()
    qT_scr = nc.dram_tensor("qT_scr", [128, 8, max(S_S, OWN)], BF16).ap()

    PG = Prog()
    w_in_v = w_in.rearrange("(c p) n -> p c n", p=128)

    def OP(eng, reads, writes, f):
        PG.op(eng, f, reads, writes)

    def DMA(eng, out, in_, reads, writes):
        PG.op(eng, lambda e: e.dma_start(out=out, in_=in_), reads, writes, dma=True)

    with contextlib.ExitStack() as st:
        NW = 53000
        arena_t = st.enter_context(nc.sbuf_tensor("arena", [128, NW], F32))
        ps_t = st.enter_context(nc.psum_tensor("ps", [128, 4096], F32))
        AR = Arena(arena_t, NW)
        ps = ps_t
        psb = ps_t[:].bitcast(BF16)

        def bank(b, n=512):
            return ps[:, b * 512:b * 512 + n]

        def bankb(b, n=1024):
            return psb[:, b * 1024:b * 1024 + n]

        bank_ctr = [0]

        def next_bank():
            b = bank_ctr[0] % 8
            bank_ctr[0] += 1
            return b

        ident_b = AR.alloc([128], BF16)
        ones_b = AR.alloc([128], BF16)
        ones_f = AR.alloc([128], F32)
        gin_bc = AR.alloc([D], F32)
        gfin_bc = AR.alloc([D], F32)
        qn_bc = AR.alloc([HD], F32)
        kn_bc = AR.alloc([HD], F32)
        chp_t = AR.alloc([8, 11], F32)
        bm_t = AR.alloc([16], F32)
        bmh_t = AR.alloc([16], F32)
        msk_t = AR.alloc([NCORES], F32)
        lam_c = AR.alloc([16], F32)
        cl_h = AR.alloc([16], F32)
        cl_1 = AR.alloc([16], F32)
        brh = AR.alloc([16], F32)
        bih = AR.alloc([16], F32)
        tmpa = AR.alloc([16], F32)
        tmpb = AR.alloc([16], F32)
        tmpc = AR.alloc([16], F32)
        negB = AR.alloc([1], F32)
        mq = AR.alloc([1], F32)
        mk = AR.alloc([1], F32)
        absq = AR.alloc([HD], F32)
        wg = AR.alloc([4, 8, 128], BF16)
        AR.persist()

        DMA("pool", ident_b, ident[:, :], [], ["ident_b"])
        OP("dve", [], ["ones_b"], lambda e: e.memset(ones_b, 1.0))
        OP("dve", [], ["ones_f"], lambda e: e.memset(ones_f, 1.0))
        DMA("sp", gin_bc, g_in[0:1, :].partition_broadcast(128), [], ["gin_bc"])
        DMA("sp", gfin_bc, g_fin[0:1, :].partition_broadcast(128), [], ["gfin_bc"])
        DMA("sp", qn_bc, qn[0:1, :].partition_broadcast(128), [], ["qn_bc"])
        DMA("sp", kn_bc, kn[0:1, :].partition_broadcast(128), [], ["kn_bc"])
        DMA("sp", chp_t, chp[:, :, :], [], ["chp"])
        DMA("sp", bm_t, bm[:, :], [], ["bm"])
        DMA("sp", msk_t, msk[:, :], [], ["msk"])
        for gi in range(4):
            DMA("pool", wg[:, gi, :, :], w_gate[gi].rearrange("b i o -> i b o"), [], ["wg"])
        for d_ in range(2):
            OP("dve", ["chp"], ["lam_c"], lambda e, d_=d_: e.tensor_copy(out=lam_c[:, d_ * 8:(d_ + 1) * 8], in_=chp_t[:, :, 9 + d_]))
            OP("dve", ["chp"], ["brh"], lambda e, d_=d_: e.tensor_scalar(out=brh[:, d_ * 8:(d_ + 1) * 8], in0=chp_t[:, :, 5 + d_], scalar1=0.5, scalar2=None, op0=ALU.mult))
            OP("dve", ["chp"], ["bih"], lambda e, d_=d_: e.tensor_scalar(out=bih[:, d_ * 8:(d_ + 1) * 8], in0=chp_t[:, :, 7 + d_], scalar1=0.5, scalar2=None, op0=ALU.mult))
        OP("dve", ["bm"], ["bmh"], lambda e: e.tensor_scalar(out=bmh_t, in0=bm_t, scalar1=0.5, scalar2=None, op0=ALU.mult))
        OP("act", ["lam_c"], ["tmpa"], lambda e: e.activation(out=tmpa, in_=lam_c, func=AF.Exp, scale=-1.0))
        OP("dve", ["tmpa"], ["tmpb"], lambda e: e.tensor_scalar(out=tmpb, in0=tmpa, scalar1=1.0, scalar2=None, op0=ALU.add))
        OP("act", ["tmpb"], ["tmpc"], lambda e: e.activation(out=tmpc, in_=tmpb, func=AF.Ln))
        OP("dve", ["tmpb"], ["tmpb"], lambda e: e.tensor_scalar(out=tmpb, in0=tmpb, scalar1=-1.0, scalar2=1e-30, op0=ALU.add, op1=ALU.max))
        OP("dve", ["tmpb"], ["tmpb"], lambda e: e.reciprocal(out=tmpb, in_=tmpb))
        OP("dve", ["tmpa", "tmpc"], ["tmpc"], lambda e: e.tensor_tensor(out=tmpc, in0=tmpc, in1=tmpa, op=ALU.mult))
        OP("dve", ["tmpb", "tmpc"], ["tmpc"], lambda e: e.tensor_tensor(out=tmpc, in0=tmpc, in1=tmpb, op=ALU.mult))
        OP("dve", ["tmpc"], ["cl_h"], lambda e: e.tensor_scalar(out=cl_h, in0=tmpc, scalar1=-4.0, scalar2=None, op0=ALU.mult))
        OP("dve", ["tmpc"], ["cl_1"], lambda e: e.tensor_scalar(out=cl_1, in0=tmpc, scalar1=-8.0, scalar2=None, op0=ALU.mult))
        OP("act", ["qn_bc"], ["absq"], lambda e: e.activation(out=absq, in_=qn_bc, func=AF.Abs))
        OP("dve", ["absq"], ["mq"], lambda e: e.tensor_reduce(out=mq, in_=absq, axis=AX.X, op=ALU.max))
        OP("act", ["kn_bc", "mq"], ["absq"], lambda e: e.activation(out=absq, in_=kn_bc, func=AF.Abs))
        OP("dve", ["absq"], ["mk"], lambda e: e.tensor_reduce(out=mk, in_=absq, axis=AX.X, op=ALU.max))
        OP("dve", ["mq", "mk"], ["negB"], lambda e: e.tensor_tensor(out=negB, in0=mq, in1=mk, op=ALU.mult))
        OP("dve", ["negB"], ["negB"], lambda e: e.tensor_scalar(out=negB, in0=negB, scalar1=-math.sqrt(HD), scalar2=None, op0=ALU.mult))
        PG.barrier()

        def wload(dst, src_cols):
            DMA("pool", dst, src_cols, [], ["W"])

        def hT_stageA(bufs, x_ap, t0, tiles):
            for i in tiles:
                sl = i % bufs["depth"]
                xt, ssq, xn, junk = bufs["xt"][sl], bufs["ssq"][sl], bufs["xn"][sl], bufs["junk"]
                xtr = "xt%d" % sl
                DMA("sp", xt, x_ap[t0 + i * 128:t0 + (i + 1) * 128, :], [], [xtr])
                OP("act", [xtr], ["junk", "ssq%d" % sl], lambda e, xt=xt, ssq=ssq: e.activation(out=junk, in_=xt, func=AF.Square, accum_out=ssq))
                OP("act", ["ssq%d" % sl], ["ssq%d" % sl], lambda e, ssq=ssq: e.activation(out=ssq, in_=ssq, func=AF.Sqrt, scale=1.0 / D, bias=EPS))
                OP("dve", ["ssq%d" % sl], ["ssq%d" % sl], lambda e, ssq=ssq: e.reciprocal(out=ssq, in_=ssq))
                OP("dve", [xtr, "ssq%d" % sl], ["xn%d" % sl], lambda e, xt=xt, ssq=ssq, xn=xn: e.scalar_tensor_tensor(out=xn, in0=xt, scalar=ssq[:, 0:1], in1=gin_bc, op0=ALU.mult, op1=ALU.mult))

        def hT_stageB(bufs, tiles, hT, hT_res, banks=None):
            for i in tiles:
                sl = i % bufs["depth"]
                xn = bufs["xn"][sl]
                if banks is None:
                    b = next_bank()
                else:
                    b = banks[i % len(banks)]
                for c in range(DC):
                    OP("pe", ["xn%d" % sl], ["pb%d" % b], lambda e, c=c, b=b, xn=xn: e.transpose(out=bankb(b)[:, c * 128:(c + 1) * 128], in_=xn[:, c * 128:(c + 1) * 128], identity=ident_b))
                OP("act", ["pb%d" % b], [hT_res], lambda e, b=b, i=i: e.activation(out=hT[:, :, i * 128:(i + 1) * 128], in_=bankb(b).rearrange("p (c t) -> p c t", c=DC), func=AF.Copy))

        def make_hT(bufs, x_ap, t0, nt, hT, hT_res, banks=None):
            dp = bufs["depth"]
            for i0 in range(0, nt, dp):
                tiles = list(range(i0, min(i0 + dp, nt)))
                hT_stageA(bufs, x_ap, t0, tiles)
                hT_stageB(bufs, tiles, hT, hT_res, banks)

        def norm_rope(tb, src_ps, nt, nh, w_bc, cs_t, cs_res, src_res, out_b, out_res):
            nu = nt * nh
            n = nu * HD
            s = tb["sl"] % tb["nbuf"]
            tb["sl"] += 1
            qf, sq, ssq, qn_, t1, t2, t3, t4 = (tb[k][s] for k in ("qf", "sq", "ssq", "qn", "t1", "t2", "t3", "t4"))
            R = lambda k: "%s%d" % (k, s)
            OP("act", src_res, [R("qf")], lambda e: e.activation(out=qf[:, 0:n].rearrange("p (t x) -> p t x", t=nt), in_=src_ps, func=AF.Copy))
            OP("act", [R("qf")], [R("sq")], lambda e: e.activation(out=sq[:, 0:n], in_=qf[:, 0:n], func=AF.Square))
            OP("dve", [R("sq")], [R("qssq")], lambda e: e.tensor_reduce(out=ssq[:, 0:nu], in_=sq[:, 0:n].rearrange("p (h d) -> p h d", h=nu), axis=AX.X, op=ALU.add))
            OP("act", [R("qssq")], [R("qssq")], lambda e: e.activation(out=ssq[:, 0:nu], in_=ssq[:, 0:nu], func=AF.Sqrt, scale=1.0 / HD, bias=EPS))
            OP("dve", [R("qssq")], [R("qssq")], lambda e: e.reciprocal(out=ssq[:, 0:nu], in_=ssq[:, 0:nu]))
            for h in range(nu):
                OP("dve", [R("qf"), R("qssq")], [R("qn")], lambda e, h=h: e.scalar_tensor_tensor(out=qn_[:, h * HD:(h + 1) * HD], in0=qf[:, h * HD:(h + 1) * HD], scalar=ssq[:, h:h + 1], in1=w_bc, op0=ALU.mult, op1=ALU.mult))
            q4 = qn_[:, 0:n].rearrange("p (t h d) -> p t h d", t=nt, h=nh)
            x0, x1 = q4[:, :, :, 0:64], q4[:, :, :, 64:128]
            cosb = cs_t[:, :, 0:64].unsqueeze(2).to_broadcast([128, nt, nh, 64])
            sinb = cs_t[:, :, 64:128].unsqueeze(2).to_broadcast([128, nt, nh, 64])
            o4 = out_b[:, 0:n].rearrange("p (t h d) -> p t h d", t=nt, h=nh)

            def v4(t):
                return t[:, 0:nu * 64].rearrange("p (t h d) -> p t h d", t=nt, h=nh)
            OP("dve", [R("qn"), cs_res], [R("t1")], lambda e: e.tensor_tensor(out=v4(t1), in0=x0, in1=cosb, op=ALU.mult))
            OP("dve", [R("qn"), cs_res], [R("t2")], lambda e: e.tensor_tensor(out=v4(t2), in0=x1, in1=sinb, op=ALU.mult))
            OP("dve", [R("t1"), R("t2")], [out_res], lambda e: e.tensor_tensor(out=o4[:, :, :, 0:64], in0=v4(t1), in1=v4(t2), op=ALU.subtract))
            OP("dve", [R("qn"), cs_res], [R("t3")], lambda e: e.tensor_tensor(out=v4(t3), in0=x0, in1=sinb, op=ALU.mult))
            OP("dve", [R("qn"), cs_res], [R("t4")], lambda e: e.tensor_tensor(out=v4(t4), in0=x1, in1=cosb, op=ALU.mult))
            OP("dve", [R("t3"), R("t4")], [out_res], lambda e: e.tensor_tensor(out=o4[:, :, :, 64:128], in0=v4(t3), in1=v4(t4), op=ALU.add))

        def alloc_x_bufs(depth=2):
            return dict(depth=depth, xt=[AR.alloc([D], F32) for _ in range(depth)], ssq=[AR.alloc([1], F32) for _ in range(depth)],
                        xn=[AR.alloc([D], BF16) for _ in range(depth)], junk=AR.alloc([D], BF16))

        def alloc_rope_bufs(nbuf=2):
            n = 8 * HD
            d = dict(sl=0)
            for k, sz in (("qf", n), ("sq", n), ("ssq", 8), ("qn", n), ("t1", n // 2), ("t2", n // 2), ("t3", n // 2), ("t4", n // 2)):
                d[k] = [AR.alloc([sz], F32) for _ in range(nbuf)]
                if nbuf == 1:
                    d[k] = d[k] * 2
            if nbuf == 1:
                d["sl"] = 0
            d["nbuf"] = nbuf
            return d

        def phase_A1(x_all, S, cs_all):
            AR.reset()
            Wkv = AR.alloc([DC, 512], BF16)
            Wxl = AR.alloc([DC, 1024], BF16)
            wload(Wkv, w_in_v[:, :, OK_:OK_ + 512])
            wload(Wxl, w_in_v[:, :, OXL:OXL + 1024])
            xb = alloc_x_bufs(4)
            rb = alloc_rope_bufs(2)
            hTs = [AR.alloc([DC, CH], BF16) for _ in range(2)]
            cst = [AR.alloc([4, HD], F32) for _ in range(2)]
            kb = [AR.alloc([4 * KVH * HD], BF16) for _ in range(2)]
            kTst = [AR.alloc([KVH, CH], BF16) for _ in range(2)]
            vst = [AR.alloc([KVH, 4, HD], BF16) for _ in range(2)]
            xlst = [AR.alloc([8, CH], F32) for _ in range(2)]
            nchunk = S // CH
            T4 = [0, 1, 2, 3]
            HB = (2, 3, 4, 5)
            hT_stageA(xb, x_all, 0, T4)
            hT_stageB(xb, T4, hTs[0], "hT0", banks=HB)
            for j in range(nchunk):
                s2 = j % 2
                t0 = j * CH
                hT = hTs[s2]
                hres = "hT%d" % s2
                DMA("sp", cst[s2], cs_all[t0:t0 + CH, :].rearrange("(i p) d -> p i d", p=128), [], ["cs%d" % s2])
                if j + 1 < nchunk:
                    hT_stageA(xb, x_all, t0 + CH, T4)
                for i in range(4):
                    b = 2 + i
                    for c in range(DC):
                        OP("pe", [hres, "W"], ["pb%d" % b], lambda e, c=c, b=b, i=i, hT=hT: e.matmul(bank(b), lhsT=hT[:, c, i * 128:(i + 1) * 128], rhs=Wkv[:, c, :], start=(c == 0), stop=(c == DC - 1)))
                kvr = ["pb%d" % (2 + i) for i in range(4)]
                kv4 = ps[:, 2 * 512:6 * 512].rearrange("p (i x) -> p i x", i=4)
                OP("act", kvr, ["vst%d" % s2], lambda e, s2=s2, kv4=kv4: e.activation(out=vst[s2].rearrange("p g i d -> p i g d"), in_=kv4[:, :, 256:512].rearrange("p i (g d) -> p i g d", g=KVH), func=AF.Copy))
                norm_rope(rb, kv4[:, :, 0:256], 4, KVH, kn_bc, cst[s2], "cs%d" % s2, kvr, kb[s2], "kb%d" % s2)
                for eb in range(8):
                    b = 7 if eb % 2 == 0 else 6
                    for c in range(DC):
                        OP("pe", [hres, "W"], ["pb%d" % b], lambda e, c=c, b=b, eb=eb, hT=hT: e.matmul(bank(b), lhsT=Wxl[:, c, eb * 128:(eb + 1) * 128], rhs=hT[:, c, :], start=(c == 0), stop=(c == DC - 1)))
                    if eb % 2 == 0:
                        OP("act", ["pb%d" % b], ["xlst%d" % s2], lambda e, b=b, eb=eb, s2=s2: e.activation(out=xlst[s2][:, eb, :], in_=bank(b), func=AF.Copy))
                    else:
                        OP("dve", ["pb%d" % b], ["xlst%d" % s2], lambda e, b=b, eb=eb, s2=s2: e.tensor_copy(out=xlst[s2][:, eb, :], in_=bank(b)))
                DMA("sp", xl_scr[:, :, t0:t0 + CH], xlst[s2], ["xlst%d" % s2], [])
                kbk = j % 2
                for u in range(4 * KVH):
                    OP("pe", ["kb%d" % s2], ["pb%d" % kbk], lambda e, u=u, s2=s2, kbk=kbk: e.transpose(out=bankb(kbk)[:, u * 128:(u + 1) * 128], in_=kb[s2][:, u * HD:(u + 1) * HD], identity=ident_b))
                OP("dve", ["pb%d" % kbk], ["kTst%d" % s2], lambda e, s2=s2, kbk=kbk: e.tensor_copy(out=kTst[s2].rearrange("p g (i t) -> p i g t", i=4), in_=bankb(kbk).rearrange("p (i g t) -> p i g t", i=4, g=KVH)))
                for g in range(KVH):
                    DMA("sp", kT_scr[g, :, t0:t0 + CH], kTst[s2][:, g, :], ["kTst%d" % s2], [])
                    DMA("sp", v_scr[g, :, j * 4:(j + 1) * 4, :], vst[s2][:, g, :, :], ["vst%d" % s2], [])
                if j + 1 < nchunk:
                    hT_stageB(xb, T4, hTs[1 - s2], "hT%d" % (1 - s2), banks=HB)
            PG.barrier()

        def phase_LRU(S, direction, own_acc):
            AR.reset()
            pipe = True
            nb = 2
            ng = 2
            BG = 4
            xlh = [AR.alloc([8, CH + 3], F32) for _ in range(nb)]
            hst = [AR.alloc([8, CH], F32) for _ in range(2)]
            hfl = [AR.alloc([8, CH], F32) for _ in range(nb)] if direction == 1 else None
            xc = [AR.alloc([CH], F32) for _ in range(BG * ng)]
            xcb = [AR.alloc([CH], BF16) for _ in range(BG * ng)]
            thr = [AR.alloc([CH], F32) for _ in range(BG)]
            thi = [AR.alloc([CH], F32) for _ in range(BG)]
            a_t = [AR.alloc([BG, CH], F32) for _ in range(ng)]
            s_t = [AR.alloc([BG, CH], F32) for _ in range(ng)]
            w_t = [AR.alloc([BG, CH], F32) for _ in range(ng)]
            if own_acc:
                OP("dve", [], ["w0"], lambda e: e.memset(w_t[0], 0.0))
                for slot in range(NSLOT):
                    for hb in range(8 // BG):
                        DMA("sp", ls_scr[:, hb * BG:(hb + 1) * BG, slot * CH:(slot + 1) * CH], w_t[0], ["w0"], ["lsacc%d" % slot])
            nchunk = S // CH
            order = list(range(nchunk)) if direction == 0 else list(range(nchunk - 1, -1, -1))

            def load(k):
                j = order[k]
                sl = k % nb
                t0 = j * CH
                lo = max(t0 - 2, 0)
                hi = min(t0 + CH + 1, S)
                if t0 == 0:
                    OP("dve", [], ["xlh%d" % sl], lambda e, sl=sl: e.memset(xlh[sl][:, :, 0:2], 0.0))
                if t0 + CH == S:
                    OP("dve", [], ["xlh%d" % sl], lambda e, sl=sl: e.memset(xlh[sl][:, :, CH + 2:CH + 3], 0.0))
                DMA("sp", xlh[sl][:, :, lo - (t0 - 2):hi - (t0 - 2)], xl_scr[:, :, lo:hi], [], ["xlh%d" % sl])
                if direction == 1:
                    DMA("sp", hfl[sl], hf_scr[:, :, t0:t0 + CH], [], ["hfl%d" % sl])

            items = [(k, g0) for k in range(nchunk) for g0 in range(0, 8, BG)]

            def S1(n):
                k, g0 = items[n]
                sl = k % nb
                for bi in range(BG):
                    eb = g0 + bi
                    xi = (n % ng) * BG + bi
                    cw = chp_t[:, eb, :]
                    src = xlh[sl][:, eb, :]
                    OP("act", ["xlh%d" % sl, "chp"], ["xc%d" % xi], lambda e, xi=xi, src=src, cw=cw: e.activation(out=xc[xi], in_=src[:, 0:CH], func=AF.Identity, scale=cw[:, 0:1], bias=cw[:, 4:5]))
                    for tap in range(1, 4):
                        OP("dve", ["xlh%d" % sl, "xc%d" % xi], ["xc%d" % xi], lambda e, xi=xi, src=src, cw=cw, tap=tap: e.scalar_tensor_tensor(out=xc[xi], in0=src[:, tap:tap + CH], scalar=cw[:, tap:tap + 1], in1=xc[xi], op0=ALU.mult, op1=ALU.add))

            def S2(n):
                k, g0 = items[n]
                for bi in range(BG):
                    eb = g0 + bi
                    xi = (n % ng) * BG + bi
                    OP("pool", ["xc%d" % xi], ["xcb%d" % xi], lambda e, xi=xi: e.tensor_copy(out=xcb[xi], in_=xc[xi]))
                    OP("pe", ["xcb%d" % xi, "wg"], ["pb%d" % (2 * bi)], lambda e, bi=bi, eb=eb, xi=xi: e.matmul(bank(2 * bi), lhsT=wg[:, 0 + direction, eb, :], rhs=xcb[xi], start=True, stop=True))
                    OP("pe", ["xcb%d" % xi, "wg"], ["pb%d" % (2 * bi + 1)], lambda e, bi=bi, eb=eb, xi=xi: e.matmul(bank(2 * bi + 1), lhsT=wg[:, 2 + direction, eb, :], rhs=xcb[xi], start=True, stop=True))

            def S3a(n):
                k, g0 = items[n]
                gs = n % ng
                for bi in range(BG):
                    eb = g0 + bi
                    ci = direction * 8 + eb
                    OP("act", ["pb%d" % (2 * bi)], ["thr%d" % bi], lambda e, bi=bi, ci=ci: e.activation(out=thr[bi], in_=bank(2 * bi), func=AF.Tanh, scale=0.5, bias=brh[:, ci:ci + 1]))
                    OP("act", ["pb%d" % (2 * bi + 1)], ["thi%d" % bi], lambda e, bi=bi, ci=ci: e.activation(out=thi[bi], in_=bank(2 * bi + 1), func=AF.Tanh, scale=0.5, bias=bih[:, ci:ci + 1]))
                    OP("act", ["thr%d" % bi], ["a%d" % gs], lambda e, bi=bi, gs=gs, ci=ci: e.activation(out=a_t[gs][:, bi, :], in_=thr[bi], func=AF.Exp, scale=cl_h[:, ci:ci + 1], bias=cl_h[:, ci:ci + 1]))
                    OP("act", ["thr%d" % bi], ["s%d" % gs], lambda e, bi=bi, gs=gs, ci=ci: e.activation(out=s_t[gs][:, bi, :], in_=thr[bi], func=AF.Exp, scale=cl_1[:, ci:ci + 1], bias=cl_1[:, ci:ci + 1]))
                OP("act", ["s%d" % gs], ["s%d" % gs], lambda e, gs=gs: e.activation(out=s_t[gs], in_=s_t[gs], func=AF.Sqrt, scale=-1.0, bias=1.0))

            def S3b(n):
                k, g0 = items[n]
                gs = n % ng
                hs = k % 2
                sl = k % nb
                j = order[k]
                t0 = j * CH
                for bi in range(BG):
                    xi = (n % ng) * BG + bi
                    OP("dve", ["thi%d" % bi, "xc%d" % xi], ["w%d" % gs], lambda e, bi=bi, gs=gs, xi=xi: e.scalar_tensor_tensor(out=w_t[gs][:, bi, :], in0=thi[bi], scalar=1.0, in1=xc[xi], op0=ALU.add, op1=ALU.mult))
                OP("dve", ["s%d" % gs, "w%d" % gs], ["w%d" % gs], lambda e, gs=gs: e.scalar_tensor_tensor(out=w_t[gs], in0=s_t[gs], scalar=0.5, in1=w_t[gs], op0=ALU.mult, op1=ALU.mult))
                for bi in range(BG):
                    eb = g0 + bi
                    if k == 0:
                        init = 0.0
                        rd = []
                    else:
                        pcol = CH - 1 if direction == 0 else 0
                        init = hst[1 - hs][:, eb, pcol:pcol + 1]
                        rd = ["hst%d" % (1 - hs)]
                    if direction == 0:
                        OP("dve", ["a%d" % gs, "w%d" % gs] + rd, ["hst%d" % hs], lambda e, gs=gs, bi=bi, eb=eb, hs=hs, init=init: e.tensor_tensor_scan(out=hst[hs][:, eb, :], data0=a_t[gs][:, bi, :], data1=w_t[gs][:, bi, :], initial=init, op0=ALU.mult, op1=ALU.add))
                    else:
                        OP("dve", ["a%d" % gs, "w%d" % gs] + rd, ["hst%d" % hs], lambda e, gs=gs, bi=bi, eb=eb, hs=hs, init=init: e.tensor_tensor_scan(out=hst[hs][:, eb, ::-1], data0=a_t[gs][:, bi, ::-1], data1=w_t[gs][:, bi, ::-1], initial=init, op0=ALU.mult, op1=ALU.add))
                if g0 + BG == 8:
                    if direction == 0:
                        DMA("sp", hf_scr[:, :, t0:t0 + CH], hst[hs], ["hst%d" % hs], [])
                    else:
                        OP("dve", ["hst%d" % hs, "hfl%d" % sl], ["hfl%d" % sl], lambda e, hs=hs, sl=sl: e.tensor_tensor(out=hfl[sl], in0=hfl[sl], in1=hst[hs], op=ALU.add))
                        if own_acc:
                            slot = j % NSLOT
                            grp = j // NSLOT
                            OP("dve", ["hfl%d" % sl, "msk"], ["hfl%d" % sl], lambda e, sl=sl, grp=grp: e.tensor_scalar(out=hfl[sl], in0=hfl[sl], scalar1=msk_t[:, grp:grp + 1], scalar2=None, op0=ALU.mult))
                            PG.op("pool", lambda e, sl=sl, slot=slot: e.dma_start(out=ls_scr[:, :, slot * CH:(slot + 1) * CH], in_=hfl[sl], accum_op=ALU.add), ["hfl%d" % sl], ["lsacc%d" % slot], dma=True)
                        else:
                            DMA("sp", ls_scr[:, :, t0:t0 + CH], hfl[sl], ["hfl%d" % sl], [])

            load(0)
            if pipe:
                S1(0)
                S2(0)
                for n in range(len(items)):
                    k, g0 = items[n]
                    if g0 == 0 and k + 1 < nchunk:
                        load(k + 1)
                    if n + 1 < len(items):
                        S1(n + 1)
                    S3a(n)
                    if n + 1 < len(items):
                        S2(n + 1)
                    S3b(n)
            else:
                for n in range(len(items)):
                    k, g0 = items[n]
                    S1(n)
                    S2(n)
                    S3a(n)
                    S3b(n)
                    if g0 + BG == 8 and k + 1 < nchunk:
                        load(k + 1)
            PG.barrier()

        def phase_B0(x_own, N_OWN, cs_own):
            AR.reset()
            Wq = AR.alloc([DC, 1024], BF16)
            wload(Wq, w_in_v[:, :, OQ:OQ + 1024])
            xb = alloc_x_bufs(4)
            rb = alloc_rope_bufs(2)
            hTs = [AR.alloc([DC, CH], BF16) for _ in range(2)]
            cst = [AR.alloc([4, HD], F32) for _ in range(2)]
            qb_ = [AR.alloc([H * HD], BF16) for _ in range(2)]
            qTst = [AR.alloc([H, CH], BF16) for _ in range(2)]
            nchunk = N_OWN // CH
            T4 = [0, 1, 2, 3]
            HB = (2, 3, 4, 5)
            hT_stageA(xb, x_own, 0, T4)
            hT_stageB(xb, T4, hTs[0], "hT0", banks=HB)
            tctr = [0]
            for j in range(nchunk):
                s2 = j % 2
                t0 = j * CH
                hT = hTs[s2]
                hres = "hT%d" % s2
                DMA("sp", cst[s2], cs_own[t0:t0 + CH, :].rearrange("(i p) d -> p i d", p=128), [], ["cs%d" % s2])
                if j + 1 < nchunk:
                    hT_stageA(xb, x_own, t0 + CH, T4)
                DMA("sp", hT_scr[:, :, t0:t0 + CH], hT, [hres], [])
                info = {}

                def proj(i, hT=hT, hres=hres, s2=s2):
                    ts = tctr[0] % 2
                    tctr[0] += 1
                    b0 = 0 if ts == 0 else 6
                    info[i] = ts
                    for hf in range(2):
                        for c in range(DC):
                            OP("pe", [hres, "W"], ["pb%d" % (b0 + hf)], lambda e, c=c, b0=b0, hf=hf, i=i: e.matmul(bank(b0 + hf), lhsT=hT[:, c, i * 128:(i + 1) * 128], rhs=Wq[:, c, hf * 512:(hf + 1) * 512], start=(c == 0), stop=(c == DC - 1)))
                    norm_rope(rb, ps[:, b0 * 512:b0 * 512 + 1024].rearrange("p (t x) -> p t x", t=1), 1, H, qn_bc, cst[s2][:, i:i + 1, :], "cs%d" % s2, ["pb%d" % b0, "pb%d" % (b0 + 1)], qb_[ts], "qb%d" % ts)

                def trans(i, s2=s2):
                    ts = info[i]
                    b2 = 2 + i
                    for h in range(H):
                        OP("pe", ["qb%d" % ts], ["pb%d" % b2], lambda e, h=h, b2=b2, ts=ts: e.transpose(out=bankb(b2)[:, h * 128:(h + 1) * 128], in_=qb_[ts][:, h * HD:(h + 1) * HD], identity=ident_b))
                    OP("dve", ["pb%d" % b2], ["qTst%d" % s2], lambda e, b2=b2, i=i, s2=s2: e.tensor_copy(out=qTst[s2][:, :, i * 128:(i + 1) * 128], in_=bankb(b2).rearrange("p (h t) -> p h t", h=H)))

                proj(0)
                for i in range(4):
                    if i + 1 < 4:
                        proj(i + 1)
                    trans(i)
                DMA("sp", qT_scr[:, :, t0:t0 + CH], qTst[s2], ["qTst%d" % s2], [])
                if j + 1 < nchunk:
                    hT_stageB(xb, T4, hTs[1 - s2], "hT%d" % (1 - s2), banks=HB)
            PG.barrier()

        def phase_B1(N_OWN, S):
            AR.reset()
            QB = 1024 if N_OWN % 1024 == 0 else 512
            QH = QB // 512
            NKT = S // 128
            big = S > 8192
            Wga = AR.alloc([DC, 1024], BF16)
            wload(Wga, w_in_v[:, :, OGA:OGA + 1024])
            resident = not big
            if resident:
                KT2 = AR.alloc([KVH, S], BF16)
                VV2 = AR.alloc([KVH, NKT, HD], BF16)
                for g in range(KVH):
                    DMA("sp", KT2[:, g, :], kT_scr[g, :, 0:S], [], ["KT"])
                    DMA("sp", VV2[:, g, :, :], v_scr[g, :, 0:NKT, :], [], ["VV"])
            else:
                KT1 = AR.alloc([S], BF16)
                VV1 = AR.alloc([NKT, HD], BF16)
            hTb = [AR.alloc([DC, QB], BF16) for _ in range(2)]
            qTb = [AR.alloc([H, QB], BF16) for _ in range(2)]
            pT = [AR.alloc([QB], BF16) for _ in range(3)]
            rs = AR.alloc([QB], F32)
            sg = [AR.alloc([QB], F32) for _ in range(2)]
            aob = [AR.alloc([QB], BF16) for _ in range(2)]
            XP = 384 if QB == 1024 else 192
            XD = QB - XP
            cnt = [0]
            hctr = 0
            nqb = N_OWN // QB

            def loadq(qb):
                s2 = qb % 2
                DMA("sp", hTb[s2], hT_scr[:, :, qb * QB:(qb + 1) * QB], [], ["hTb%d" % s2])
                DMA("sp", qTb[s2], qT_scr[:, :, qb * QB:(qb + 1) * QB], [], ["qTb%d" % s2])

            loadq(0)
            for qb in range(nqb):
                q0 = qb * QB
                s2 = qb % 2
                hT, qT = hTb[s2], qTb[s2]
                hres, qres = "hTb%d" % s2, "qTb%d" % s2
                if qb + 1 < nqb:
                    loadq(qb + 1)
                for g in range(KVH):
                    if resident:
                        KT = KT2[:, g, :]
                        VV = VV2[:, g, :, :]
                    else:
                        KT, VV = KT1, VV1
                        DMA("sp", KT, kT_scr[g, :, 0:S], [], ["KT"])
                        DMA("sp", VV, v_scr[g, :, 0:NKT, :], [], ["VV"])
                    for hh in range(H // KVH):
                        h = g * (H // KVH) + hh
                        hsg = hctr % 2
                        hctr += 1
                        for hf in range(QH):
                            for c in range(DC):
                                OP("pe", [hres, "W"], ["pb%d" % hf], lambda e, c=c, hf=hf, h=h, hT=hT: e.matmul(bank(hf), lhsT=Wga[:, c, h * 128:(h + 1) * 128], rhs=hT[:, c, hf * 512:(hf + 1) * 512], start=(c == 0), stop=(c == DC - 1)))
                        OP("act", ["pb%d" % hf for hf in range(QH)], ["sg%d" % hsg], lambda e, hsg=hsg: e.activation(out=sg[hsg], in_=ps[:, 0:QB], func=AF.Silu))
                        slots = []

                        def emit_qk(kt, h=h, KT=KT, qT=qT, qres=qres):
                            sp_ = (cnt[0] % 2) * 2
                            pt = cnt[0] % 3
                            cnt[0] += 1
                            slots.append((sp_, pt))
                            for hf in range(QH):
                                OP("pe", ["KT", qres], ["pb%d" % (sp_ + hf)], lambda e, kt=kt, sp_=sp_, hf=hf: e.matmul(bank(sp_ + hf), lhsT=KT[:, kt * 128:(kt + 1) * 128], rhs=qT[:, h, hf * 512:(hf + 1) * 512], start=True, stop=True))

                        emit_qk(0)
                        for kt in range(NKT):
                            if kt + 1 < NKT:
                                emit_qk(kt + 1)
                            sp_, pt = slots[kt]
                            OP("act", ["pb%d" % (sp_ + hf) for hf in range(QH)] + ["negB"], ["pT%d" % pt], lambda e, sp_=sp_, pt=pt: e.activation(out=pT[pt], in_=ps[:, sp_ * 512:sp_ * 512 + QB], func=AF.Exp, scale=SM_SCALE, bias=negB[:, 0:1]))
                            for hf in range(QH):
                                OP("pe", ["pT%d" % pt, "VV"], ["pb%d" % (4 + hf)], lambda e, kt=kt, pt=pt, hf=hf, VV=VV: e.matmul(bank(4 + hf), lhsT=VV[:, kt, :], rhs=pT[pt][:, hf * 512:(hf + 1) * 512], start=(kt == 0), stop=(kt == NKT - 1)))
                            OP("pe", ["pT%d" % pt, "ones_b"], ["pb%d" % (6 + QH - 1)], lambda e, kt=kt, pt=pt: e.matmul(ps[:, 6 * 512 + XD:6 * 512 + QB], lhsT=ones_b, rhs=pT[pt][:, XD:QB], start=(kt == 0), stop=(kt == NKT - 1)))
                            if kt == 0:
                                OP("dve", ["pT%d" % pt], ["rs"], lambda e, pt=pt: e.tensor_copy(out=rs[:, 0:XD], in_=pT[pt][:, 0:XD]))
                            else:
                                OP("dve", ["pT%d" % pt, "rs"], ["rs"], lambda e, pt=pt: e.tensor_tensor(out=rs[:, 0:XD], in0=rs[:, 0:XD], in1=pT[pt][:, 0:XD], op=ALU.add))
                        accr = ["pb%d" % (4 + hf) for hf in range(QH)]
                        sumr = ["pb%d" % (6 + hf) for hf in range(QH)]
                        for hf in range(QH):
                            c0, c1 = hf * 512, min((hf + 1) * 512, XD)
                            if c1 > c0:
                                OP("pe", ["rs", "ones_f"], ["pb%d" % (6 + hf)], lambda e, c0=c0, c1=c1: e.matmul(ps[:, 6 * 512 + c0:6 * 512 + c1], lhsT=ones_f, rhs=rs[:, c0:c1], start=True, stop=True))
                        OP("dve", sumr, ["rs"], lambda e: e.reciprocal(out=rs, in_=ps[:, 6 * 512:6 * 512 + QB]))
                        OP("dve", accr + ["rs"], ["rs"], lambda e: e.tensor_tensor(out=rs, in0=ps[:, 4 * 512:4 * 512 + QB], in1=rs, op=ALU.mult))
                        OP("dve", ["rs", "sg%d" % hsg], ["aob%d" % hsg], lambda e, hsg=hsg: e.tensor_tensor(out=aob[hsg], in0=rs, in1=sg[hsg], op=ALU.mult))
                        DMA("sp", ao_scr[:, h, q0:q0 + QB], aob[hsg], ["aob%d" % hsg], [])
            PG.barrier()

        def phase_B2(x_own, N_OWN, y_out):
            AR.reset()
            Wgl = AR.alloc([DC, 1024], BF16)
            Wml = AR.alloc([DC, 2048], BF16)
            WbA = AR.alloc([DC, 1024], BF16)
            WbL = AR.alloc([DC, 1024], BF16)
            Wo = AR.alloc([DC, 1024], BF16)
            wload(Wgl, w_in_v[:, :, OGL:OGL + 1024])
            wload(Wml[:, :, 0:1024], w_in_v[:, :, OML:OML + 1024])
            wload(Wml[:, :, 1024:2048], w_in_v[:, :, OML + 1024:OML + 2048])
            wload(WbA, w_br[0].rearrange("(c p) n -> p c n", p=128))
            wload(WbL, w_br[1].rearrange("(c p) n -> p c n", p=128))
            wload(Wo, w_out.rearrange("(c p) n -> p c n", p=128))
            hTs2 = [AR.alloc([DC, CH], BF16) for _ in range(2)]
            junkb = AR.alloc([D], BF16)
            lsl = [AR.alloc([CH], F32) for _ in range(2)]
            aol = AR.alloc([8, CH], BF16)
            loT = AR.alloc([8, CH], BF16)
            mT = AR.alloc([8, CH], BF16)
            sgl = [AR.alloc([CH], F32) for _ in range(2)]
            ga = [AR.alloc([CH], F32) for _ in range(2)]
            gl = [AR.alloc([CH], F32) for _ in range(2)]
            t1 = [AR.alloc([CH], F32) for _ in range(2)]
            t2 = [AR.alloc([CH], F32) for _ in range(2)]
            yt = [AR.alloc([D], F32) for _ in range(2)]
            yo = [AR.alloc([D], F32) for _ in range(2)]
            yss = [AR.alloc([1], F32) for _ in range(2)]
            xres = [AR.alloc([D], F32) for _ in range(2)]
            junk2 = junkb
            nchunk = N_OWN // CH
            ectr = 0
            yctr = 0
            for j in range(nchunk):
                t0 = j * CH
                DMA("sp", aol, ao_scr[:, :, t0:t0 + CH], [], ["aol"])
                hT = hTs2[j % 2]
                hres = "hT%d" % (j % 2)
                if j == 0:
                    DMA("sp", hTs2[0], hT_scr[:, :, 0:CH], [], ["hT0"])
                if j + 1 < nchunk:
                    DMA("sp", hTs2[(j + 1) % 2], hT_scr[:, :, t0 + CH:t0 + 2 * CH], [], ["hT%d" % ((j + 1) % 2)])
                for eb in range(8):
                    es = ectr % 2
                    ectr += 1
                    DMA("sp", lsl[es], ls_scr[:, eb, t0:t0 + CH], [], ["lsl%d" % es])
                    b = next_bank()
                    for c in range(DC):
                        OP("pe", [hres, "W"], ["pb%d" % b], lambda e, c=c, b=b, eb=eb, hT=hT: e.matmul(bank(b), lhsT=Wgl[:, c, eb * 128:(eb + 1) * 128], rhs=hT[:, c, :], start=(c == 0), stop=(c == DC - 1)))
                    OP("act", ["pb%d" % b], ["sgl%d" % es], lambda e, b=b, es=es: e.activation(out=sgl[es], in_=bank(b), func=AF.Silu))
                    OP("dve", ["sgl%d" % es, "lsl%d" % es], ["loT"], lambda e, es=es, eb=eb: e.tensor_tensor(out=loT[:, eb, :], in0=lsl[es], in1=sgl[es], op=ALU.mult))
                for eb in range(8):
                    es = ectr % 2
                    ectr += 1
                    bA, bL, bga, bgl = next_bank(), next_bank(), next_bank(), next_bank()
                    for c in range(DC):
                        OP("pe", ["aol", "W"], ["pb%d" % bA], lambda e, c=c, bA=bA, eb=eb: e.matmul(bank(bA), lhsT=WbA[:, c, eb * 128:(eb + 1) * 128], rhs=aol[:, c, :], start=(c == 0), stop=(c == DC - 1)))
                    for c in range(DC):
                        OP("pe", ["loT", "W"], ["pb%d" % bL], lambda e, c=c, bL=bL, eb=eb: e.matmul(bank(bL), lhsT=WbL[:, c, eb * 128:(eb + 1) * 128], rhs=loT[:, c, :], start=(c == 0), stop=(c == DC - 1)))
                    for c in range(DC):
                        OP("pe", [hres, "W"], ["pb%d" % bga], lambda e, c=c, bga=bga, eb=eb, hT=hT: e.matmul(bank(bga), lhsT=Wml[:, c, eb * 128:(eb + 1) * 128], rhs=hT[:, c, :], start=(c == 0), stop=(c == DC - 1)))
                    for c in range(DC):
                        OP("pe", [hres, "W"], ["pb%d" % bgl], lambda e, c=c, bgl=bgl, eb=eb, hT=hT: e.matmul(bank(bgl), lhsT=Wml[:, c, 1024 + eb * 128:1024 + (eb + 1) * 128], rhs=hT[:, c, :], start=(c == 0), stop=(c == DC - 1)))
                    OP("act", ["pb%d" % bga, "bm"], ["ga%d" % es], lambda e, bga=bga, es=es, eb=eb: e.activation(out=ga[es], in_=bank(bga), func=AF.Sigmoid, bias=bm_t[:, eb:eb + 1]))
                    OP("act", ["pb%d" % bgl, "bm"], ["gl%d" % es], lambda e, bgl=bgl, es=es, eb=eb: e.activation(out=gl[es], in_=bank(bgl), func=AF.Sigmoid, bias=bm_t[:, 8 + eb:9 + eb]))
                    OP("dve", ["pb%d" % bA, "ga%d" % es], ["t1%d" % es], lambda e, bA=bA, es=es: e.tensor_tensor(out=t1[es], in0=bank(bA), in1=ga[es], op=ALU.mult))
                    OP("dve", ["pb%d" % bL, "gl%d" % es], ["t2%d" % es], lambda e, bL=bL, es=es: e.tensor_tensor(out=t2[es], in0=bank(bL), in1=gl[es], op=ALU.mult))
                    OP("pool", ["t1%d" % es, "t2%d" % es], ["mT"], lambda e, es=es, eb=eb: e.tensor_tensor(out=mT[:, eb, :], in0=t1[es], in1=t2[es], op=ALU.add))
                for i in range(4):
                    ys_ = yctr % 2
                    yctr += 1
                    b0 = 2 * (next_bank() % 4)
                    DMA("sp", xres[ys_], x_own[t0 + i * 128:t0 + (i + 1) * 128, :], [], ["xres%d" % ys_])
                    for hf in range(2):
                        for c in range(DC):
                            OP("pe", ["mT", "W"], ["pb%d" % (b0 + hf)], lambda e, c=c, b0=b0, hf=hf, i=i: e.matmul(bank(b0 + hf), lhsT=mT[:, c, i * 128:(i + 1) * 128], rhs=Wo[:, c, hf * 512:(hf + 1) * 512], start=(c == 0), stop=(c == DC - 1)))
                    OP("dve", ["pb%d" % b0, "pb%d" % (b0 + 1), "xres%d" % ys_], ["yt%d" % ys_], lambda e, b0=b0, ys_=ys_: e.tensor_tensor(out=yt[ys_], in0=ps[:, b0 * 512:b0 * 512 + 1024], in1=xres[ys_], op=ALU.add))
                    OP("act", ["yt%d" % ys_], ["junk2", "yss%d" % ys_], lambda e, ys_=ys_: e.activation(out=junk2, in_=yt[ys_], func=AF.Square, accum_out=yss[ys_]))
                    OP("act", ["yss%d" % ys_], ["yss%d" % ys_], lambda e, ys_=ys_: e.activation(out=yss[ys_], in_=yss[ys_], func=AF.Sqrt, scale=1.0 / D, bias=EPS))
                    OP("dve", ["yss%d" % ys_], ["yss%d" % ys_], lambda e, ys_=ys_: e.reciprocal(out=yss[ys_], in_=yss[ys_]))
                    OP("dve", ["yt%d" % ys_, "yss%d" % ys_], ["yo%d" % ys_], lambda e, ys_=ys_: e.scalar_tensor_tensor(out=yo[ys_], in0=yt[ys_], scalar=yss[ys_][:, 0:1], in1=gfin_bc, op0=ALU.mult, op1=ALU.mult))
                    DMA("sp", y_out[t0 + i * 128:t0 + (i + 1) * 128, :], yo[ys_], ["yo%d" % ys_], [])
            PG.barrier()

        for (x_all, S, cs_all, x_own, n_own, cs_own, y_out, is_p) in (
            (xs, S_S, cs_s, xs, S_S, cs_s, ys, False),
            (xp, S_P, cs_p, xq, OWN, cs_q, yq, True),
        ):
            phase_A1(x_all, S, cs_all)
            phase_LRU(S, 0, False)
            phase_LRU(S, 1, is_p)
            phase_B0(x_own, n_own, cs_own)
            phase_B1(n_own, S)
            phase_B2(x_own, n_own, y_out)
        PG.emit(nc)
    return nc


def _rope_table(S):
    grid_w = 64
    pos = np.arange(S)
    rows = (pos // grid_w).astype(np.float32)
    cols = (pos % grid_w).astype(np.float32)
    n = HD // 4
    inv = (np.float32(10000.0) ** (-(np.arange(n, dtype=np.float32) / np.float32(n)))).astype(np.float32)
    ang = np.concatenate([rows[:, None] * inv[None, :], cols[:, None] * inv[None, :]], axis=-1).astype(np.float32)
    return np.concatenate([np.cos(ang), np.sin(ang)], axis=-1).astype(np.float32)


_NC_CACHE = {}


def kernel(x_prompt, x_sample, norm_in, w_in, b_merge, q_norm, k_norm, conv_w, conv_b,
           w_rgate, b_rgate, w_igate, b_igate, lam, w_branch, w_out, norm_final):
    f = lambda a: np.ascontiguousarray(np.asarray(a, dtype=np.float32))
    x_prompt, x_sample = f(x_prompt), f(x_sample)
    S_P = x_prompt.shape[1]
    S_S = x_sample.shape[1]
    OWN = S_P // NCORES
    assert x_prompt.shape[0] == 1 and x_sample.shape[0] == NCORES
    perm = np.concatenate([np.arange(0, HD, 2), np.arange(1, HD, 2)])
    cols = np.arange(NPROJ)
    for hh in range(H):
        cols[OQ + hh * HD:OQ + (hh + 1) * HD] = OQ + hh * HD + perm
    for hh in range(KVH):
        cols[OK_ + hh * HD:OK_ + (hh + 1) * HD] = OK_ + hh * HD + perm
    w_in_p = f(f(w_in)[0][:, cols])
    qn = f(f(q_norm)[0][perm][None, :])
    kn = f(f(k_norm)[0][perm][None, :])
    plist = [f(conv_w)[0][0], f(conv_w)[0][1], f(conv_w)[0][2], f(conv_w)[0][3], f(conv_b)[0],
             f(b_rgate)[0][0], f(b_rgate)[0][1], f(b_igate)[0][0], f(b_igate)[0][1], f(lam)[0][0], f(lam)[0][1]]
    chp = f(np.stack(plist, axis=-1).reshape(8, 128, 11).transpose(1, 0, 2))
    w_gate = f(np.stack([f(w_rgate)[0][0], f(w_rgate)[0][1], f(w_igate)[0][0], f(w_igate)[0][1]], axis=0))
    bm = f(f(b_merge)[0].reshape(16, 128).T)
    cs_p = _rope_table(S_P)
    cs_s = _rope_table(S_S)
    ident = np.eye(128, dtype=np.float32)
    key = (S_S, S_P)
    if key not in _NC_CACHE:
        _NC_CACHE[key] = build_nc(S_S, S_P)
    nc = _NC_CACHE[key]
    xp2 = x_prompt[0]
    in_maps = []
    for c in range(NCORES):
        m = np.zeros((128, NCORES), np.float32)
        m[:, c] = 1.0
        in_maps.append(dict(
            xs=x_sample[c], xp=xp2, xq=f(xp2[c * OWN:(c + 1) * OWN]),
            cs_s=cs_s, cs_p=cs_p, cs_q=f(cs_p[c * OWN:(c + 1) * OWN]),
            w_in=w_in_p, w_br=f(w_branch)[0], w_out=f(w_out)[0], w_gate=w_gate, chp=chp,
            g_in=f(norm_in), g_fin=f(norm_final)[None, :], qn=qn, kn=kn, bm=bm, msk=m, ident=ident))
    res = run_bass_kernel_spmd(nc, in_maps, core_ids=list(range(NCORES)))
    y_s = np.stack([np.asarray(res.results[c]["ys"], dtype=np.float32) for c in range(NCORES)], axis=0)
    y_p = np.concatenate([np.asarray(res.results[c]["yq"], dtype=np.float32) for c in range(NCORES)], axis=0)[None]
    return (y_p, y_s)
```
